# Optimizing a Trainium2 kernel written in Bass

```python
import math
import jax
import jax.numpy as jnp
from jax import lax
import numpy as np

D_MODEL = 1024
BATCH = 8
SEQ = 8192
DEPTH = 2

HEAD_DIM = 64
MIX_HALF = D_MODEL // 2
GLA_HEADS = MIX_HALF // HEAD_DIM
FOX_HEADS = MIX_HALF // HEAD_DIM
GLA_RANK = 16
GLA_TAU = 16.0
GLA_CHUNK = 64
FOX_BLOCK = 128
S5_GROUP_WIDTH = 16
S5_GROUPS = MIX_HALF // S5_GROUP_WIDTH
S5_STATE = 64
S5_CHUNK = 128
SGU_GROUPS = 8
SGU_GROUP_WIDTH = MIX_HALF // SGU_GROUPS
SGU_CHUNK = 128
D_FF = 4 * D_MODEL
N_EVEN = (DEPTH + 1) // 2
N_ODD = DEPTH // 2
EPS = 1e-6
EVEN_SIZES = (MIX_HALF, MIX_HALF, MIX_HALF, MIX_HALF, GLA_RANK, MIX_HALF, MIX_HALF, MIX_HALF, FOX_HEADS)
EVEN_SPLITS = tuple(int(v) for v in np.cumsum(EVEN_SIZES)[:-1])
EVEN_WIDTH = int(sum(EVEN_SIZES))
ODD_WIDTH = 3 * MIX_HALF

kernel_name = "hybrid_gla_fox_s5_sgu_adaln_trunk"


def _rms(x):
    x = x.astype(jnp.float32)
    return x * lax.rsqrt(jnp.mean(x * x, axis=-1, keepdims=True) + EPS)


def _gla(q, k, v, log_a):
    B, S, H, Dh = q.shape
    n = S // GLA_CHUNK

    def chunks(t):
        return t.astype(jnp.float32).reshape(B, n, GLA_CHUNK, H, Dh).transpose(1, 0, 3, 2, 4)

    q = q.astype(jnp.float32) * (Dh ** -0.5)
    mask = jnp.tril(jnp.ones((GLA_CHUNK, GLA_CHUNK), dtype=bool))

    def step(state, inp):
        qc, kc, vc, lc = inp
        bc = jnp.cumsum(lc, axis=2)
        o_inter = jnp.einsum('bhtk,bhkv->bhtv', qc * jnp.exp(bc), state)
        diff = bc[:, :, :, None, :] - bc[:, :, None, :, :]
        decay = jnp.exp(jnp.where(mask[:, :, None], diff, -jnp.inf))
        scores = jnp.einsum('bhtk,bhtsk,bhsk->bhts', qc, decay, kc)
        o = o_inter + jnp.einsum('bhts,bhsv->bhtv', scores, vc)
        b_last = bc[:, :, -1:, :]
        state = (jnp.exp(b_last[:, :, 0, :])[..., None] * state
                 + jnp.einsum('bhsk,bhsv->bhkv', kc * jnp.exp(b_last - bc), vc))
        return state, o

    state0 = jnp.zeros((B, H, Dh, Dh), jnp.float32)
    _, o = lax.scan(step, state0, (chunks(q), chunks(k), chunks(v), chunks(log_a)))
    return o.transpose(1, 0, 3, 2, 4).reshape(B, S, H, Dh)


def _fox(q, k, v, f_logit, q_gain, k_gain):
    B, S, H, Dh = q.shape
    q = (_rms(q) * q_gain).transpose(0, 2, 1, 3)
    k = (_rms(k) * k_gain).transpose(0, 2, 1, 3)
    v = v.astype(jnp.float32).transpose(0, 2, 1, 3)
    cum = jnp.cumsum(jax.nn.log_sigmoid(f_logit.astype(jnp.float32)), axis=1).transpose(0, 2, 1)
    nb = S // FOX_BLOCK
    qb = q.reshape(B, H, nb, FOX_BLOCK, Dh).transpose(2, 0, 1, 3, 4)
    cb = cum.reshape(B, H, nb, FOX_BLOCK).transpose(2, 0, 1, 3)
    pos = jnp.arange(S, dtype=jnp.int32)
    pb = pos.reshape(nb, FOX_BLOCK)
    scale = Dh ** -0.5

    def block(args):
        qi, ci, pi = args
        logits = (jnp.einsum('bhqd,bhkd->bhqk', qi, k) * scale
                  + ci[..., None] - cum[:, :, None, :])
        logits = jnp.where(pi[:, None] >= pos[None, :], logits, -jnp.inf)
        return jnp.einsum('bhqk,bhkd->bhqd', jax.nn.softmax(logits, axis=-1), v)

    out = lax.map(block, (qb, cb, pb))
    return out.transpose(1, 0, 3, 2, 4).reshape(B, S, H, Dh)


def _ssm_combine(e1, e2):
    a1r, a1i, b1r, b1i = e1
    a2r, a2i, b2r, b2i = e2
    return (a2r * a1r - a2i * a1i,
            a2r * a1i + a2i * a1r,
            a2r * b1r - a2i * b1i + b2r,
            a2r * b1i + a2i * b1r + b2i)


def _s5(u, lam_re, lam_im, log_dt, b_re, b_im, c_re, c_im, d_skip):
    B, S, _ = u.shape
    f32 = jnp.float32
    lam_re, lam_im, b_re, b_im, c_re, c_im, d_skip = (
        t.astype(f32) for t in (lam_re, lam_im, b_re, b_im, c_re, c_im, d_skip))
    dt = jnp.exp(log_dt.astype(f32))[:, None]
    mag = jnp.exp(lam_re * dt)
    ang = lam_im * dt
    abar_re = mag * jnp.cos(ang)
    abar_im = mag * jnp.sin(ang)
    den = lam_re * lam_re + lam_im * lam_im
    coef_re = ((abar_re - 1.0) * lam_re + abar_im * lam_im) / den
    coef_im = (abar_im * lam_re - (abar_re - 1.0) * lam_im) / den
    bbar_re = coef_re[..., None] * b_re - coef_im[..., None] * b_im
    bbar_im = coef_re[..., None] * b_im + coef_im[..., None] * b_re
    n = S // S5_CHUNK
    uc = u.astype(f32).reshape(B, n, S5_CHUNK, S5_GROUPS, S5_GROUP_WIDTH).transpose(1, 0, 2, 3, 4)

    def step(carry, u_chunk):
        x_re0, x_im0 = carry
        bu_re = jnp.einsum('bcgi,gpi->bcgp', u_chunk, bbar_re)
        bu_im = jnp.einsum('bcgi,gpi->bcgp', u_chunk, bbar_im)
        a_re = jnp.broadcast_to(abar_re, bu_re.shape)
        a_im = jnp.broadcast_to(abar_im, bu_re.shape)
        acc_re, acc_im, x_re, x_im = lax.associative_scan(
            _ssm_combine, (a_re, a_im, bu_re, bu_im), axis=1)
        x_re = x_re + acc_re * x_re0[:, None] - acc_im * x_im0[:, None]
        x_im = x_im + acc_re * x_im0[:, None] + acc_im * x_re0[:, None]
        y = (jnp.einsum('bcgp,gip->bcgi', x_re, c_re)
             - jnp.einsum('bcgp,gip->bcgi', x_im, c_im)
             + d_skip * u_chunk)
        return (x_re[:, -1], x_im[:, -1]), y

    zeros = jnp.zeros((B, S5_GROUPS, S5_STATE), f32)
    _, y = lax.scan(step, (zeros, zeros), uc)
    return y.transpose(1, 0, 2, 3, 4).reshape(B, S, MIX_HALF)


def _sgu(z, ln_gain, ln_bias, w_s, b_s):
    B, S, _ = z.shape
    z = jax.nn.gelu(z.astype(jnp.float32))
    u, v = z[..., :MIX_HALF], z[..., MIX_HALF:]
    mu = jnp.mean(v, axis=-1, keepdims=True)
    var = jnp.mean(jnp.square(v - mu), axis=-1, keepdims=True)
    v = (v - mu) * lax.rsqrt(var + EPS) * ln_gain + ln_bias
    n = S // SGU_CHUNK
    v = v.reshape(B, n, SGU_CHUNK, SGU_GROUPS, SGU_GROUP_WIDTH)
    mask = jnp.tril(jnp.ones((SGU_CHUNK, SGU_CHUNK), dtype=bool))
    w = jnp.where(mask[None], w_s.astype(jnp.float32), 0.0)
    mixed = jnp.einsum('gts,bnsgc->bntgc', w, v) + b_s.astype(jnp.float32).T[None, None, :, :, None]
    return u * mixed.reshape(B, S, MIX_HALF)


def _even_mixer(h, w_in, w_out, w_lr, b_lr, gla_gain, b_f, q_gain, k_gain):
    B, S, _ = h.shape
    proj = jnp.einsum('bsd,de->bse', h, w_in)
    gq, gk, gv, gg, glr, fq, fk, fv, ff = jnp.split(proj, EVEN_SPLITS, axis=-1)

    def heads(t):
        return t.reshape(B, S, -1, HEAD_DIM)

    log_a = jax.nn.log_sigmoid((jnp.einsum('bsr,re->bse', glr, w_lr) + b_lr).astype(jnp.float32)) / GLA_TAU
    o_gla = _gla(heads(gq), heads(gk), heads(gv), heads(log_a))
    o_gla = _rms(o_gla) * gla_gain * jax.nn.silu(heads(gg).astype(jnp.float32))
    o_fox = _fox(heads(fq), heads(fk), heads(fv), ff + b_f, q_gain, k_gain)
    mixed = jnp.concatenate([o_gla.reshape(B, S, -1), o_fox.reshape(B, S, -1)], axis=-1)
    return jnp.einsum('bse,ed->bsd', mixed.astype(h.dtype), w_out)


def _odd_mixer(h, w_in, w_out, lam_re, lam_im, log_dt, b_re, b_im, c_re, c_im, d_skip,
               w_glu, b_glu, ln_gain, ln_bias, w_s, b_s):
    proj = jnp.einsum('bsd,de->bse', h, w_in)
    s5_in, sgu_z = proj[..., :MIX_HALF], proj[..., MIX_HALF:]
    y = jax.nn.gelu(_s5(s5_in, lam_re, lam_im, log_dt, b_re, b_im, c_re, c_im, d_skip))
    y = y * jax.nn.sigmoid(jnp.einsum('bse,ef->bsf', y, w_glu.astype(jnp.float32)) + b_glu)
    y_sgu = _sgu(sgu_z, ln_gain, ln_bias, w_s, b_s)
    mixed = jnp.concatenate([y, y_sgu], axis=-1)
    return jnp.einsum('bse,ed->bsd', mixed.astype(h.dtype), w_out)


def setup_inputs(seed: int = 0) -> dict:
    key = jax.random.key(seed)
    ks = jax.random.split(key, 30)
    f32 = jnp.float32

    def nrm(k, shape, scale):
        return jax.random.normal(k, shape, f32) * scale

    n_idx = jnp.arange(S5_STATE, dtype=f32)
    return {
        "x": nrm(ks[0], (BATCH, SEQ, D_MODEL), 1.0),
        "c": nrm(ks[1], (BATCH, D_MODEL), 1.0),
        "ada_w": nrm(ks[2], (DEPTH, D_MODEL, 6 * D_MODEL), D_MODEL ** -0.5),
        "ada_b": nrm(ks[3], (DEPTH, 6 * D_MODEL), 0.01),
        "even_w_in": nrm(ks[4], (N_EVEN, D_MODEL, EVEN_WIDTH), D_MODEL ** -0.5),
        "even_w_out": nrm(ks[5], (N_EVEN, D_MODEL, D_MODEL), D_MODEL ** -0.5),
        "gla_w_lr": nrm(ks[6], (N_EVEN, GLA_RANK, MIX_HALF), GLA_RANK ** -0.5),
        "gla_b_lr": nrm(ks[7], (N_EVEN, MIX_HALF), 0.01),
        "gla_gain": 1.0 + nrm(ks[8], (N_EVEN, GLA_HEADS, HEAD_DIM), 0.01),
        "fox_b_f": nrm(ks[9], (N_EVEN, FOX_HEADS), 0.01),
        "fox_q_gain": 1.0 + nrm(ks[10], (N_EVEN, FOX_HEADS, HEAD_DIM), 0.01),
        "fox_k_gain": 1.0 + nrm(ks[11], (N_EVEN, FOX_HEADS, HEAD_DIM), 0.01),
        "odd_w_in": nrm(ks[12], (N_ODD, D_MODEL, ODD_WIDTH), D_MODEL ** -0.5),
        "odd_w_out": nrm(ks[13], (N_ODD, D_MODEL, D_MODEL), D_MODEL ** -0.5),
        "s5_lam_re": -0.5 + nrm(ks[14], (N_ODD, S5_GROUPS, S5_STATE), 0.01),
        "s5_lam_im": jnp.pi * n_idx + nrm(ks[15], (N_ODD, S5_GROUPS, S5_STATE), 0.01),
        "s5_log_dt": jax.random.uniform(ks[16], (N_ODD, S5_GROUPS), f32,
                                        minval=math.log(1e-3), maxval=math.log(1e-1)),
        "s5_b_re": nrm(ks[17], (N_ODD, S5_GROUPS, S5_STATE, S5_GROUP_WIDTH), (2 * S5_GROUP_WIDTH) ** -0.5),
        "s5_b_im": nrm(ks[18], (N_ODD, S5_GROUPS, S5_STATE, S5_GROUP_WIDTH), (2 * S5_GROUP_WIDTH) ** -0.5),
        "s5_c_re": nrm(ks[19], (N_ODD, S5_GROUPS, S5_GROUP_WIDTH, S5_STATE), (2 * S5_STATE) ** -0.5),
        "s5_c_im": nrm(ks[20], (N_ODD, S5_GROUPS, S5_GROUP_WIDTH, S5_STATE), (2 * S5_STATE) ** -0.5),
        "s5_d": nrm(ks[21], (N_ODD, S5_GROUPS, S5_GROUP_WIDTH), 1.0),
        "s5_w_glu": nrm(ks[22], (N_ODD, MIX_HALF, MIX_HALF), MIX_HALF ** -0.5),
        "s5_b_glu": nrm(ks[23], (N_ODD, MIX_HALF), 0.01),
        "sgu_ln_gain": 1.0 + nrm(ks[24], (N_ODD, MIX_HALF), 0.01),
        "sgu_ln_bias": nrm(ks[25], (N_ODD, MIX_HALF), 0.01),
        "sgu_w_s": nrm(ks[26], (N_ODD, SGU_GROUPS, SGU_CHUNK, SGU_CHUNK), SGU_CHUNK ** -0.5),
        "sgu_b_s": 1.0 + nrm(ks[27], (N_ODD, SGU_GROUPS, SGU_CHUNK), 0.01),
        "mlp_w1": nrm(ks[28], (DEPTH, D_MODEL, D_FF), D_MODEL ** -0.5),
        "mlp_w2": nrm(ks[29], (DEPTH, D_FF, D_MODEL), D_FF ** -0.5),
    }


def reference(x, c, ada_w, ada_b, even_w_in, even_w_out, gla_w_lr, gla_b_lr, gla_gain,
              fox_b_f, fox_q_gain, fox_k_gain, odd_w_in, odd_w_out, s5_lam_re, s5_lam_im,
              s5_log_dt, s5_b_re, s5_b_im, s5_c_re, s5_c_im, s5_d, s5_w_glu, s5_b_glu,
              sgu_ln_gain, sgu_ln_bias, sgu_w_s, sgu_b_s, mlp_w1, mlp_w2):
    c_act = jax.nn.silu(c)
    for layer in range(DEPTH):
        mod = jnp.einsum('bd,de->be', c_act, ada_w[layer]) + ada_b[layer]
        sh1, sc1, g1, sh2, sc2, g2 = jnp.split(mod.astype(jnp.float32), 6, axis=-1)
        h = (_rms(x) * (1.0 + sc1[:, None]) + sh1[:, None]).astype(x.dtype)
        i = layer // 2
        if layer % 2 == 0:
            y = _even_mixer(h, even_w_in[i], even_w_out[i], gla_w_lr[i], gla_b_lr[i], gla_gain[i],
                            fox_b_f[i], fox_q_gain[i], fox_k_gain[i])
        else:
            y = _odd_mixer(h, odd_w_in[i], odd_w_out[i], s5_lam_re[i], s5_lam_im[i], s5_log_dt[i],
                           s5_b_re[i], s5_b_im[i], s5_c_re[i], s5_c_im[i], s5_d[i], s5_w_glu[i],
                           s5_b_glu[i], sgu_ln_gain[i], sgu_ln_bias[i], sgu_w_s[i], sgu_b_s[i])
        x = x + (g1[:, None] * y).astype(x.dtype)
        h = (_rms(x) * (1.0 + sc2[:, None]) + sh2[:, None]).astype(x.dtype)
        hid = jnp.square(jax.nn.relu(jnp.einsum('bsd,df->bsf', h, mlp_w1[layer])))
        x = x + (g2[:, None] * jnp.einsum('bsf,fd->bsd', hid, mlp_w2[layer])).astype(x.dtype)
    return x
```

```python
import numpy as np
from contextlib import ExitStack
import concourse.bass as bass
import concourse.mybir as mybir
from concourse.bass_utils import run_bass_kernel_spmd

F32 = mybir.dt.float32
BF16 = mybir.dt.bfloat16
I32 = mybir.dt.int32
AF = mybir.ActivationFunctionType
ALU = mybir.AluOpType
AX = mybir.AxisListType

D = 1024
DFF = 4096
EPS = 1e-6
EVEN_W = 3608
SB_BASE = 16640
SB_LIMIT = 229344 - 64


class Buf:
    __slots__ = ("name", "w", "r")

    def __init__(self, name=""):
        self.name = name
        self.w = None
        self.r = []


class KB:
    EPOCH = 20000
    NDMA = 48
    NDMA_HW = 32
    SAME_GAP = 2

    def __init__(self, nc, stack):
        self.nc = nc
        self.stack = stack
        self.names = ["pe", "act", "dve", "pool", "sp"]
        self.stream = {e: [] for e in self.names}
        self.cnt = {e: 0 for e in self.names}
        self.pend = {e: False for e in self.names}
        self.seen = {e: {} for e in self.names}
        self.semh = {}
        self.dma_cnt = [0] * self.NDMA
        self.dma_rr = 0
        self.dma_rr_sw = 0
        self.nsem = 0
        self.ninstr = 0

    def _sem(self, key):
        h = self.semh.get(key)
        if h is None:
            self.nsem += 1
            h = self.stack.enter_context(self.nc.semaphore("s%d" % self.nsem))
            self.semh[key] = h
        return h

    def _next_token(self, eng):
        g = self.cnt[eng] + 1
        ep, v = divmod(g - 1, self.EPOCH)
        return (("e", eng, ep), v + 1)

    def _need(self, eng, key, val):
        if self.seen[eng].get(key, 0) >= val:
            return False
        if key[0] == "e":
            for k2 in self.seen[eng]:
                if k2[0] == "e" and k2[1] == key[1] and k2[2] > key[2]:
                    return False
        self.seen[eng][key] = val
        return True

    def _waits(self, eng, reads, writes, skip_same=False):
        toks = set()
        for b in reads:
            if b.w is not None:
                toks.add(b.w)
        for b in writes:
            if b.w is not None:
                toks.add(b.w)
            toks.update(b.r)
        out = []
        for key, val in sorted(toks, key=lambda t: (str(t[0]), t[1])):
            if skip_same and key[0] == "e" and key[1] == eng:
                continue
            if self._need(eng, key, val):
                out.append((key, val))
        return out

    def _mark(self, tok, reads, writes):
        for b in writes:
            b.w = tok
            b.r = []
        for b in reads:
            b.r = [t for t in b.r if t[0] != tok[0]]
            b.r.append(tok)

    def op(self, eng, fn, r=(), w=(), inc=True):
        waits = self._waits(eng, r, w, skip_same=(eng == "pe"))
        tok = self._next_token(eng)
        self.stream[eng].append((waits, fn, tok if inc else None))
        if inc:
            self.cnt[eng] += 1
            self.pend[eng] = False
        else:
            self.pend[eng] = True
        self._mark(tok, r, w)
        self.ninstr += 1
        return tok

    def pe(self, fn, r=(), w=(), inc=True):
        return self.op("pe", fn, r, w, inc)

    def act(self, fn, r=(), w=(), inc=True):
        return self.op("act", fn, r, w, inc)

    def dve(self, fn, r=(), w=(), inc=True):
        return self.op("dve", fn, r, w, inc)

    def pool(self, fn, r=(), w=(), inc=True):
        return self.op("pool", fn, r, w, inc)

    def dma(self, q, out, in_, r=(), w=(), **kw):
        if q == "pool":
            i = self.NDMA_HW + self.dma_rr_sw
            self.dma_rr_sw = (self.dma_rr_sw + 1) % (self.NDMA - self.NDMA_HW)
        else:
            i = self.dma_rr
            self.dma_rr = (self.dma_rr + 1) % self.NDMA_HW
        key = ("d", i)
        waits = self._waits(q, r, w)
        prev = self.dma_cnt[i]
        if prev > 0 and self._need(q, key, prev * 16):
            waits.append((key, prev * 16))
        self.dma_cnt[i] += 1
        tok = (key, self.dma_cnt[i] * 16)

        def fn(e, out=out, in_=in_, kw=kw):
            return e.dma_start(out=out, in_=in_, **kw)
        self.stream[q].append((waits, fn, ("dma", tok)))
        self._mark(tok, r, w)
        self.ninstr += 1
        return tok

    def all_tokens(self):
        toks = []
        for e in self.names:
            if self.cnt[e] > 0:
                ep, v = divmod(self.cnt[e] - 1, self.EPOCH)
                toks.append((("e", e, ep), v + 1))
        for i in range(self.NDMA):
            if self.dma_cnt[i]:
                toks.append((("d", i), self.dma_cnt[i] * 16))
        return toks

    def barrier(self, engs=None):
        for e in self.names:
            assert not self.pend[e], e
        toks = self.all_tokens()
        for e in (engs or self.names):
            waits = [(key, val) for key, val in toks
                     if not (key[0] == "e" and key[1] == e) and self._need(e, key, val)]
            if waits:
                self.stream[e].append((waits, None, None))

    def check(self):
        sem = {}
        ptr = {e: 0 for e in self.names}
        progress = True
        while progress:
            progress = False
            for e in self.names:
                items = self.stream[e]
                while ptr[e] < len(items):
                    waits, fn, tok = items[ptr[e]]
                    if any(sem.get(key, 0) < val for key, val in waits):
                        break
                    if tok is not None:
                        if tok[0] == "dma":
                            sem[tok[1][0]] = sem.get(tok[1][0], 0) + 16
                        else:
                            sem[tok[0]] = sem.get(tok[0], 0) + 1
                    ptr[e] += 1
                    progress = True
        for e in self.names:
            if ptr[e] < len(self.stream[e]):
                waits, fn, tok = self.stream[e][ptr[e]]
                bad = [(key, val, sem.get(key, 0)) for key, val in waits if sem.get(key, 0) < val]
                raise RuntimeError("DEADLOCK: engine %s stuck at item %d/%d waiting %s" % (e, ptr[e], len(self.stream[e]), bad))

    def emit(self):
        for e in self.names:
            assert not self.pend[e], "engine %s has trailing non-inc instructions" % e
        self.check()
        nc = self.nc
        handles = {"pe": "tensor", "act": "scalar", "dve": "vector", "pool": "gpsimd", "sp": "sync"}
        for e in self.names:
            for waits, fn, tok in self.stream[e]:
                for key, val in waits:
                    self._sem(key)
                if tok is not None:
                    self._sem(tok[1][0] if tok[0] == "dma" else tok[0])
        with nc.Block() as block:
            for e in self.names:
                items = self.stream[e]
                if not items:
                    continue

                def body(eng, items=items):
                    for waits, fn, tok in items:
                        for key, val in waits:
                            eng.wait_ge(self._sem(key), val)
                        if fn is None:
                            continue
                        ins = fn(eng)
                        if tok is not None:
                            if tok[0] == "dma":
                                ins.then_inc(self._sem(tok[1][0]), 16)
                            else:
                                ins.then_inc(self._sem(tok[0]), 1)
                getattr(block, handles[e])(body)


class Alloc:
    def __init__(self, nc):
        self.nc = nc
        self.off = SB_BASE
        self.n = 0

    def tile(self, shape, dt, name="t"):
        esz = 4 if dt in (F32, I32) else 2
        nbytes = int(np.prod(shape[1:])) * esz
        nbytes = (nbytes + 63) // 64 * 64
        assert self.off + nbytes <= SB_LIMIT, "SBUF overflow at %s: %d + %d" % (name, self.off, nbytes)
        self.n += 1
        h = self.nc.alloc_sbuf_tensor_at("%s_%d" % (name, self.n), list(shape), dt, offset=self.off)
        self.off += nbytes
        return h

    def mark(self):
        return self.off

    def reset(self, m):
        self.off = m


class Ctx:
    pass


def setup_consts(c):
    k, nc, A = c.k, c.nc, c.A
    c.ident_bf = A.tile([128, 128], BF16, "identbf")
    c.ident_f = A.tile([128, 128], F32, "identf")
    c.B_const = Buf("const")
    onesf = A.tile([128, 128], F32, "onesf")
    c.ones_f = onesf
    k.pool(lambda e: e.memset(onesf[:], 1.0), w=[c.B_const])
    k.pool(lambda e: e.affine_select(out=c.ident_f[:], in_=onesf[:], pattern=[[-1, 128]], compare_op=ALU.is_equal,
                                     fill=0.0, base=0, channel_multiplier=1), r=[c.B_const], w=[c.B_const])
    k.pool(lambda e: e.tensor_copy(out=c.ident_bf[:], in_=c.ident_f[:]), r=[c.B_const], w=[c.B_const])
    c.ones_bf = A.tile([128, 128], BF16, "onesbf")
    k.pool(lambda e: e.memset(c.ones_bf[:], 1.0), w=[c.B_const])
    c.eps_col = A.tile([128, 1], F32, "epscol")
    k.pool(lambda e: e.memset(c.eps_col[:], EPS), w=[c.B_const])


def masked(c, out_ap, in_ap, pattern, cmp, base, cm, fill=0.0, r=(), w=()):
    c.k.pool(lambda e: e.affine_select(out=out_ap, in_=in_ap, pattern=pattern, compare_op=cmp, fill=fill,
                                       base=base, channel_multiplier=cm), r=list(r), w=list(w))


def dram_cols_FM(ap2d):
    return ap2d.rearrange("(kt p) e -> p kt e", p=128)


def alloc_mod(c):
    c.modFM1 = c.A.tile([128, 4, 8], F32, "modFM")
    c.gbc1 = [c.A.tile([128, 1024], F32, "gbc") for _ in range(2)]
    c.modFM = [c.modFM1, c.modFM1]
    c.gbc = [c.gbc1, c.gbc1]
    c.B_mod = Buf("mod")


def phase_adaln(c, l, c_in, ada_w, ada_b):
    k, nc, A, ps = c.k, c.nc, c.A, c.ps
    m = A.mark()
    ccol = A.tile([128, 8], F32, "ccol")
    cact = A.tile([128, 8], F32, "cact")
    crep = A.tile([128, 8, 128], F32, "crep")
    wch = [A.tile([128, 8, 1024], F32, "wch") for _ in range(2)]
    bFM = A.tile([128, 8], F32, "bFM")
    brow = A.tile([128, 1024], F32, "brow")
    Bc, Bw, Bb, Bp = Buf(), [Buf(), Buf()], Buf(), [Buf(), Buf()]
    k.dma("sp", ccol[:], c_in.rearrange("(kt p) -> p kt", p=128), w=[Bc], allow_slow_non_contiguous=True)
    k.act(lambda e: e.activation(out=cact[:], in_=ccol[:], func=AF.Silu), r=[Bc], w=[Bc])
    k.dve(lambda e: e.tensor_copy(out=crep[:], in_=cact[:].unsqueeze(2).broadcast_to([128, 8, 128])), r=[Bc], w=[Bc])
    it = 0
    if True:
        for q in range(6):
            wb = it % 2
            it += 1
            cols = slice(q * 1024, (q + 1) * 1024)
            k.dma("sp", wch[wb][:], dram_cols_FM(ada_w[l][:, cols]), w=[Bw[wb]])
            if q in (2, 5):
                gi = 0 if q == 2 else 1
                k.dma("sp", brow[:], ada_b[l][cols].partition_broadcast(128), w=[Bb])
                for half in range(2):
                    pt = ps[half]
                    cs = slice(half * 512, (half + 1) * 512)
                    for kt in range(8):
                        k.pe(lambda e, pt=pt, kt=kt, wb=wb, cs=cs: e.matmul(pt[:, :], lhsT=crep[:, kt, :], rhs=wch[wb][:, kt, cs],
                                                                           start=(kt == 0), stop=(kt == 7)),
                             r=[Bc, Bw[wb]], w=[Bp[half]], inc=(kt == 7))
                    k.dve(lambda e, pt=pt, cs=cs, l=l, gi=gi: e.tensor_tensor(out=c.gbc[l][gi][:, cs], in0=pt[:, :], in1=brow[:, cs], op=ALU.add),
                          r=[Bp[half], Bb], w=[c.B_mod])
            else:
                mi = {0: 0, 1: 1, 3: 2, 4: 3}[q]
                k.dma("sp", bFM[:], ada_b[l][cols].rearrange("(j p) -> p j", p=128), w=[Bb], allow_slow_non_contiguous=True)
                pt = ps[2 + (it % 2)]
                Bq = Bp[0] if it % 2 == 0 else Bp[1]
                Bq = Buf()
                for j in range(8):
                    for kt in range(8):
                        last = (j == 7 and kt == 7)
                        k.pe(lambda e, pt=pt, kt=kt, wb=wb, j=j: e.matmul(pt[:, j:j + 1], lhsT=wch[wb][:, kt, j * 128:(j + 1) * 128],
                                                                          rhs=cact[:, kt:kt + 1], start=(kt == 0), stop=(kt == 7)),
                             r=[Bc, Bw[wb]], w=[c.B_ps[2 + (it % 2)]], inc=last)
                k.dve(lambda e, pt=pt, l=l, mi=mi: e.tensor_tensor(out=c.modFM[l][:, mi, :], in0=pt[:, 0:8], in1=bFM[:], op=ALU.add),
                      r=[c.B_ps[2 + (it % 2)], Bb], w=[c.B_mod])
                if mi in (1, 3):
                    k.dve(lambda e, l=l, mi=mi: e.tensor_scalar(out=c.modFM[l][:, mi, :], in0=c.modFM[l][:, mi, :], scalar1=1.0,
                                                               scalar2=None, op0=ALU.add), r=[c.B_mod], w=[c.B_mod])
    k.barrier()
    A.reset(m)


def norm_transpose(c, xt, Bx, hT, BhT, tcols, l, which, scr, pbank, Bpbank, part=0):
    k = c.k
    mi_sh, mi_sc = (0, 1) if which == 1 else (2, 3)
    if part in (0, 1):
      k.act(lambda e: e.activation(out=scr["hn"][:], in_=xt, func=AF.Square, accum_out=scr["ss"][:]), r=[Bx], w=[scr["Bhn"], scr["Bss"]])
      k.act(lambda e: e.activation(out=scr["ss"][:], in_=scr["ss"][:], func=AF.Sqrt, bias=c.eps_col[:], scale=1.0 / D),
            r=[scr["Bss"], c.B_const], w=[scr["Bss"]])
      k.dve(lambda e: e.reciprocal(out=scr["rstd"][:], in_=scr["ss"][:]), r=[scr["Bss"]], w=[scr["Brstd"]])
      k.dve(lambda e: e.tensor_scalar(out=scr["hn"][:], in0=xt, scalar1=scr["rstd"][:, 0:1], scalar2=None, op0=ALU.mult),
            r=[Bx, scr["Brstd"]], w=[scr["Bhn"]])
    if part == 1:
        return
    pv = pbank[:].bitcast(BF16)
    for j in range(8):
        k.pe(lambda e, j=j: e.transpose(pv[:, j * 128:(j + 1) * 128], scr["hn"][:, j * 128:(j + 1) * 128], c.ident_bf[:]),
             r=[scr["Bhn"], c.B_const], w=[Bpbank], inc=(j == 7))
    for j in range(8):
        eng = k.act if j % 2 == 0 else k.dve
        if j % 2 == 0:
            k.act(lambda e, j=j: e.activation(out=hT[:, j, tcols], in_=pv[:, j * 128:(j + 1) * 128], func=AF.Identity,
                                              bias=c.modFM[l][:, mi_sh, j:j + 1], scale=c.modFM[l][:, mi_sc, j:j + 1]),
                  r=[Bpbank, c.B_mod], w=[BhT])
        else:
            k.dve(lambda e, j=j: e.tensor_scalar(out=hT[:, j, tcols], in0=pv[:, j * 128:(j + 1) * 128],
                                                 scalar1=c.modFM[l][:, mi_sc, j:j + 1], scalar2=c.modFM[l][:, mi_sh, j:j + 1],
                                                 op0=ALU.mult, op1=ALU.add),
                  r=[Bpbank, c.B_mod], w=[BhT])


def norm_scratch(c, n=2):
    A = c.A
    out = []
    for i in range(n):
        out.append(dict(ss=A.tile([128, 1], F32, "ss"), rstd=A.tile([128, 1], F32, "rstd"),
                        hn=A.tile([128, 1024], BF16, "hn"), Bjunk=Buf(), Bss=Buf(), Brstd=Buf(), Bhn=Buf()))
    return out


def phase_outproj(c, l, MixT, Xin, Xout, w_out):
    k, nc, A, ps, S = c.k, c.nc, c.A, c.ps, c.S
    m = A.mark()
    Wo = A.tile([128, 8, 1024], BF16, "Wo")
    BWo = Buf()
    k.dma("pool", Wo[:], dram_cols_FM(w_out), w=[BWo])
    NB = 2
    mix = [A.tile([128, 8, 512], BF16, "mix") for _ in range(NB)]
    Bmix = [Buf() for _ in range(NB)]
    NX = 4
    xs = [A.tile([128, 1024], F32, "xo") for _ in range(NX)]
    Bxs = [Buf() for _ in range(NX)]
    tmp = [A.tile([128, 512], F32, "tmpo") for _ in range(2)]
    Btmp = [Buf(), Buf()]
    MixT_v = MixT.rearrange("(et p) t -> p et t", p=128)
    ti = 0
    pi = 0
    for sb in range(S // 512):
        b = sb % NB
        k.dma("sp", mix[b][:], MixT_v[:, :, sb * 512:(sb + 1) * 512], w=[Bmix[b]])
        for tt in range(4):
            t0 = sb * 512 + tt * 128
            xb = ti % NX
            ti += 1
            k.dma("sp", xs[xb][:], Xin[t0:t0 + 128, :], w=[Bxs[xb]])
            for half in range(2):
                pb = pi % 4
                pi += 1
                cs = slice(half * 512, (half + 1) * 512)
                for et in range(8):
                    k.pe(lambda e, pb=pb, b=b, et=et, tt=tt, cs=cs: e.matmul(ps[pb][:, :], lhsT=mix[b][:, et, tt * 128:(tt + 1) * 128],
                                                                             rhs=Wo[:, et, cs], start=(et == 0), stop=(et == 7)),
                         r=[Bmix[b], BWo], w=[c.B_ps[pb]], inc=(et == 7))
                th = pi % 2
                k.dve(lambda e, pb=pb, cs=cs, th=th: e.tensor_tensor(out=tmp[th][:], in0=ps[pb][:, :], in1=c.gbc[l][0][:, cs], op=ALU.mult),
                      r=[c.B_ps[pb], c.B_mod], w=[Btmp[th]])
                getattr(k, c.add_eng)(lambda e, xb=xb, cs=cs, th=th: e.tensor_tensor(out=xs[xb][:, cs], in0=xs[xb][:, cs], in1=tmp[th][:], op=ALU.add),
                       r=[Btmp[th], Bxs[xb]], w=[Bxs[xb]])
            k.dma(c.stq, Xout[t0:t0 + 128, :], xs[xb][:], r=[Bxs[xb]], w=[c.B_X[l]])
    k.barrier()
    A.reset(m)


def phase_mlp(c, l, Xin, Xout, w1, w2):
    k, nc, A, ps, S = c.k, c.nc, c.A, c.ps, c.S
    m = A.mark()
    W1 = A.tile([128, 8, DFF], BF16, "W1")
    W2 = A.tile([128, 32, D], BF16, "W2")
    BW1 = [Buf() for _ in range(4)]
    BW2 = [Buf() for _ in range(4)]
    w1v = dram_cols_FM(w1)
    w2v = dram_cols_FM(w2)
    for q in range(4):
        k.dma("pool", W1[:, :, q * 1024:(q + 1) * 1024], w1v[:, :, q * 1024:(q + 1) * 1024], w=[BW1[q]])
    for q in range(4):
        k.dma("pool", W2[:, q * 8:(q + 1) * 8, :], w2v[:, q * 8:(q + 1) * 8, :], w=[BW2[q]])
    NX = 3
    xs = [A.tile([128, 1024], F32, "xm") for _ in range(NX)]
    Bxs = [Buf() for _ in range(NX)]
    scr = norm_scratch(c, 2)
    hTs = [A.tile([128, 8, 512], BF16, "h2T") for _ in range(2)]
    BhTs = [[Buf() for _ in range(4)] for _ in range(2)]
    hid = A.tile([128, 32, 512], BF16, "hid")
    Bhid = [Buf() for _ in range(32)]
    tmp = [A.tile([128, 512], F32, "tmpm") for _ in range(2)]
    Btmp = [Buf(), Buf()]
    ti = 0
    hp = 0
    yp = 0
    tic = [0]

    def do_norm(sb_):
        for tt in range(4):
            t0 = sb_ * 512 + tt * 128
            xb = tic[0] % NX
            tic[0] += 1
            k.dma("sp", xs[xb][:], Xin[t0:t0 + 128, :], r=[c.B_X[l]], w=[Bxs[xb]])
            norm_transpose(c, xs[xb][:], Bxs[xb], hTs[sb_ % 2], BhTs[sb_ % 2][tt], slice(tt * 128, (tt + 1) * 128), l, 2, scr[tt % 2], ps[0], c.B_ps[0])

    do_norm(0)
    for sb in range(S // 512):
        hT = hTs[sb % 2]
        BhT = BhTs[sb % 2]
        for ft in range(32):
            if ft == 16 and sb + 1 < S // 512:
                do_norm(sb + 1)
            pb = 1 + hp % 3
            hp += 1
            for kt in range(8):
                k.pe(lambda e, hT=hT, pb=pb, kt=kt, ft=ft: e.matmul(ps[pb][:, :], lhsT=W1[:, kt, ft * 128:(ft + 1) * 128], rhs=hT[:, kt, :],
                                                             start=(kt == 0), stop=(kt == 7)),
                     r=BhT + [BW1[ft // 8]], w=[c.B_ps[pb]], inc=(kt == 7))
            if c.relu2_dve:
                k.dve(lambda e, pb=pb, ft=ft: e.scalar_tensor_tensor(out=hid[:, ft, :], in0=ps[pb][:, :], scalar=0.0, in1=ps[pb][:, :],
                                                                     op0=ALU.max, op1=ALU.mult), r=[c.B_ps[pb]], w=[Bhid[ft]])
            else:
                k.act(lambda e, pb=pb, ft=ft: e.activation(out=hid[:, ft, :], in_=ps[pb][:, :], func=AF.Relu), r=[c.B_ps[pb]], w=[Bhid[ft]])
                k.dve(lambda e, ft=ft: e.tensor_tensor(out=hid[:, ft, :], in0=hid[:, ft, :], in1=hid[:, ft, :], op=ALU.mult),
                      r=[Bhid[ft]], w=[Bhid[ft]])
        for tt in range(4):
            t0 = sb * 512 + tt * 128
            xb = tic[0] % NX
            tic[0] += 1
            k.dma("sp", xs[xb][:], Xin[t0:t0 + 128, :], r=[c.B_X[l]], w=[Bxs[xb]])
            for half in range(2):
                pb = 4 + yp % 4
                yp += 1
                cs = slice(half * 512, (half + 1) * 512)
                for ft in range(32):
                    k.pe(lambda e, pb=pb, ft=ft, tt=tt, cs=cs: e.matmul(ps[pb][:, :], lhsT=hid[:, ft, tt * 128:(tt + 1) * 128], rhs=W2[:, ft, cs],
                                                                        start=(ft == 0), stop=(ft == 31)),
                         r=[Bhid[ft], BW2[ft // 8]], w=[c.B_ps[pb]], inc=(ft == 31))
                th = yp % 2
                k.dve(lambda e, pb=pb, cs=cs, th=th: e.tensor_tensor(out=tmp[th][:], in0=ps[pb][:, :], in1=c.gbc[l][1][:, cs], op=ALU.mult),
                      r=[c.B_ps[pb], c.B_mod], w=[Btmp[th]])
                getattr(k, c.add_eng)(lambda e, xb=xb, cs=cs, th=th: e.tensor_tensor(out=xs[xb][:, cs], in0=xs[xb][:, cs], in1=tmp[th][:], op=ALU.add),
                       r=[Btmp[th], Bxs[xb]], w=[Bxs[xb]])
            k.dma(c.stq, Xout[t0:t0 + 128, :], xs[xb][:], r=[Bxs[xb]], w=[c.B_Xout[l]])
    k.barrier()
    A.reset(m)


def make_ctx(nc, st, S):
    c = Ctx()
    c.nc, c.S = nc, S
    c.k = KB(nc, st)
    c.A = Alloc(nc)
    c.ps = [nc.alloc_psum_tensor("psb%d" % i, [128, 512], F32) for i in range(8)]
    c.B_ps = [Buf("ps%d" % i) for i in range(8)]
    c.B_X = [Buf("X1_0"), Buf("X1_1")]
    c.B_Xout = [Buf("X2_0"), Buf("X2_1")]
    c.relu2_dve = False
    c.stq = "pool"
    c.add_eng = "dve"
    setup_consts(c)
    alloc_mod(c)
    return c


def finish(c):
    k = c.k
    k.barrier(engs=["sp"])
    k.emit()


def phase_n1_even(c, l, Xin, w_in, w_lr, b_lr, b_f, q_gain, k_gain, T):
    k, nc, A, ps, S = c.k, c.nc, c.A, c.ps, c.S
    m = A.mark()
    LSn = A.tile([8, S], F32, "LSn")
    BLS = Buf()
    m_post = A.mark()
    W = A.tile([128, 8, EVEN_W], BF16, "Win")
    BW = [Buf() for _ in range(4)]
    wv = dram_cols_FM(w_in)
    bounds = [0, 1024, 2048, 2576, EVEN_W]
    for q in range(4):
        k.dma("pool", W[:, :, bounds[q]:bounds[q + 1]], wv[:, :, bounds[q]:bounds[q + 1]], w=[BW[q]])

    def BWof(c0):
        return [BW[q] for q in range(4) if bounds[q] <= c0 < bounds[q + 1]][0]
    wlr = A.tile([17, 512], BF16, "wlr")
    Bsm = Buf("small")
    k.dma("pool", wlr[0:16, :], w_lr, w=[Bsm])
    k.dma("pool", wlr[16:17, :], b_lr.rearrange("(o e) -> o e", o=1), w=[Bsm])
    nbf = A.tile([8, 1], F32, "nbf")
    k.dma("sp", nbf[:], b_f.rearrange("(h o) -> h o", o=1), w=[Bsm], allow_slow_non_contiguous=True)
    k.dve(lambda e: e.tensor_scalar(out=nbf[:], in0=nbf[:], scalar1=-1.0, scalar2=None, op0=ALU.mult), r=[Bsm], w=[Bsm])
    grow = A.tile([128, 512], F32, "grow")
    g2 = A.tile([128, 512], F32, "grow2")
    k.dma("sp", grow[:], q_gain.rearrange("h d -> (h d)").partition_broadcast(128), w=[Bsm])
    k.dma("sp", g2[:], k_gain.rearrange("h d -> (h d)").partition_broadcast(128), w=[Bsm])
    k.dve(lambda e: e.scalar_tensor_tensor(out=grow[:], in0=grow[:], scalar=0.125, in1=g2[:], op0=ALU.mult, op1=ALU.mult), r=[Bsm], w=[Bsm])
    glrT = A.tile([17, 512], BF16, "glrT")
    Bglr = Buf()
    k.pool(lambda e: e.memset(glrT[:], 1.0), w=[Bglr])
    NX = 4
    xs = [A.tile([128, 1024], F32, "xn") for _ in range(NX)]
    Bxs = [Buf() for _ in range(NX)]
    scr = norm_scratch(c, 4)
    hTs = [A.tile([128, 8, 512], BF16, "h1T") for _ in range(2)]
    BhTs = [[Buf() for _ in range(4)] for _ in range(2)]
    NST = 6
    fst = [A.tile([128, 512], BF16, "fst") for _ in range(NST)]
    Bfst = [Buf() for _ in range(NST)]
    vst = [A.tile([128, 8, 65], BF16, "vst") for _ in range(2)]
    Bvst = [Buf(), Buf()]
    for i in range(2):
        k.pool(lambda e, i=i: e.memset(vst[i][:], 1.0), w=[Bvst[i]])
    sq = [A.tile([128, 512], F32, "sq") for _ in range(2)]
    Bsq = [Buf(), Buf()]
    ssh = [A.tile([128, 8], F32, "ssh") for _ in range(2)]
    Bssh = [Buf(), Buf()]
    qn = [A.tile([128, 512], BF16, "qn") for _ in range(2)]
    Bqn = [Buf(), Buf()]
    qTs = [A.tile([128, 4, 512], BF16, "qTs") for _ in range(2)]
    BqTs = [Buf(), Buf()]
    last = [A.tile([128, 512], F32, "last") for _ in range(2)]
    Blast = [Buf(), Buf()]
    e8 = A.tile([8, 512], F32, "e8")
    Be8 = Buf()
    cnt = dict(x=0, st=0, p=0, v=0, s=0, l=0)

    def pbank():
        b = (1, 2, 3, 6, 7)[cnt["p"] % 5]
        cnt["p"] += 1
        return b

    def stage():
        i = cnt["st"] % NST
        cnt["st"] += 1
        return i

    QgT, KgT, GgT, Kg, Vg, La, QfT, KfT, Vf = (T[n] for n in ("QgT", "KgT", "GgT", "Kg", "Vg", "La", "QfT", "KfT", "Vf"))
    xbuf = {}

    def do_norm1(sb_):
        for tt in range(4):
            t0 = sb_ * 512 + tt * 128
            xb = cnt["x"] % NX
            cnt["x"] += 1
            xbuf[(sb_, tt)] = xb
            k.dma("sp", xs[xb][:], Xin[t0:t0 + 128, :], w=[Bxs[xb]])
            norm_transpose(c, xs[xb][:], Bxs[xb], hTs[sb_ % 2], BhTs[sb_ % 2][tt], slice(tt * 128, (tt + 1) * 128), l, 1, scr[tt], ps[0], c.B_ps[0], part=1)

    def do_norm2(sb_):
        for tt in range(4):
            xb = xbuf[(sb_, tt)]
            norm_transpose(c, xs[xb][:], Bxs[xb], hTs[sb_ % 2], BhTs[sb_ % 2][tt], slice(tt * 128, (tt + 1) * 128), l, 1, scr[tt], ps[0], c.B_ps[0], part=2)

    deferred = []

    def flush():
        while deferred:
            deferred.pop(0)()

    do_norm1(0)
    do_norm2(0)
    for sb in range(S // 512):
        tc = slice(sb * 512, (sb + 1) * 512)
        hT = hTs[sb % 2]
        BhT = BhTs[sb % 2]
        if sb + 1 < S // 512:
            do_norm1(sb + 1)
        for mt in range(12):
            c0 = [0, 128, 256, 384, 512, 640, 768, 896, 1536, 1664, 1792, 1920][mt]
            pb = pbank()
            for kt in range(8):
                k.pe(lambda e, hT=hT, pb=pb, kt=kt, c0=c0: e.matmul(ps[pb][:, :], lhsT=W[:, kt, c0:c0 + 128], rhs=hT[:, kt, :], start=(kt == 0), stop=(kt == 7)),
                     r=BhT + [BWof(c0)], w=[c.B_ps[pb]], inc=(kt == 7))
            si = stage()
            if mt < 4:
                k.act(lambda e, pb=pb, si=si: e.mul(out=fst[si][:], in_=ps[pb][:, :], mul=0.125), r=[c.B_ps[pb]], w=[Bfst[si]])
                dst = QgT[mt * 128:(mt + 1) * 128, tc]
            elif mt < 8:
                k.dve(lambda e, pb=pb, si=si: e.tensor_copy(out=fst[si][:], in_=ps[pb][:, :]), r=[c.B_ps[pb]], w=[Bfst[si]])
                dst = KgT[(mt - 4) * 128:(mt - 3) * 128, tc]
            else:
                k.act(lambda e, pb=pb, si=si: e.activation(out=fst[si][:], in_=ps[pb][:, :], func=AF.Silu), r=[c.B_ps[pb]], w=[Bfst[si]])
                dst = GgT[(mt - 8) * 128:(mt - 7) * 128, tc]
            k.dma(c.stq, dst, fst[si][:], r=[Bfst[si]])
        pb = pbank()
        for kt in range(8):
            k.pe(lambda e, hT=hT, pb=pb, kt=kt: e.matmul(ps[pb][0:16, :], lhsT=W[:, kt, 2048:2064], rhs=hT[:, kt, :], start=(kt == 0), stop=(kt == 7)),
                 r=BhT + [BWof(2048)], w=[c.B_ps[pb]], inc=(kt == 7))
        k.dve(lambda e, pb=pb: e.tensor_copy(out=glrT[0:16, :], in_=ps[pb][0:16, :]), r=[c.B_ps[pb]], w=[Bglr])
        pb = pbank()
        for kt in range(8):
            k.pe(lambda e, hT=hT, pb=pb, kt=kt: e.matmul(ps[pb][0:8, :], lhsT=W[:, kt, 3600:3608], rhs=hT[:, kt, :], start=(kt == 0), stop=(kt == 7)),
                 r=BhT + [BWof(3600)], w=[c.B_ps[pb]], inc=(kt == 7))
        k.act(lambda e, pb=pb: e.activation(out=e8[:], in_=ps[pb][0:8, :], func=AF.Exp, bias=nbf[:, 0:1], scale=-1.0), r=[c.B_ps[pb], Bsm], w=[Be8])
        k.act(lambda e, tc=tc: e.activation(out=LSn[:, tc], in_=e8[:], func=AF.Ln, bias=1.0, scale=1.0), r=[Be8], w=[BLS])
        if sb + 1 < S // 512:
            do_norm2(sb + 1)
        for tt in range(4):
            t0 = sb * 512 + tt * 128
            tcs = slice(tt * 128, (tt + 1) * 128)
            pb = pbank()
            k.pe(lambda e, pb=pb, tcs=tcs: e.matmul(ps[pb][:, :], lhsT=glrT[0:17, tcs], rhs=wlr[0:17, :], start=True, stop=True),
                 r=[Bglr, Bsm], w=[c.B_ps[pb]])
            li = cnt["l"] % 2
            cnt["l"] += 1
            k.act(lambda e, pb=pb, li=li: e.activation(out=last[li][:], in_=ps[pb][:, :], func=AF.Exp, scale=-1.0), r=[c.B_ps[pb]], w=[Blast[li]])
            k.act(lambda e, li=li: e.activation(out=last[li][:], in_=last[li][:], func=AF.Ln, bias=1.0, scale=1.0), r=[Blast[li]], w=[Blast[li]])
            k.dve(lambda e, li=li: e.tensor_scalar(out=last[li][:], in0=last[li][:], scalar1=-1.0 / 16.0, scalar2=None, op0=ALU.mult), r=[Blast[li]], w=[Blast[li]])
            k.dma(c.stq, La[t0:t0 + 128, :], last[li][:], r=[Blast[li]])
            for gi, c0 in enumerate([512, 1024, 2064, 2576, 3088]):
                pb = pbank()
                for kt in range(8):
                    k.pe(lambda e, hT=hT, pb=pb, kt=kt, c0=c0, tcs=tcs: e.matmul(ps[pb][:, :], lhsT=hT[:, kt, tcs], rhs=W[:, kt, c0:c0 + 512], start=(kt == 0), stop=(kt == 7)),
                         r=[BhT[tt], BWof(c0), BWof(c0 + 511)], w=[c.B_ps[pb]], inc=(kt == 7))
                if gi < 2:
                    si = stage()
                    if gi == 0:
                        k.act(lambda e, pb=pb, si=si: e.copy(out=fst[si][:], in_=ps[pb][:, :]), r=[c.B_ps[pb]], w=[Bfst[si]])
                    else:
                        k.dve(lambda e, pb=pb, si=si: e.tensor_copy(out=fst[si][:], in_=ps[pb][:, :]), r=[c.B_ps[pb]], w=[Bfst[si]])
                    k.dma(c.stq, (Kg if gi == 0 else Vg)[t0:t0 + 128, :], fst[si][:], r=[Bfst[si]])
                elif gi == 4:
                    vi = cnt["v"] % 2
                    cnt["v"] += 1
                    k.act(lambda e, pb=pb, vi=vi: e.copy(out=vst[vi][:, :, 0:64], in_=ps[pb][:, :].rearrange("p (h d) -> p h d", d=64)),
                          r=[c.B_ps[pb]], w=[Bvst[vi]])
                    k.dma(c.stq, Vf[t0:t0 + 128, :, :], vst[vi][:], r=[Bvst[vi]])
                else:
                    qi = gi - 2
                    s_ = cnt["s"] % 2
                    cnt["s"] += 1
                    k.act(lambda e, pb=pb, s_=s_: e.activation(out=sq[s_][:], in_=ps[pb][:, :], func=AF.Square), r=[c.B_ps[pb]], w=[Bsq[s_]])
                    k.dve(lambda e, s_=s_: e.tensor_reduce(out=ssh[s_][:], in_=sq[s_][:].rearrange("p (h d) -> p h d", d=64), axis=AX.X, op=ALU.add),
                          r=[Bsq[s_]], w=[Bssh[s_]])
                    k.act(lambda e, s_=s_: e.activation(out=ssh[s_][:], in_=ssh[s_][:], func=AF.Sqrt, bias=c.eps_col[:], scale=1.0 / 64.0),
                          r=[Bssh[s_], c.B_const], w=[Bssh[s_]])
                    k.dve(lambda e, s_=s_: e.reciprocal(out=ssh[s_][:], in_=ssh[s_][:]), r=[Bssh[s_]], w=[Bssh[s_]])
                    if qi == 0:
                        k.dve(lambda e, pb=pb, s_=s_: e.tensor_tensor(out=sq[s_][:].rearrange("p (h d) -> p h d", d=64),
                                                                      in0=ps[pb][:, :].rearrange("p (h d) -> p h d", d=64),
                                                                      in1=ssh[s_][:].unsqueeze(2).broadcast_to([128, 8, 64]), op=ALU.mult),
                              r=[c.B_ps[pb], Bssh[s_]], w=[Bsq[s_]])
                        k.dve(lambda e, s_=s_: e.tensor_tensor(out=qn[s_][:], in0=sq[s_][:], in1=grow[:], op=ALU.mult), r=[Bsq[s_], Bsm], w=[Bqn[s_]])
                    else:
                        k.dve(lambda e, pb=pb, s_=s_: e.tensor_tensor(out=qn[s_][:].rearrange("p (h d) -> p h d", d=64),
                                                                      in0=ps[pb][:, :].rearrange("p (h d) -> p h d", d=64),
                                                                      in1=ssh[s_][:].unsqueeze(2).broadcast_to([128, 8, 64]), op=ALU.mult),
                              r=[c.B_ps[pb], Bssh[s_]], w=[Bqn[s_]])
                    def tr(qi=qi, s_=s_, tcs=tcs):
                        pv = ps[4 + qi][:].bitcast(BF16)
                        for pr in range(4):
                            k.pe(lambda e, pr=pr: e.transpose(pv[:, pr * 128:(pr + 1) * 128], qn[s_][:, pr * 128:(pr + 1) * 128], c.ident_bf[:]),
                                 r=[Bqn[s_], c.B_const], w=[c.B_ps[4 + qi]], inc=(pr == 3))
                        if qi == 0:
                            k.act(lambda e: e.copy(out=qTs[0][:, :, tcs], in_=pv[:, 0:512].rearrange("p (a t) -> p a t", t=128)),
                                  r=[c.B_ps[4]], w=[BqTs[0]])
                        else:
                            k.dve(lambda e: e.tensor_copy(out=qTs[1][:, :, tcs], in_=pv[:, 0:512].rearrange("p (a t) -> p a t", t=128)),
                                  r=[c.B_ps[5]], w=[BqTs[1]])
                    deferred.append(tr)
                if gi == 4 or gi == 1:
                    flush()
            flush()
        for qi, dst in enumerate([QfT, KfT]):
            for pr in range(4):
                for hh in range(2):
                    k.dma(c.stq, dst[2 * pr + hh, 0:64, tc], qTs[qi][64 * hh:64 * hh + 64, pr, :], r=[BqTs[qi]])
    k.barrier()
    A.reset(m_post)
    cumN = A.tile([8, S], F32, "cumN")
    Bcum = Buf()
    k.dve(lambda e: e.tensor_tensor_scan(out=cumN[:], data0=c.ones_f[0:8, 0:1].broadcast_to([8, S]), data1=LSn[:], initial=0.0,
                                         op0=ALU.mult, op1=ALU.add), r=[BLS, c.B_const], w=[Bcum])
    CH = min(2048, S)
    augq = A.tile([8, 6, CH], BF16, "augq")
    augk = A.tile([8, 6, CH], BF16, "augk")
    r1 = A.tile([8, CH], F32, "r1")
    hf = A.tile([8, CH], F32, "hf")
    Baq, Bak, Br1, Bhf = Buf(), Buf(), Buf(), Buf()
    for ch in range(S // CH):
        cs = slice(ch * CH, (ch + 1) * CH)
        k.pool(lambda e: e.memset(augq[:, 3:6, :], 1.0), w=[Baq])
        k.pool(lambda e: e.memset(augk[:, 0:3, :], 1.0), w=[Bak])
        src = cumN[:, cs]
        for j in range(3):
            k.dve(lambda e, j=j, src=src: e.tensor_copy(out=augk[:, 3 + j, :], in_=(src if j == 0 else r1[:])), r=[Bcum, Br1], w=[Bak])
            k.dve(lambda e, j=j: e.tensor_scalar(out=augq[:, j, :], in0=augk[:, 3 + j, :], scalar1=-1.0, scalar2=None, op0=ALU.mult), r=[Bak], w=[Baq])
            if j < 2:
                k.dve(lambda e, j=j: e.tensor_copy(out=hf[:], in_=augk[:, 3 + j, :]), r=[Bak], w=[Bhf])
                k.dve(lambda e, j=j, src=src: e.tensor_tensor(out=r1[:], in0=(src if j == 0 else r1[:]), in1=hf[:], op=ALU.subtract),
                      r=[Bcum, Bhf, Br1], w=[Br1])
        k.dma(c.stq, QfT[:, 64:70, cs], augq[:], r=[Baq])
        k.dma(c.stq, KfT[:, 64:70, cs], augk[:], r=[Bak])
    k.barrier()
    A.reset(m)


def phase_gla(c, T, gla_gain, MixT):
    k, nc, A, ps, S = c.k, c.nc, c.A, c.ps, c.S
    m = A.mark()
    Bc = Buf("glaconst")
    tri = A.tile([128, 128], F32, "tri")
    triBD = A.tile([128, 128], F32, "triBD")
    upBD = A.tile([128, 128], F32, "upBD")
    cind = A.tile([128, 2], F32, "cind")
    m64 = A.tile([128, 64], F32, "m64")
    bones = A.tile([128, 128], BF16, "bones")
    k.pool(lambda e: e.affine_select(out=tri[:], in_=c.ones_f[:], pattern=[[1, 128]], compare_op=ALU.is_ge, fill=0.0, base=0, channel_multiplier=-1),
           r=[c.B_const], w=[Bc])
    k.pool(lambda e: e.tensor_copy(out=triBD[:], in_=tri[:]), r=[Bc], w=[Bc])
    k.pool(lambda e: e.memset(triBD[0:64, 64:128], 0.0), w=[Bc])
    k.pool(lambda e: e.affine_select(out=upBD[:], in_=c.ones_f[:], pattern=[[-1, 128]], compare_op=ALU.is_gt, fill=0.0, base=0, channel_multiplier=1),
           r=[c.B_const], w=[Bc])
    k.pool(lambda e: e.memset(upBD[64:128, 0:64], 0.0), w=[Bc])
    k.pool(lambda e: e.memset(cind[:], 0.0), w=[Bc])
    k.pool(lambda e: e.memset(cind[0:64, 0:1], 1.0), w=[Bc])
    k.pool(lambda e: e.memset(cind[64:128, 1:2], 1.0), w=[Bc])
    k.pool(lambda e: e.tensor_copy(out=m64[0:64, :], in_=tri[0:64, 0:64]), r=[Bc], w=[Bc])
    k.pool(lambda e: e.tensor_copy(out=m64[64:128, :], in_=tri[64:128, 64:128]), r=[Bc], w=[Bc])
    k.pool(lambda e: e.memset(bones[:], 0.0), w=[Bc])
    k.pool(lambda e: e.memset(bones[0:64, 0:64], 1.0), w=[Bc])
    k.pool(lambda e: e.memset(bones[64:128, 64:128], 1.0), w=[Bc])
    gcol = A.tile([128, 4], F32, "gcol")
    for pr in range(4):
        k.dma("sp", gcol[:, pr:pr + 1], gla_gain[2 * pr:2 * pr + 2, :].rearrange("h (v o) -> (h v) o", o=1), w=[Bc], allow_slow_non_contiguous=True)
    state = A.tile([128, 4, 64], F32, "state")
    state_bf = A.tile([128, 4, 64], BF16, "statebf")
    Bst, Bstb = Buf(), Buf()
    k.dve(lambda e: e.memset(state[:], 0.0), w=[Bst])
    k.dve(lambda e: e.memset(state_bf[:], 0.0), w=[Bstb])
    NB = 2
    qT = [A.tile([128, 4, 512], BF16, "gqT") for _ in range(NB)]
    kT = [A.tile([128, 4, 512], BF16, "gkT") for _ in range(NB)]
    gT = [A.tile([128, 4, 512], BF16, "ggT") for _ in range(NB)]
    kg = [A.tile([128, 4, 512], BF16, "gkg") for _ in range(NB)]
    vg = [A.tile([128, 4, 512], BF16, "gvg") for _ in range(NB)]
    la = [A.tile([128, 4, 512], F32, "gla") for _ in range(NB)]
    Bin = [[Buf() for _ in range(6)] for _ in range(NB)]
    mixo = [A.tile([128, 4, 512], BF16, "gmix") for _ in range(NB)]
    Bmixo = [Buf() for _ in range(NB)]
    P2 = range(2)
    e1 = [A.tile([128, 512], F32, "ge1") for _ in P2]; Be1 = [Buf() for _ in P2]
    e2 = [A.tile([128, 512], F32, "ge2") for _ in P2]; Be2 = [Buf() for _ in P2]
    ed = [A.tile([128, 512], F32, "ged") for _ in P2]; Bed = [Buf() for _ in P2]
    ebs = [A.tile([128, 8], F32, "gebs") for _ in P2]; Bebs = [Buf() for _ in P2]
    qd = [A.tile([128, 4, 2, 128], BF16, "gqd") for _ in P2]; Bqd = [Buf() for _ in P2]
    e1z = [A.tile([128, 4, 2, 128], F32, "ge1z") for _ in P2]; Be1z = [Buf() for _ in P2]
    e2z = [A.tile([128, 2, 512], F32, "ge2z") for _ in P2]; Be2z = [Buf() for _ in P2]
    kd = [A.tile([128, 4, 128], BF16, "gkd") for _ in P2]; Bkd = [Buf() for _ in P2]
    kdec = [A.tile([128, 2, 512], BF16, "gkdec") for _ in P2]; Bkdec = [Buf() for _ in P2]
    scT = [A.tile([128, 8, 2, 64], BF16, "gscT") for _ in P2]; BscT = [Buf() for _ in P2]
    m64z = A.tile([128, 2, 64], F32, "m64z")
    k.pool(lambda e: e.memset(m64z[:], 0.0), w=[Bc])
    k.pool(lambda e: e.tensor_copy(out=m64z[0:64, 0, :], in_=tri[0:64, 0:64]), r=[Bc], w=[Bc])
    k.pool(lambda e: e.tensor_copy(out=m64z[64:128, 1, :], in_=tri[64:128, 64:128]), r=[Bc], w=[Bc])
    osb = A.tile([128, 512], F32, "gosb"); Bosb = Buf()
    sqb = A.tile([128, 512], BF16, "gsqb"); Bsqb = Buf()
    rs = A.tile([128, 512], F32, "grs"); Brs = Buf()
    BP = c.B_ps
    v3 = lambda ap: ap.rearrange("p (a t) -> p a t", a=4)
    NT = S // 128

    def load_sb(sb):
        b = sb % NB
        tc = slice(sb * 512, (sb + 1) * 512)
        k.dma("sp", qT[b][:], T["QgT"].rearrange("(a p) t -> p a t", p=128)[:, :, tc], w=[Bin[b][0]])
        k.dma("sp", kT[b][:], T["KgT"].rearrange("(a p) t -> p a t", p=128)[:, :, tc], w=[Bin[b][1]])
        k.dma("sp", gT[b][:], T["GgT"].rearrange("(a p) t -> p a t", p=128)[:, :, tc], w=[Bin[b][2]])
        k.dma("sp", kg[b][:], T["Kg"][tc, :].rearrange("(a p) e -> p a e", p=128), w=[Bin[b][3]])
        k.dma("sp", vg[b][:], T["Vg"][tc, :].rearrange("(a p) e -> p a e", p=128), w=[Bin[b][4]])
        k.dma("sp", la[b][:], T["La"][tc, :].rearrange("(a p) e -> p a e", p=128), w=[Bin[b][5]])
        k.dve(lambda e: e.tensor_tensor(out=gT[b][:], in0=gT[b][:], in1=gcol[:].unsqueeze(2).broadcast_to([128, 4, 512]), op=ALU.mult),
              r=[Bin[b][2], Bc], w=[Bin[b][2]])

    def stageA(i):
        sb, tt = divmod(i, 4)
        if tt == 0:
            load_sb(sb)
        b, q = sb % NB, i % 2
        tcs = slice(tt * 128, (tt + 1) * 128)
        for pr in range(4):
            k.pe(lambda e, pr=pr: e.matmul(ps[1][:, pr * 128:(pr + 1) * 128], lhsT=la[b][:, tt, pr * 128:(pr + 1) * 128], rhs=triBD[:],
                                           start=True, stop=True), r=[Bin[b][5], Bc], w=[BP[1]], inc=(pr == 3))
        for pr in range(4):
            k.pe(lambda e, pr=pr: e.matmul(ps[2][:, pr * 2:pr * 2 + 2], lhsT=la[b][:, tt, pr * 128:(pr + 1) * 128], rhs=cind[:],
                                           start=True, stop=True), r=[Bin[b][5], Bc], w=[BP[2]], inc=(pr == 3))
        k.pe(lambda e: e.matmul(ps[3][:, :], lhsT=upBD[:], rhs=la[b][:, tt, :], start=True, stop=True), r=[Bin[b][5], Bc], w=[BP[3]])
        k.act(lambda e: e.activation(out=e1[q][:], in_=ps[1][:, :], func=AF.Exp), r=[BP[1]], w=[Be1[q]])
        k.act(lambda e: e.activation(out=e2[q][:], in_=ps[1][:, :], func=AF.Exp, scale=-1.0), r=[BP[1]], w=[Be2[q]])
        k.act(lambda e: e.activation(out=ed[q][:], in_=ps[3][:, :], func=AF.Exp), r=[BP[3]], w=[Bed[q]])
        k.act(lambda e: e.activation(out=ebs[q][:], in_=ps[2][:, 0:8], func=AF.Exp), r=[BP[2]], w=[Bebs[q]])
        k.dve(lambda e: e.tensor_tensor(out=e1z[q][:], in0=v3(e1[q][:]).unsqueeze(2).broadcast_to([128, 4, 2, 128]),
                                        in1=cind[:].unsqueeze(1).unsqueeze(3).broadcast_to([128, 4, 2, 128]), op=ALU.mult), r=[Be1[q], Bc], w=[Be1z[q]])
        k.dve(lambda e: e.tensor_tensor(out=kd[q][:], in0=kT[b][:, :, tcs], in1=v3(e2[q][:]), op=ALU.mult), r=[Bin[b][1], Be2[q]], w=[Bkd[q]])
        k.dve(lambda e: e.tensor_tensor(out=e2z[q][:], in0=ed[q][:].unsqueeze(1).broadcast_to([128, 2, 512]),
                                        in1=cind[:].unsqueeze(2).broadcast_to([128, 2, 512]), op=ALU.mult), r=[Bed[q], Bc], w=[Be2z[q]])
        k.dve(lambda e: e.tensor_tensor(out=qd[q][:], in0=qT[b][:, :, tcs].unsqueeze(2).broadcast_to([128, 4, 2, 128]), in1=e1z[q][:], op=ALU.mult),
              r=[Bin[b][0], Be1z[q]], w=[Bqd[q]])
        k.dve(lambda e: e.tensor_tensor(out=kdec[q][:], in0=kg[b][:, tt, :].unsqueeze(1).broadcast_to([128, 2, 512]), in1=e2z[q][:], op=ALU.mult),
              r=[Bin[b][3], Be2z[q]], w=[Bkdec[q]])

    def stageB(i):
        q = i % 2
        n = 0
        for pr in range(4):
            for hh in range(2):
                for cc in range(2):
                    n += 1
                    k.pe(lambda e, pr=pr, hh=hh, cc=cc: e.matmul(ps[4][64 * cc:64 * cc + 64, (2 * pr + hh) * 64:(2 * pr + hh) * 64 + 64],
                                                                 lhsT=kd[q][:, pr, 64 * cc:64 * cc + 64],
                                                                 rhs=qd[q][:, pr, hh, 64 * cc:64 * cc + 64], start=True, stop=True),
                         r=[Bkd[q], Bqd[q]], w=[BP[4]], inc=(n == 16))
        k.dve(lambda e: e.tensor_tensor(out=scT[q][:], in0=ps[4][:, :].rearrange("p (a t) -> p a t", t=64).unsqueeze(2).broadcast_to([128, 8, 2, 64]),
                                        in1=m64z[:].unsqueeze(1).broadcast_to([128, 8, 2, 64]), op=ALU.mult), r=[BP[4], Bc], w=[BscT[q]])

    def stageC(i, cc):
        sb, tt = divmod(i, 4)
        b, q = sb % NB, i % 2
        n = 0
        for pr in range(4):
            for hh in range(2):
                h = 2 * pr + hh
                oc = slice(pr * 128 + cc * 64, pr * 128 + cc * 64 + 64)
                k.pe(lambda e, pr=pr, hh=hh, oc=oc: e.matmul(ps[5][64 * hh:64 * hh + 64, oc], lhsT=state_bf[:, pr, :],
                                                             rhs=qd[q][:, pr, hh, 64 * cc:64 * cc + 64], start=True, stop=False),
                     r=[Bstb, Bqd[q]], w=[BP[5]], inc=False)
                n += 1
                k.pe(lambda e, h=h, hh=hh, oc=oc: e.matmul(ps[5][64 * hh:64 * hh + 64, oc], lhsT=vg[b][:, tt, h * 64:(h + 1) * 64],
                                                           rhs=scT[q][:, h, cc, :], start=False, stop=True),
                     r=[Bin[b][4], BscT[q]], w=[BP[5]], inc=(n == 8))
        n = 0
        for pr in range(4):
            for hh in range(2):
                h = 2 * pr + hh
                n += 1
                k.pe(lambda e, h=h, hh=hh, pr=pr: e.matmul(ps[6][64 * hh:64 * hh + 64, (cc * 4 + pr) * 64:(cc * 4 + pr) * 64 + 64],
                                                           lhsT=kdec[q][:, cc, h * 64:(h + 1) * 64],
                                                           rhs=vg[b][:, tt, h * 64:(h + 1) * 64], start=True, stop=True),
                     r=[Bkdec[q], Bin[b][4]], w=[BP[6]], inc=(n == 8))
        for pr in range(4):
            k.dve(lambda e, pr=pr: e.scalar_tensor_tensor(out=state[:, pr, :], in0=state[:, pr, :], scalar=ebs[q][:, pr * 2 + cc:pr * 2 + cc + 1],
                                                          in1=ps[6][:, (cc * 4 + pr) * 64:(cc * 4 + pr) * 64 + 64], op0=ALU.mult, op1=ALU.add),
                  r=[Bst, Bebs[q], BP[6]], w=[Bst])
        k.act(lambda e: e.copy(out=state_bf[:], in_=state[:]), r=[Bst], w=[Bstb])

    def stageD(i):
        sb, tt = divmod(i, 4)
        b = sb % NB
        tcs = slice(tt * 128, (tt + 1) * 128)
        k.act(lambda e: e.copy(out=osb[:], in_=ps[5][:, :]), r=[BP[5]], w=[Bosb])
        k.act(lambda e: e.activation(out=sqb[:], in_=ps[5][:, :], func=AF.Square), r=[BP[5]], w=[Bsqb])
        k.pe(lambda e: e.matmul(ps[7][:, :], lhsT=bones[:], rhs=sqb[:], start=True, stop=True), r=[Bsqb, Bc], w=[BP[7]])
        k.act(lambda e: e.activation(out=rs[:], in_=ps[7][:, :], func=AF.Sqrt, bias=c.eps_col[:], scale=1.0 / 64.0), r=[BP[7], c.B_const], w=[Brs])
        k.dve(lambda e: e.reciprocal(out=rs[:], in_=rs[:]), r=[Brs], w=[Brs])
        k.dve(lambda e: e.tensor_tensor(out=osb[:], in0=osb[:], in1=rs[:], op=ALU.mult), r=[Bosb, Brs], w=[Bosb])
        k.dve(lambda e: e.tensor_tensor(out=mixo[b][:, :, tcs], in0=v3(osb[:]), in1=gT[b][:, :, tcs], op=ALU.mult),
              r=[Bosb, Bin[b][2]], w=[Bmixo[b]])
        if tt == 3:
            tc = slice(sb * 512, (sb + 1) * 512)
            k.dma(c.stq, MixT[0:512, :].rearrange("(a p) t -> p a t", p=128)[:, :, tc], mixo[b][:], r=[Bmixo[b]])

    stageA(0)
    stageB(0)
    for i in range(NT):
        if i + 1 < NT:
            stageA(i + 1)
        stageC(i, 0)
        if i + 1 < NT:
            stageB(i + 1)
        stageC(i, 1)
        stageD(i)
    k.barrier()
    A.reset(m)


def phase_fox(c, T, MixT):
    k, nc, A, ps, S = c.k, c.nc, c.A, c.ps, c.S
    m = A.mark()
    NBLK = S // 128
    Bc = Buf("foxconst")
    negm = A.tile([128, 128], BF16, "negm")
    zer = A.tile([128, 128], F32, "zer")
    k.pool(lambda e: e.memset(zer[:], 0.0), w=[Bc])
    k.pool(lambda e: e.affine_select(out=zer[:], in_=zer[:], pattern=[[1, 128]], compare_op=ALU.is_ge, fill=-30000.0, base=0, channel_multiplier=-1),
           r=[Bc], w=[Bc])
    k.pool(lambda e: e.tensor_copy(out=negm[:], in_=zer[:]), r=[Bc], w=[Bc])
    m8 = A.tile([128, 1], F32, "m8")
    k.pool(lambda e: e.memset(m8[:], -8.0), w=[Bc])
    V = A.tile([128, NBLK, 8 * 65], BF16, "foxV")
    BV = Buf()
    k.dma("sp", V[:], T["Vf"].rearrange("(a p) h d -> p a (h d)", p=128), w=[BV])
    KT = [A.tile([128, S], BF16, "foxK") for _ in range(2)]
    QT = [A.tile([128, S], BF16, "foxQ") for _ in range(2)]
    BKQ = [Buf(), Buf()]
    for hb_ in range(2):
        k.pool(lambda e, hb_=hb_: e.memset(KT[hb_][:], 0.0), w=[BKQ[hb_]])
        k.pool(lambda e, hb_=hb_: e.memset(QT[hb_][:], 0.0), w=[BKQ[hb_]])
    NPT = 4
    pt = [A.tile([128, 512], BF16, "foxP") for _ in range(NPT)]
    Bpt = [Buf() for _ in range(NPT)]
    osb = [A.tile([128, 512], F32, "foxO") for _ in range(2)]
    Bosb = [Buf(), Buf()]
    fout = [A.tile([64, 512], BF16, "foxF") for _ in range(2)]
    Bfout = [Buf(), Buf()]
    BP = c.B_ps
    work = []
    for h in range(8):
        for qb in range(S // 512):
            nj = 4 * qb + 4
            for j in range(nj):
                work.append((h, qb, j, nj))
    LA = 2
    state = dict(si=0)

    def issue_qk(i):
        h, qb, j, nj = work[i]
        hb = h % 2
        if qb == 0 and j == 0:
            k.dma("sp", KT[hb][0:70, :], T["KfT"][h], w=[BKQ[hb]])
            k.dma("sp", QT[hb][0:70, :], T["QfT"][h], w=[BKQ[hb]])
        sbank = i % 4
        q0 = qb * 512
        jj = j - 4 * qb
        lk = KT[hb][:, j * 128:(j + 1) * 128]
        if jj < 0:
            k.pe(lambda e: e.matmul(ps[sbank][:, :], lhsT=lk, rhs=QT[hb][:, q0:q0 + 512], start=True, stop=True), r=[BKQ[hb]], w=[BP[sbank]])
            c0 = 0
        else:
            c0 = 128 * jj
            k.pe(lambda e: e.matmul(ps[sbank][:, c0:c0 + 128], lhsT=lk, rhs=QT[hb][:, q0 + c0:q0 + c0 + 128], start=True, stop=False),
                 r=[BKQ[hb]], w=[BP[sbank]], inc=False)
            k.pe(lambda e: e.matmul(ps[sbank][:, c0:c0 + 128], lhsT=c.ident_bf[:], rhs=negm[:], start=False, stop=True),
                 r=[c.B_const, Bc], w=[BP[sbank]], inc=(c0 + 128 >= 512))
            if c0 + 128 < 512:
                k.pe(lambda e: e.matmul(ps[sbank][:, c0 + 128:512], lhsT=lk, rhs=QT[hb][:, q0 + c0 + 128:q0 + 512], start=True, stop=True),
                     r=[BKQ[hb]], w=[BP[sbank]])
        pi = i % NPT
        k.act(lambda e: e.activation(out=pt[pi][:, c0:512], in_=ps[sbank][:, c0:512], func=AF.Exp, bias=m8[:, 0:1], scale=1.0),
              r=[BP[sbank], Bc], w=[Bpt[pi]])

    def issue_pv(i):
        h, qb, j, nj = work[i]
        jj = j - 4 * qb
        c0 = 0 if jj < 0 else 128 * jj
        pi = i % NPT
        ob = 4 + (h * (S // 512) + qb) % 2
        k.pe(lambda e: e.matmul(ps[ob][0:65, c0:512], lhsT=V[:, j, h * 65:(h + 1) * 65], rhs=pt[pi][:, c0:512], start=(j == 0), stop=(j == nj - 1)),
             r=[BV, Bpt[pi]], w=[BP[ob]], inc=True)
        if j == nj - 1:
            oi = state["si"] % 2
            state["si"] += 1
            k.act(lambda e: e.copy(out=osb[oi][0:65, :], in_=ps[ob][0:65, :]), r=[BP[ob]], w=[Bosb[oi]])
            k.dve(lambda e: e.reciprocal(out=osb[oi][64:65, :], in_=osb[oi][64:65, :]), r=[Bosb[oi]], w=[Bosb[oi]])
            k.pe(lambda e: e.matmul(ps[6][0:64, :], lhsT=c.ones_f[64:65, 0:64], rhs=osb[oi][64:65, :], start=True, stop=True),
                 r=[Bosb[oi], c.B_const], w=[BP[6]])
            k.dve(lambda e: e.tensor_tensor(out=fout[oi][:], in0=osb[oi][0:64, :], in1=ps[6][0:64, :], op=ALU.mult), r=[Bosb[oi], BP[6]], w=[Bfout[oi]])
            k.dma(c.stq, MixT[512 + 64 * h:512 + 64 * h + 64, qb * 512:(qb + 1) * 512], fout[oi][:], r=[Bfout[oi]])

    n = len(work)
    for i in range(n + LA):
        if i < n:
            issue_qk(i)
        if i >= LA:
            issue_pv(i - LA)
    k.barrier()
    A.reset(m)


TWO_PI = 6.283185307179586
PI = 3.141592653589793


def s5_range_reduce(c, arg, shape, scr_i, scr_f, Bs):
    k = c.k
    k.dve(lambda e: e.tensor_scalar(out=scr_i, in0=arg, scalar1=1.0 / TWO_PI, scalar2=None, op0=ALU.mult), r=[Bs], w=[Bs])
    k.dve(lambda e: e.tensor_copy(out=scr_f, in_=scr_i), r=[Bs], w=[Bs])
    k.dve(lambda e: e.scalar_tensor_tensor(out=arg, in0=scr_f, scalar=-TWO_PI, in1=arg, op0=ALU.mult, op1=ALU.add), r=[Bs], w=[Bs])
    k.dve(lambda e: e.tensor_scalar(out=scr_f, in0=arg, scalar1=PI, scalar2=TWO_PI, op0=ALU.is_gt, op1=ALU.mult), r=[Bs], w=[Bs])
    k.dve(lambda e: e.tensor_tensor(out=arg, in0=arg, in1=scr_f, op=ALU.subtract), r=[Bs], w=[Bs])
    k.dve(lambda e: e.tensor_scalar(out=scr_f, in0=arg, scalar1=-PI, scalar2=TWO_PI, op0=ALU.is_lt, op1=ALU.mult), r=[Bs], w=[Bs])
    k.dve(lambda e: e.tensor_tensor(out=arg, in0=arg, in1=scr_f, op=ALU.add), r=[Bs], w=[Bs])
    k.dve(lambda e: e.tensor_scalar(out=arg, in0=arg, scalar1=PI, scalar2=-PI, op0=ALU.min, op1=ALU.max), r=[Bs], w=[Bs])


def s5_disc(c, Lre, Lim, ldt, F, Bs):
    k, A = c.k, c.A
    t = {n: A.tile([128, F], F32, "s5_" + n) for n in ("dt", "lr", "th", "r", "sn", "cs", "ar", "ai", "den", "t1", "t2", "cre", "cim", "sf")}
    ti = A.tile([128, F], I32, "s5_i")
    k.act(lambda e: e.activation(out=t["dt"][:], in_=ldt, func=AF.Exp), r=[Bs], w=[Bs])
    k.dve(lambda e: e.tensor_tensor(out=t["lr"][:], in0=Lre, in1=t["dt"][:], op=ALU.mult), r=[Bs], w=[Bs])
    k.dve(lambda e: e.tensor_tensor(out=t["th"][:], in0=Lim, in1=t["dt"][:], op=ALU.mult), r=[Bs], w=[Bs])
    k.act(lambda e: e.activation(out=t["r"][:], in_=t["lr"][:], func=AF.Exp), r=[Bs], w=[Bs])
    k.dve(lambda e: e.tensor_copy(out=t["sn"][:], in_=t["th"][:]), r=[Bs], w=[Bs])
    s5_range_reduce(c, t["sn"][:], [128, F], ti[:], t["sf"][:], Bs)
    k.act(lambda e: e.activation(out=t["sn"][:], in_=t["sn"][:], func=AF.Sin), r=[Bs], w=[Bs])
    k.dve(lambda e: e.tensor_scalar(out=t["cs"][:], in0=t["th"][:], scalar1=PI / 2, scalar2=None, op0=ALU.add), r=[Bs], w=[Bs])
    s5_range_reduce(c, t["cs"][:], [128, F], ti[:], t["sf"][:], Bs)
    k.act(lambda e: e.activation(out=t["cs"][:], in_=t["cs"][:], func=AF.Sin), r=[Bs], w=[Bs])
    k.dve(lambda e: e.tensor_tensor(out=t["ar"][:], in0=t["r"][:], in1=t["cs"][:], op=ALU.mult), r=[Bs], w=[Bs])
    k.dve(lambda e: e.tensor_tensor(out=t["ai"][:], in0=t["r"][:], in1=t["sn"][:], op=ALU.mult), r=[Bs], w=[Bs])
    k.dve(lambda e: e.tensor_tensor(out=t["den"][:], in0=Lre, in1=Lre, op=ALU.mult), r=[Bs], w=[Bs])
    k.dve(lambda e: e.tensor_tensor(out=t["t1"][:], in0=Lim, in1=Lim, op=ALU.mult), r=[Bs], w=[Bs])
    k.dve(lambda e: e.tensor_tensor(out=t["den"][:], in0=t["den"][:], in1=t["t1"][:], op=ALU.add), r=[Bs], w=[Bs])
    k.dve(lambda e: e.reciprocal(out=t["den"][:], in_=t["den"][:]), r=[Bs], w=[Bs])
    k.dve(lambda e: e.tensor_scalar(out=t["t1"][:], in0=t["ar"][:], scalar1=-1.0, scalar2=None, op0=ALU.add), r=[Bs], w=[Bs])
    k.dve(lambda e: e.tensor_tensor(out=t["cre"][:], in0=t["t1"][:], in1=Lre, op=ALU.mult), r=[Bs], w=[Bs])
    k.dve(lambda e: e.tensor_tensor(out=t["t2"][:], in0=t["ai"][:], in1=Lim, op=ALU.mult), r=[Bs], w=[Bs])
    k.dve(lambda e: e.tensor_tensor(out=t["cre"][:], in0=t["cre"][:], in1=t["t2"][:], op=ALU.add), r=[Bs], w=[Bs])
    k.dve(lambda e: e.tensor_tensor(out=t["cre"][:], in0=t["cre"][:], in1=t["den"][:], op=ALU.mult), r=[Bs], w=[Bs])
    k.dve(lambda e: e.tensor_tensor(out=t["cim"][:], in0=t["ai"][:], in1=Lre, op=ALU.mult), r=[Bs], w=[Bs])
    k.dve(lambda e: e.tensor_tensor(out=t["t2"][:], in0=t["t1"][:], in1=Lim, op=ALU.mult), r=[Bs], w=[Bs])
    k.dve(lambda e: e.tensor_tensor(out=t["cim"][:], in0=t["cim"][:], in1=t["t2"][:], op=ALU.subtract), r=[Bs], w=[Bs])
    k.dve(lambda e: e.tensor_tensor(out=t["cim"][:], in0=t["cim"][:], in1=t["den"][:], op=ALU.mult), r=[Bs], w=[Bs])
    return t


def phase_odd(c, l, Xin, w_in, P, MixT):
    k, nc, A, ps, S = c.k, c.nc, c.A, c.ps, c.S
    BP = c.B_ps
    m0 = A.mark()
    LB = 512
    W = A.tile([128, 8, 1536], BF16, "Wodd")
    BW = [Buf() for _ in range(3)]
    wv = dram_cols_FM(w_in)
    for q in range(3):
        k.dma("pool", W[:, :, q * 512:(q + 1) * 512], wv[:, :, q * 512:(q + 1) * 512], w=[BW[q]])
    Bs = Buf("s5prep")
    Bloads = []

    def newbuf():
        bb_ = Buf()
        Bloads.append(bb_)
        return bb_

    def join_loads():
        if Bloads:
            k.dve(lambda e: e.memset(jn[:], 0.0), r=list(Bloads), w=[Bs])
            del Bloads[:]

    Ecb = A.tile([128, 16, LB], BF16, "Ecb")
    Esb = A.tile([128, 16, LB], BF16, "Esb")
    EcL = A.tile([128, 16], F32, "EcL")
    EsL = A.tile([128, 16], F32, "EsL")
    rcol = A.tile([128, 16], F32, "rcol")
    Bbd = [A.tile([128, 4, 512], BF16, "Bbd") for _ in range(2)]
    Cbd = [A.tile([128, 16, 128], BF16, "Cbd") for _ in range(2)]
    Dd = A.tile([128, 4, 128], BF16, "Dd")
    Wglu = A.tile([128, 4, 512], BF16, "Wglu")
    bglu = A.tile([128, 4], F32, "bglu")
    WmT = A.tile([128, 8, 128], BF16, "WmT")
    bsel = A.tile([8, 4, 128], BF16, "bsel")
    bs8 = A.tile([8, 128], BF16, "bs8")
    lnrow = [A.tile([128, 512], F32, "lnrow") for _ in range(2)]
    jn = A.tile([128, 1], F32, "jn")
    z0 = A.tile([128, 16, 2], F32, "z0")
    Bz0 = Buf()
    k.dve(lambda e: e.memset(z0[:], 0.0), w=[Bz0])
    mp = A.mark()
    Ec = A.tile([128, 16, LB], F32, "Ec")
    Es = A.tile([128, 16, LB], F32, "Es")
    LA_re = A.tile([128, 16], F32, "LAre"); LA_im = A.tile([128, 16], F32, "LAim"); LA_dt = A.tile([128, 16], F32, "LAdt")
    for gl in range(2):
        k.dma("sp", LA_re[64 * gl:64 * gl + 64, :], P["lam_re"].rearrange("(m gl) p -> gl p m", gl=2)[gl], w=[newbuf()], allow_slow_non_contiguous=True)
        k.dma("sp", LA_im[64 * gl:64 * gl + 64, :], P["lam_im"].rearrange("(m gl) p -> gl p m", gl=2)[gl], w=[newbuf()], allow_slow_non_contiguous=True)
    ldv = P["log_dt"].rearrange("(m gl) -> gl m", gl=2)
    for gl in range(2):
        k.dma("sp", LA_dt[64 * gl:64 * gl + 64, :], ldv[gl].partition_broadcast(64), w=[newbuf()], allow_slow_non_contiguous=True)
    join_loads()
    dA = s5_disc(c, LA_re[:], LA_im[:], LA_dt[:], 16, Bs)
    k.dve(lambda e: e.tensor_copy(out=rcol[:], in_=dA["r"][:]), r=[Bs], w=[Bs])
    sI = A.tile([128, LB], F32, "sI")
    sIi = A.tile([128, LB], I32, "sIi")
    k.pool(lambda e: e.iota(sIi[:], pattern=[[1, LB]], base=1, channel_multiplier=0), w=[Bs])
    k.dve(lambda e: e.tensor_copy(out=sI[:], in_=sIi[:]), r=[Bs], w=[Bs])
    argi = A.tile([128, 4, LB], I32, "argi")
    argf = A.tile([128, 4, LB], F32, "argf")
    for tab, shift in ((Es, 0.0), (Ec, PI / 2)):
        for mq in range(4):
            tv = tab[:, 4 * mq:4 * mq + 4, :]
            k.dve(lambda e, tv=tv, mq=mq: e.tensor_tensor(out=tv, in0=dA["th"][:, 4 * mq:4 * mq + 4].unsqueeze(2).broadcast_to([128, 4, LB]),
                                                          in1=sI[:].unsqueeze(1).broadcast_to([128, 4, LB]), op=ALU.mult), r=[Bs], w=[Bs])
            if shift:
                k.dve(lambda e, tv=tv, shift=shift: e.tensor_scalar(out=tv, in0=tv, scalar1=shift, scalar2=None, op0=ALU.add), r=[Bs], w=[Bs])
            s5_range_reduce(c, tv, None, argi[:], argf[:], Bs)
            k.act(lambda e, tv=tv: e.activation(out=tv, in_=tv, func=AF.Sin), r=[Bs], w=[Bs])
    k.dve(lambda e: e.tensor_copy(out=EcL[:], in_=Ec[:, :, LB - 1]), r=[Bs], w=[Bs])
    k.dve(lambda e: e.tensor_copy(out=EsL[:], in_=Es[:, :, LB - 1]), r=[Bs], w=[Bs])
    k.act(lambda e: e.copy(out=Ecb[:], in_=Ec[:]), r=[Bs], w=[Bs])
    k.act(lambda e: e.copy(out=Esb[:], in_=Es[:]), r=[Bs], w=[Bs])
    k.barrier()
    A.reset(mp)
    CT = [A.tile([128, 16, 16], F32, "CT") for _ in range(2)]
    for gl in range(2):
        for m in range(16):
            k.dma("sp", CT[0][64 * gl:64 * gl + 64, m, :], P["c_re"][2 * m + gl].rearrange("i p -> p i"), w=[newbuf()], allow_slow_non_contiguous=True)
            k.dma("sp", CT[1][64 * gl:64 * gl + 64, m, :], P["c_im"][2 * m + gl].rearrange("i p -> p i"), w=[newbuf()], allow_slow_non_contiguous=True)
    join_loads()
    k.dve(lambda e: e.tensor_scalar(out=CT[1][:], in0=CT[1][:], scalar1=-1.0, scalar2=None, op0=ALU.mult), r=[Bs], w=[Bs])
    for ri in range(2):
        k.dve(lambda e, ri=ri: e.memset(Cbd[ri][:], 0.0), w=[Bs])
        for m in range(16):
            for gl in range(2):
                c0 = (2 * (m % 4) + gl) * 16
                k.dve(lambda e, ri=ri, m=m, gl=gl, c0=c0: e.tensor_copy(out=Cbd[ri][64 * gl:64 * gl + 64, m, c0:c0 + 16], in_=CT[ri][64 * gl:64 * gl + 64, m, :]),
                      r=[Bs], w=[Bs])
    LB_re = A.tile([128, 4, 64], F32, "LBre"); LB_im = A.tile([128, 4, 64], F32, "LBim"); LB_d4 = A.tile([128, 4], F32, "LBd4")
    LB_dt = A.tile([128, 4, 64], F32, "LBdt")
    for gl in range(8):
        k.dma("sp", LB_re[16 * gl:16 * gl + 16, :, :], P["lam_re"].rearrange("(ut gl) p -> gl ut p", gl=8)[gl].partition_broadcast(16), w=[newbuf()])
        k.dma("sp", LB_im[16 * gl:16 * gl + 16, :, :], P["lam_im"].rearrange("(ut gl) p -> gl ut p", gl=8)[gl].partition_broadcast(16), w=[newbuf()])
        k.dma("sp", LB_d4[16 * gl:16 * gl + 16, :], P["log_dt"].rearrange("(ut gl) -> gl ut", gl=8)[gl].partition_broadcast(16), w=[newbuf()],
              allow_slow_non_contiguous=True)
    join_loads()
    k.dve(lambda e: e.tensor_copy(out=LB_dt[:], in_=LB_d4[:].unsqueeze(2).broadcast_to([128, 4, 64])), r=[Bs], w=[Bs])
    fl = lambda t_: t_[:].rearrange("p a b -> p (a b)")
    dB = s5_disc(c, fl(LB_re), fl(LB_im), fl(LB_dt), 256, Bs)
    Bt = [A.tile([128, 4, 64], F32, "Bt") for _ in range(2)]
    for gl in range(8):
        for ut in range(4):
            k.dma("sp", Bt[0][16 * gl:16 * gl + 16, ut, :], P["b_re"][8 * ut + gl].rearrange("p j -> j p"), w=[newbuf()], allow_slow_non_contiguous=True)
            k.dma("sp", Bt[1][16 * gl:16 * gl + 16, ut, :], P["b_im"][8 * ut + gl].rearrange("p j -> j p"), w=[newbuf()], allow_slow_non_contiguous=True)
    join_loads()
    bb = [A.tile([128, 256], F32, "bbar") for _ in range(2)]
    tb = A.tile([128, 256], F32, "tb")
    k.dve(lambda e: e.tensor_tensor(out=bb[0][:], in0=dB["cre"][:], in1=fl(Bt[0]), op=ALU.mult), r=[Bs], w=[Bs])
    k.dve(lambda e: e.tensor_tensor(out=tb[:], in0=dB["cim"][:], in1=fl(Bt[1]), op=ALU.mult), r=[Bs], w=[Bs])
    k.dve(lambda e: e.tensor_tensor(out=bb[0][:], in0=bb[0][:], in1=tb[:], op=ALU.subtract), r=[Bs], w=[Bs])
    k.dve(lambda e: e.tensor_tensor(out=bb[1][:], in0=dB["cre"][:], in1=fl(Bt[1]), op=ALU.mult), r=[Bs], w=[Bs])
    k.dve(lambda e: e.tensor_tensor(out=tb[:], in0=dB["cim"][:], in1=fl(Bt[0]), op=ALU.mult), r=[Bs], w=[Bs])
    k.dve(lambda e: e.tensor_tensor(out=bb[1][:], in0=bb[1][:], in1=tb[:], op=ALU.add), r=[Bs], w=[Bs])
    mk8 = A.tile([128, 8], F32, "mk8")
    k.pool(lambda e: e.memset(mk8[:], 1.0), w=[Bs])
    k.pool(lambda e: e.affine_select(out=mk8[:], in_=mk8[:], pattern=[[-16, 8]], compare_op=ALU.is_ge, fill=0.0, base=0, channel_multiplier=1), r=[Bs], w=[Bs])
    k.pool(lambda e: e.affine_select(out=mk8[:], in_=mk8[:], pattern=[[16, 8]], compare_op=ALU.is_gt, fill=0.0, base=16, channel_multiplier=-1), r=[Bs], w=[Bs])
    for ri in range(2):
        k.dve(lambda e, ri=ri: e.tensor_tensor(out=Bbd[ri][:].rearrange("p u (g q) -> p u g q", q=64),
                                               in0=bb[ri][:].rearrange("p (u q) -> p u q", q=64).unsqueeze(2).broadcast_to([128, 4, 8, 64]),
                                               in1=mk8[:].unsqueeze(1).unsqueeze(3).broadcast_to([128, 4, 8, 64]), op=ALU.mult), r=[Bs], w=[Bs])
    dcol = A.tile([128, 4], F32, "dcol")
    for gl in range(8):
        k.dma("sp", dcol[16 * gl:16 * gl + 16, :], P["d"].rearrange("(o gl) j -> gl j o", gl=8)[gl], w=[newbuf()], allow_slow_non_contiguous=True)
    join_loads()
    for o in range(4):
        k.dve(lambda e, o=o: e.tensor_scalar(out=Dd[:, o, :], in0=c.ident_f[:], scalar1=dcol[:, o:o + 1], scalar2=None, op0=ALU.mult), r=[Bs, c.B_const], w=[Bs])
    k.dma("pool", Wglu[:], P["w_glu"].rearrange("(o p) f -> p o f", p=128), w=[newbuf()])
    k.dma("sp", bglu[:], P["b_glu"].rearrange("(f p) -> p f", p=128), w=[newbuf()], allow_slow_non_contiguous=True)
    wtmp = A.tile([128, 8, 128], F32, "wtmp")
    k.dma("sp", wtmp[:], P["w_s"].rearrange("g t s -> t g s"), w=[newbuf()])
    join_loads()
    for g in range(8):
        k.pe(lambda e, g=g: e.transpose(ps[7][:, g * 128:(g + 1) * 128] if g < 4 else ps[6][:, (g - 4) * 128:(g - 3) * 128], wtmp[:, g, :], c.ident_f[:]),
             r=[Bs, c.B_const], w=[BP[7] if g < 4 else BP[6]])
    for hb, bank in ((0, 7), (1, 6)):
        k.dve(lambda e, hb=hb, bank=bank: e.tensor_copy(out=wtmp[:, 4 * hb:4 * hb + 4, :], in_=ps[bank][:, :].rearrange("p (g t) -> p g t", t=128)), r=[BP[bank]], w=[Bs])
    k.pool(lambda e: e.affine_select(out=wtmp[:], in_=wtmp[:], pattern=[[0, 8], [1, 128]], compare_op=ALU.is_ge, fill=0.0, base=0, channel_multiplier=-1),
           r=[Bs], w=[Bs])
    k.dve(lambda e: e.tensor_copy(out=WmT[:], in_=wtmp[:]), r=[Bs], w=[Bs])
    k.dma("pool", bs8[:], P["b_s"], w=[newbuf()])
    join_loads()
    bself = A.tile([8, 4, 128], F32, "bself")
    k.pool(lambda e: e.memset(bself[:], 1.0), w=[Bs])
    k.pool(lambda e: e.affine_select(out=bself[:], in_=bself[:], pattern=[[128, 4], [1, 128]], compare_op=ALU.is_ge, fill=0.0, base=0, channel_multiplier=-64),
           r=[Bs], w=[Bs])
    k.pool(lambda e: e.affine_select(out=bself[:], in_=bself[:], pattern=[[-128, 4], [-1, 128]], compare_op=ALU.is_gt, fill=0.0, base=64, channel_multiplier=64),
           r=[Bs], w=[Bs])
    k.dve(lambda e: e.tensor_copy(out=bsel[:], in_=bself[:]), r=[Bs], w=[Bs])
    k.dma("sp", lnrow[0][:], P["ln_gain"].partition_broadcast(128), w=[newbuf()])
    k.dma("sp", lnrow[1][:], P["ln_bias"].partition_broadcast(128), w=[newbuf()])
    join_loads()
    k.barrier()
    A.reset(mp)
    NX = 2
    xs = [A.tile([128, 1024], F32, "xo") for _ in range(NX)]
    Bxs = [Buf() for _ in range(NX)]
    scr = norm_scratch(c, 2)
    hTs = [A.tile([128, 8, 512], BF16, "h1T") for _ in range(2)]
    BhTs = [[Buf() for _ in range(4)] for _ in range(2)]
    uT = A.tile([128, 4, 512], BF16, "uT"); BuT = [Buf() for _ in range(4)]
    suT = A.tile([128, 4, 512], BF16, "suT"); BsuT = [Buf() for _ in range(4)]
    vg = [A.tile([128, 512], F32, "vgel") for _ in range(2)]; Bvg = [Buf(), Buf()]
    vsq = A.tile([128, 512], BF16, "vsq"); Bvsq = Buf()
    st4 = [A.tile([128, 4], F32, "st4") for _ in range(2)]; Bst4 = [Buf(), Buf()]
    vn = A.tile([128, 4, 512], BF16, "vn"); Bvn = [Buf() for _ in range(4)]
    vtmp = [A.tile([128, 512], F32, "vtmp") for _ in range(2)]; Bvtmp = [Buf(), Buf()]
    mixo = [A.tile([128, 8, 512], BF16, "omix")] * 2; Bmixo = [Buf()] * 2
    t2 = A.tile([128, 512], F32, "r2")
    Bt_ = {"t2": Buf()}
    PB = []
    for _q in range(2):
        d_ = dict(t1=A.tile([128, 512], BF16, "r1"), t2b=A.tile([128, 512], BF16, "r2b"), t3=A.tile([128, 512], BF16, "r3"), t4=A.tile([128, 512], BF16, "r4"),
                  u1=A.tile([128, 512], BF16, "u1"), u2=A.tile([128, 512], BF16, "u2"), u3=A.tile([128, 512], BF16, "u3"), u4=A.tile([128, 512], BF16, "u4"),
                  bre=A.tile([128, 512], BF16, "bre"), bim=A.tile([128, 512], BF16, "bim"), wreb=A.tile([128, 512], BF16, "wreb"),
                  wimb=A.tile([128, 512], BF16, "wimb"), zt=A.tile([128, 4], F32, "zt"), cre=A.tile([128, 512], BF16, "cre"),
                  cim=A.tile([128, 512], BF16, "cim"), wre=A.tile([128, 512], F32, "wre"), wim=A.tile([128, 512], F32, "wim"))
        for n_ in list(d_):
            d_["B" + n_] = Buf()
        PB.append(d_)
    zre = [A.tile([128, 512], BF16, "zre") for _ in range(4)]; zim = [A.tile([128, 512], BF16, "zim") for _ in range(4)]
    Bz = [Buf() for _ in range(4)]
    yT = A.tile([128, 4, 512], BF16, "yT"); ByT = [Buf() for _ in range(4)]
    sig = [A.tile([128, 512], BF16, "sig") for _ in range(2)]; Bsig = [Buf(), Buf()]
    cnt = dict(x=0, p=0, v=0, s=0)

    def pbank():
        b = 1 + cnt["p"] % 2
        cnt["p"] += 1
        return b

    import os
    OCUT = os.environ.get("ODD_CUT", "Z")
    def do_norm(sb_):
        for tt in range(4):
            t0 = sb_ * 512 + tt * 128
            xb = cnt["x"] % NX
            cnt["x"] += 1
            k.dma("sp", xs[xb][:], Xin[t0:t0 + 128, :], w=[Bxs[xb]])
            norm_transpose(c, xs[xb][:], Bxs[xb], hTs[sb_ % 2], BhTs[sb_ % 2][tt], slice(tt * 128, (tt + 1) * 128), l, 1, scr[tt % 2], ps[0], BP[0])

    do_norm(0)
    for sb in range(S // 512 if OCUT != "P" else 0):
        tc = slice(sb * 512, (sb + 1) * 512)
        ob = sb % 2
        hT = hTs[sb % 2]
        BhT = BhTs[sb % 2]
        for mt in range(8):
            pb = pbank()
            for kt in range(8):
                k.pe(lambda e, hT=hT, pb=pb, kt=kt, mt=mt: e.matmul(ps[pb][:, :], lhsT=W[:, kt, mt * 128:(mt + 1) * 128], rhs=hT[:, kt, :], start=(kt == 0), stop=(kt == 7)),
                     r=BhT + [BW[mt // 4]], w=[BP[pb]], inc=(kt == 7))
            if mt < 4:
                k.act(lambda e, pb=pb, mt=mt: e.copy(out=uT[:, mt, :], in_=ps[pb][:, :]), r=[BP[pb]], w=[BuT[mt]])
            else:
                k.act(lambda e, pb=pb, mt=mt: e.activation(out=suT[:, mt - 4, :], in_=ps[pb][:, :], func=AF.Gelu), r=[BP[pb]], w=[BsuT[mt - 4]])
        if OCUT == "1":
            continue
        for tt in range(4):
            pb = pbank()
            tcs = slice(tt * 128, (tt + 1) * 128)
            for kt in range(8):
                k.pe(lambda e, hT=hT, pb=pb, kt=kt, tcs=tcs: e.matmul(ps[pb][:, :], lhsT=hT[:, kt, tcs], rhs=W[:, kt, 1024:1536], start=(kt == 0), stop=(kt == 7)),
                     r=[BhT[tt], BW[2]], w=[BP[pb]], inc=(kt == 7))
            vi = cnt["v"] % 2
            cnt["v"] += 1
            k.act(lambda e, pb=pb, vi=vi: e.activation(out=vg[vi][:], in_=ps[pb][:, :], func=AF.Gelu), r=[BP[pb]], w=[Bvg[vi]])
            k.dve(lambda e, vi=vi: e.tensor_reduce(out=st4[vi][:, 0:1], in_=vg[vi][:], axis=AX.X, op=ALU.add), r=[Bvg[vi]], w=[Bst4[vi]])
            k.act(lambda e, vi=vi: e.activation(out=vsq[:], in_=vg[vi][:], func=AF.Square, accum_out=st4[vi][:, 1:2]), r=[Bvg[vi]], w=[Bvsq, Bst4[vi]])
            k.dve(lambda e, vi=vi: e.tensor_scalar(out=st4[vi][:, 0:2], in0=st4[vi][:, 0:2], scalar1=1.0 / 512.0, scalar2=None, op0=ALU.mult), r=[Bst4[vi]], w=[Bst4[vi]])
            k.dve(lambda e, vi=vi: e.tensor_tensor(out=st4[vi][:, 2:3], in0=st4[vi][:, 0:1], in1=st4[vi][:, 0:1], op=ALU.mult), r=[Bst4[vi]], w=[Bst4[vi]])
            k.dve(lambda e, vi=vi: e.tensor_tensor(out=st4[vi][:, 2:3], in0=st4[vi][:, 1:2], in1=st4[vi][:, 2:3], op=ALU.subtract), r=[Bst4[vi]], w=[Bst4[vi]])
            k.act(lambda e, vi=vi: e.activation(out=st4[vi][:, 3:4], in_=st4[vi][:, 2:3], func=AF.Sqrt, bias=c.eps_col[:], scale=1.0), r=[Bst4[vi], c.B_const], w=[Bst4[vi]])
            k.dve(lambda e, vi=vi: e.reciprocal(out=st4[vi][:, 3:4], in_=st4[vi][:, 3:4]), r=[Bst4[vi]], w=[Bst4[vi]])
            k.dve(lambda e, vi=vi: e.tensor_scalar(out=vtmp[vi][:], in0=vg[vi][:], scalar1=st4[vi][:, 0:1], scalar2=st4[vi][:, 3:4], op0=ALU.subtract, op1=ALU.mult),
                  r=[Bvg[vi], Bst4[vi]], w=[Bvtmp[vi]])
            k.dve(lambda e, vi=vi: e.tensor_tensor(out=vtmp[vi][:], in0=vtmp[vi][:], in1=lnrow[0][:], op=ALU.mult), r=[Bvtmp[vi], Bs], w=[Bvtmp[vi]])
            k.dve(lambda e, vi=vi, tt=tt: e.tensor_tensor(out=vn[:, tt, :], in0=vtmp[vi][:], in1=lnrow[1][:], op=ALU.add), r=[Bvtmp[vi], Bs], w=[Bvn[tt]])
        if OCUT == "2":
            continue
        for gp in range(4):
            for tt in range(4):
                tcs = slice(tt * 128, (tt + 1) * 128)
                for g2 in range(2):
                    g = 2 * gp + g2
                    k.pe(lambda e, g=g, g2=g2, tt=tt, tcs=tcs: e.matmul(ps[6][64 * g2:64 * g2 + 64, tcs], lhsT=vn[:, tt, g * 64:(g + 1) * 64], rhs=WmT[:, g, :],
                                                                        start=True, stop=False), r=[Bvn[tt], Bs], w=[BP[6]], inc=False)
                k.pe(lambda e, gp=gp, tcs=tcs: e.matmul(ps[6][:, tcs], lhsT=bsel[0:8, gp, :], rhs=bs8[0:8, :], start=False, stop=True), r=[Bs], w=[BP[6]],
                     inc=(tt == 3))
            k.dve(lambda e, gp=gp, ob=ob: e.tensor_tensor(out=mixo[ob][:, 4 + gp, :], in0=ps[6][:, :], in1=suT[:, gp, :], op=ALU.mult),
                  r=[BP[6], BsuT[gp]], w=[Bmixo[ob]])
        if OCUT == "3":
            continue
        if sb + 1 < S // 512:
            do_norm(sb + 1)
        def stageR(m):
            ut, mm = m // 4, m % 4
            d = PB[m % 2]
            ec, es = Ecb[:, m, :], Esb[:, m, :]
            pa, pb2 = (3, 4)
            rb = rcol[:, m:m + 1].broadcast_to([128, 512])

            def g0():
                k.pe(lambda e: e.matmul(ps[pa][:, :], lhsT=Bbd[0][:, ut, mm * 128:(mm + 1) * 128], rhs=uT[:, ut, :], start=True, stop=True),
                     r=[Bs, BuT[ut]], w=[BP[pa]])
                k.pe(lambda e: e.matmul(ps[pb2][:, :], lhsT=Bbd[1][:, ut, mm * 128:(mm + 1) * 128], rhs=uT[:, ut, :], start=True, stop=True),
                     r=[Bs, BuT[ut]], w=[BP[pb2]])
                k.act(lambda e: e.copy(out=d["bre"][:], in_=ps[pa][:, :]), r=[BP[pa]], w=[d["Bbre"]])
                k.act(lambda e: e.copy(out=d["bim"][:], in_=ps[pb2][:, :]), r=[BP[pb2]], w=[d["Bbim"]])

            def g1():
                k.dve(lambda e: e.tensor_tensor(out=d["t1"][:], in0=d["bre"][:], in1=ec, op=ALU.mult), r=[d["Bbre"], Bs], w=[d["Bt1"]])
                k.dve(lambda e: e.tensor_tensor(out=d["t2b"][:], in0=d["bim"][:], in1=es, op=ALU.mult), r=[d["Bbim"], Bs], w=[d["Bt2b"]])
                k.dve(lambda e: e.tensor_tensor(out=d["t3"][:], in0=d["bim"][:], in1=ec, op=ALU.mult), r=[d["Bbim"], Bs], w=[d["Bt3"]])
                k.dve(lambda e: e.tensor_tensor(out=d["t4"][:], in0=d["bre"][:], in1=es, op=ALU.mult), r=[d["Bbre"], Bs], w=[d["Bt4"]])

            def g2():
                k.dve(lambda e: e.tensor_tensor(out=d["cre"][:], in0=d["t1"][:], in1=d["t2b"][:], op=ALU.add), r=[d["Bt1"], d["Bt2b"]], w=[d["Bcre"]])
                k.dve(lambda e: e.tensor_tensor(out=d["cim"][:], in0=d["t3"][:], in1=d["t4"][:], op=ALU.subtract), r=[d["Bt3"], d["Bt4"]], w=[d["Bcim"]])

            def g3():
                k.dve(lambda e: e.tensor_tensor_scan(out=d["wre"][:], data0=rb, data1=d["cre"][:], initial=z0[:, m, 0:1], op0=ALU.mult, op1=ALU.add),
                      r=[d["Bcre"], Bz0, Bs], w=[d["Bwre"]])
                k.dve(lambda e: e.tensor_tensor_scan(out=d["wim"][:], data0=rb, data1=d["cim"][:], initial=z0[:, m, 1:2], op0=ALU.mult, op1=ALU.add),
                      r=[d["Bcim"], Bz0, Bs], w=[d["Bwim"]])
                k.act(lambda e: e.copy(out=d["wreb"][:], in_=d["wre"][:]), r=[d["Bwre"]], w=[d["Bwreb"]])
                k.act(lambda e: e.copy(out=d["wimb"][:], in_=d["wim"][:]), r=[d["Bwim"]], w=[d["Bwimb"]])
            return [g0, g1, g2, g3]

        def stageU(m):
            ut, mm = m // 4, m % 4
            d = PB[m % 2]
            ec, es = Ecb[:, m, :], Esb[:, m, :]
            zt = d["zt"]
            zi = m % 4

            def h1():
                k.dve(lambda e: e.tensor_tensor(out=zt[:, 0:1], in0=d["wre"][:, 511:512], in1=EcL[:, m:m + 1], op=ALU.mult), r=[d["Bwre"], Bs], w=[d["Bzt"]])
                k.dve(lambda e: e.tensor_tensor(out=zt[:, 1:2], in0=d["wim"][:, 511:512], in1=EsL[:, m:m + 1], op=ALU.mult), r=[d["Bwim"], Bs], w=[d["Bzt"]])
                k.dve(lambda e: e.tensor_tensor(out=zt[:, 2:3], in0=d["wre"][:, 511:512], in1=EsL[:, m:m + 1], op=ALU.mult), r=[d["Bwre"], Bs], w=[d["Bzt"]])
                k.dve(lambda e: e.tensor_tensor(out=zt[:, 3:4], in0=d["wim"][:, 511:512], in1=EcL[:, m:m + 1], op=ALU.mult), r=[d["Bwim"], Bs], w=[d["Bzt"]])
                k.dve(lambda e: e.tensor_tensor(out=d["u1"][:], in0=d["wreb"][:], in1=ec, op=ALU.mult), r=[d["Bwreb"], Bs], w=[d["Bu1"]])
                k.dve(lambda e: e.tensor_tensor(out=d["u2"][:], in0=d["wimb"][:], in1=es, op=ALU.mult), r=[d["Bwimb"], Bs], w=[d["Bu2"]])
                k.dve(lambda e: e.tensor_tensor(out=d["u3"][:], in0=d["wreb"][:], in1=es, op=ALU.mult), r=[d["Bwreb"], Bs], w=[d["Bu3"]])
                k.dve(lambda e: e.tensor_tensor(out=d["u4"][:], in0=d["wimb"][:], in1=ec, op=ALU.mult), r=[d["Bwimb"], Bs], w=[d["Bu4"]])

            def h2():
                k.dve(lambda e: e.tensor_tensor(out=z0[:, m, 0:1], in0=zt[:, 0:1], in1=zt[:, 1:2], op=ALU.subtract), r=[d["Bzt"]], w=[Bz0])
                k.dve(lambda e: e.tensor_tensor(out=z0[:, m, 1:2], in0=zt[:, 2:3], in1=zt[:, 3:4], op=ALU.add), r=[d["Bzt"]], w=[Bz0])
                k.dve(lambda e: e.tensor_tensor(out=zre[zi][:], in0=d["u1"][:], in1=d["u2"][:], op=ALU.subtract), r=[d["Bu1"], d["Bu2"]], w=[Bz[zi]])
                k.dve(lambda e: e.tensor_tensor(out=zim[zi][:], in0=d["u3"][:], in1=d["u4"][:], op=ALU.add), r=[d["Bu3"], d["Bu4"]], w=[Bz[zi]])
                if mm == 3:
                    o = ut
                    for m2 in range(4):
                        mg = 4 * o + m2
                        k.pe(lambda e, mg=mg, m2=m2: e.matmul(ps[5][:, :], lhsT=Cbd[0][:, mg, :], rhs=zre[m2][:], start=(m2 == 0), stop=False), r=[Bs, Bz[m2]], w=[BP[5]], inc=False)
                        k.pe(lambda e, mg=mg, m2=m2: e.matmul(ps[5][:, :], lhsT=Cbd[1][:, mg, :], rhs=zim[m2][:], start=False, stop=False), r=[Bs, Bz[m2]], w=[BP[5]], inc=False)
                    k.pe(lambda e: e.matmul(ps[5][:, :], lhsT=Dd[:, o, :], rhs=uT[:, o, :], start=False, stop=True), r=[Bs, BuT[o]], w=[BP[5]])
                    k.act(lambda e: e.activation(out=yT[:, o, :], in_=ps[5][:, :], func=AF.Gelu), r=[BP[5]], w=[ByT[o]])
            return [h1, h2]

        Rs = [stageR(m) for m in range(16)]
        Rs[0][0](); Rs[1][0]()
        Rs[0][1](); Rs[0][2](); Rs[0][3]()
        for m in range(16):
            U = stageU(m)
            if m + 2 < 16:
                Rs[m + 2][0]()
            if m + 1 < 16:
                R = Rs[m + 1]
                R[1](); U[0](); R[2](); U[1](); R[3]()
            else:
                U[0](); U[1]()
        for f in range(4 if OCUT != "4" else 0):
            for o in range(4):
                k.pe(lambda e, f=f, o=o: e.matmul(ps[7][:, :], lhsT=Wglu[:, o, f * 128:(f + 1) * 128], rhs=yT[:, o, :], start=(o == 0), stop=(o == 3)),
                     r=[Bs, ByT[o]], w=[BP[7]], inc=(o == 3))
            si = f % 2
            k.dve(lambda e, f=f: e.tensor_scalar(out=t2[:], in0=ps[7][:, :], scalar1=bglu[:, f:f + 1], scalar2=None, op0=ALU.add), r=[BP[7], Bs], w=[Bt_["t2"]])
            k.act(lambda e, f=f, si=si: e.activation(out=sig[si][:], in_=t2[:], func=AF.Sigmoid), r=[Bt_["t2"]], w=[Bsig[si]])
            k.dve(lambda e, f=f, si=si, ob=ob: e.tensor_tensor(out=mixo[ob][:, f, :], in0=yT[:, f, :], in1=sig[si][:], op=ALU.mult), r=[ByT[f], Bsig[si]], w=[Bmixo[ob]])
        k.dma(c.stq, MixT.rearrange("(a p) t -> p a t", p=128)[:, :, tc], mixo[ob][:], r=[Bmixo[ob]])
    k.barrier()
    A.reset(m0)


PARAM_SHAPES = dict(
    ada_w=[2, 1024, 6144], ada_b=[2, 6144], even_w_in=[1, 1024, 3608], even_w_out=[1, 1024, 1024], gla_w_lr=[1, 16, 512], gla_b_lr=[1, 512],
    gla_gain=[1, 8, 64], fox_b_f=[1, 8], fox_q_gain=[1, 8, 64], fox_k_gain=[1, 8, 64], odd_w_in=[1, 1024, 1536], odd_w_out=[1, 1024, 1024],
    s5_lam_re=[1, 32, 64], s5_lam_im=[1, 32, 64], s5_log_dt=[1, 32], s5_b_re=[1, 32, 64, 16], s5_b_im=[1, 32, 64, 16], s5_c_re=[1, 32, 16, 64],
    s5_c_im=[1, 32, 16, 64], s5_d=[1, 32, 16], s5_w_glu=[1, 512, 512], s5_b_glu=[1, 512], sgu_ln_gain=[1, 512], sgu_ln_bias=[1, 512],
    sgu_w_s=[1, 8, 128, 128], sgu_b_s=[1, 8, 128], mlp_w1=[2, 1024, 4096], mlp_w2=[2, 4096, 1024])


def build_program(S):
    nc = bass.Bass("TRN2", target_bir_lowering=False)
    x = nc.dram_tensor("x", [S, 1024], F32, kind="ExternalInput").ap()
    cin = nc.dram_tensor("c", [1024], F32, kind="ExternalInput").ap()
    p = {n: nc.dram_tensor(n, sh, F32, kind="ExternalInput").ap() for n, sh in PARAM_SHAPES.items()}
    out = nc.dram_tensor("out", [S, 1024], F32, kind="ExternalOutput").ap()
    sc = lambda n, sh, dt: nc.dram_tensor(n, sh, dt).ap()
    T = dict(QgT=sc("QgT", [512, S], BF16), KgT=sc("KgT", [512, S], BF16), GgT=sc("GgT", [512, S], BF16), Kg=sc("Kg", [S, 512], BF16),
             Vg=sc("Vg", [S, 512], BF16), La=sc("La", [S, 512], F32), QfT=sc("QfT", [8, 70, S], BF16), KfT=sc("KfT", [8, 70, S], BF16),
             Vf=sc("Vf", [S, 8, 65], BF16))
    MixT = sc("MixT", [1024, S], BF16)
    X1 = sc("X1", [S, 1024], F32)
    X2 = sc("X2", [S, 1024], F32)
    with ExitStack() as st:
        c = make_ctx(nc, st, S)
        phase_adaln(c, 0, cin, p["ada_w"], p["ada_b"])
        phase_n1_even(c, 0, x, p["even_w_in"][0], p["gla_w_lr"][0], p["gla_b_lr"][0], p["fox_b_f"][0], p["fox_q_gain"][0], p["fox_k_gain"][0], T)
        phase_gla(c, T, p["gla_gain"][0], MixT)
        phase_fox(c, T, MixT)
        phase_outproj(c, 0, MixT, x, X1, p["even_w_out"][0])
        phase_mlp(c, 0, X1, X2, p["mlp_w1"][0], p["mlp_w2"][0])
        phase_adaln(c, 1, cin, p["ada_w"], p["ada_b"])
        P = dict(lam_re=p["s5_lam_re"][0], lam_im=p["s5_lam_im"][0], log_dt=p["s5_log_dt"][0], b_re=p["s5_b_re"][0], b_im=p["s5_b_im"][0],
                 c_re=p["s5_c_re"][0], c_im=p["s5_c_im"][0], d=p["s5_d"][0], w_glu=p["s5_w_glu"][0], b_glu=p["s5_b_glu"][0],
                 ln_gain=p["sgu_ln_gain"][0], ln_bias=p["sgu_ln_bias"][0], w_s=p["sgu_w_s"][0], b_s=p["sgu_b_s"][0])
        phase_odd(c, 1, X2, p["odd_w_in"][0], P, MixT)
        phase_outproj(c, 1, MixT, X2, X1, p["odd_w_out"][0])
        phase_mlp(c, 1, X1, out, p["mlp_w1"][1], p["mlp_w2"][1])
        finish(c)
    return nc


_NC_CACHE = {}


def kernel(**inputs):
    x = np.asarray(inputs["x"], dtype=np.float32)
    cc = np.asarray(inputs["c"], dtype=np.float32)
    B, S, _ = x.shape
    if S not in _NC_CACHE:
        _NC_CACHE[S] = build_program(S)
    nc = _NC_CACHE[S]
    params = {n: np.ascontiguousarray(np.asarray(inputs[n], dtype=np.float32)) for n in PARAM_SHAPES}
    in_maps = []
    for b in range(B):
        m = dict(params)
        m["x"] = np.ascontiguousarray(x[b])
        m["c"] = np.ascontiguousarray(cc[b])
        in_maps.append(m)
    res = run_bass_kernel_spmd(nc, in_maps, core_ids=list(range(B)))
    return np.stack([np.asarray(r["out"], dtype=np.float32) for r in res.results], axis=0)
```

```python
import numpy as np
from contextlib import ExitStack
import concourse.bass as bass
import concourse.mybir as mybir
from concourse.bass_utils import run_bass_kernel_spmd

F32 = mybir.dt.float32
BF16 = mybir.dt.bfloat16
I32 = mybir.dt.int32
AF = mybir.ActivationFunctionType
ALU = mybir.AluOpType
AX = mybir.AxisListType

D = 1024
DFF = 4096
EPS = 1e-6
EVEN_W = 3608
SB_BASE = 16640
SB_LIMIT = 229344 - 64


class Buf:
    __slots__ = ("name", "w", "r")

    def __init__(self, name=""):
        self.name = name
        self.w = None
        self.r = []


class KB:
    EPOCH = 20000
    NDMA = 48
    NDMA_HW = 32
    SAME_GAP = 2

    def __init__(self, nc, stack):
        self.nc = nc
        self.stack = stack
        self.names = ["pe", "act", "dve", "pool", "sp"]
        self.stream = {e: [] for e in self.names}
        self.cnt = {e: 0 for e in self.names}
        self.pend = {e: False for e in self.names}
        self.seen = {e: {} for e in self.names}
        self.semh = {}
        self.dma_cnt = [0] * self.NDMA
        self.dma_rr = 0
        self.dma_rr_sw = 0
        self.nsem = 0
        self.ninstr = 0

    def _sem(self, key):
        h = self.semh.get(key)
        if h is None:
            self.nsem += 1
            h = self.stack.enter_context(self.nc.semaphore("s%d" % self.nsem))
            self.semh[key] = h
        return h

    def _next_token(self, eng):
        g = self.cnt[eng] + 1
        ep, v = divmod(g - 1, self.EPOCH)
        return (("e", eng, ep), v + 1)

    def _need(self, eng, key, val):
        if self.seen[eng].get(key, 0) >= val:
            return False
        if key[0] == "e":
            for k2 in self.seen[eng]:
                if k2[0] == "e" and k2[1] == key[1] and k2[2] > key[2]:
                    return False
        self.seen[eng][key] = val
        return True

    def _waits(self, eng, reads, writes, skip_same=False):
        relaxed = eng in ("act", "dve")
        toks = set()
        for b in reads:
            if b.w is not None:
                toks.add(b.w)
        for b in writes:
            if b.w is not None and not (relaxed and b.w[0][0] == "e" and b.w[0][1] == eng):
                toks.add(b.w)
            for t in b.r:
                if not (relaxed and t[0][0] == "e" and t[0][1] == eng):
                    toks.add(t)
        out = []
        for key, val in sorted(toks, key=lambda t: (str(t[0]), t[1])):
            if key[0] == "e" and key[1] == eng:
                if skip_same:
                    continue
                if relaxed and self.cnt[eng] - (key[2] * self.EPOCH + val) >= self.SAME_GAP:
                    continue
            if self._need(eng, key, val):
                out.append((key, val))
        return out

    def _mark(self, tok, reads, writes):
        for b in writes:
            b.w = tok
            b.r = []
        for b in reads:
            b.r = [t for t in b.r if t[0] != tok[0]]
            b.r.append(tok)

    def op(self, eng, fn, r=(), w=(), inc=True):
        waits = self._waits(eng, r, w, skip_same=(eng == "pe"))
        tok = self._next_token(eng)
        self.stream[eng].append((waits, fn, tok if inc else None))
        if inc:
            self.cnt[eng] += 1
            self.pend[eng] = False
        else:
            self.pend[eng] = True
        self._mark(tok, r, w)
        self.ninstr += 1
        return tok

    def pe(self, fn, r=(), w=(), inc=True):
        return self.op("pe", fn, r, w, inc)

    def act(self, fn, r=(), w=(), inc=True):
        return self.op("act", fn, r, w, inc)

    def dve(self, fn, r=(), w=(), inc=True):
        return self.op("dve", fn, r, w, inc)

    def pool(self, fn, r=(), w=(), inc=True):
        return self.op("pool", fn, r, w, inc)

    def dma(self, q, out, in_, r=(), w=(), **kw):
        if q == "pool":
            i = self.NDMA_HW + self.dma_rr_sw
            self.dma_rr_sw = (self.dma_rr_sw + 1) % (self.NDMA - self.NDMA_HW)
        else:
            i = self.dma_rr
            self.dma_rr = (self.dma_rr + 1) % self.NDMA_HW
        key = ("d", i)
        waits = self._waits(q, r, w)
        prev = self.dma_cnt[i]
        if prev > 0 and self._need(q, key, prev * 16):
            waits.append((key, prev * 16))
        self.dma_cnt[i] += 1
        tok = (key, self.dma_cnt[i] * 16)

        def fn(e, out=out, in_=in_, kw=kw):
            return e.dma_start(out=out, in_=in_, **kw)
        self.stream[q].append((waits, fn, ("dma", tok)))
        self._mark(tok, r, w)
        self.ninstr += 1
        return tok

    def all_tokens(self):
        toks = []
        for e in self.names:
            if self.cnt[e] > 0:
                ep, v = divmod(self.cnt[e] - 1, self.EPOCH)
                toks.append((("e", e, ep), v + 1))
        for i in range(self.NDMA):
            if self.dma_cnt[i]:
                toks.append((("d", i), self.dma_cnt[i] * 16))
        return toks

    def barrier(self, engs=None):
        for e in self.names:
            assert not self.pend[e], e
        toks = self.all_tokens()
        for e in (engs or self.names):
            waits = [(key, val) for key, val in toks
                     if not (key[0] == "e" and key[1] == e) and self._need(e, key, val)]
            if waits:
                self.stream[e].append((waits, None, None))

    def check(self):
        sem = {}
        ptr = {e: 0 for e in self.names}
        progress = True
        while progress:
            progress = False
            for e in self.names:
                items = self.stream[e]
                while ptr[e] < len(items):
                    waits, fn, tok = items[ptr[e]]
                    if any(sem.get(key, 0) < val for key, val in waits):
                        break
                    if tok is not None:
                        if tok[0] == "dma":
                            sem[tok[1][0]] = sem.get(tok[1][0], 0) + 16
                        else:
                            sem[tok[0]] = sem.get(tok[0], 0) + 1
                    ptr[e] += 1
                    progress = True
        for e in self.names:
            if ptr[e] < len(self.stream[e]):
                waits, fn, tok = self.stream[e][ptr[e]]
                bad = [(key, val, sem.get(key, 0)) for key, val in waits if sem.get(key, 0) < val]
                raise RuntimeError("DEADLOCK: engine %s stuck at item %d/%d waiting %s" % (e, ptr[e], len(self.stream[e]), bad))

    def emit(self):
        for e in self.names:
            assert not self.pend[e], "engine %s has trailing non-inc instructions" % e
        self.check()
        nc = self.nc
        handles = {"pe": "tensor", "act": "scalar", "dve": "vector", "pool": "gpsimd", "sp": "sync"}
        for e in self.names:
            for waits, fn, tok in self.stream[e]:
                for key, val in waits:
                    self._sem(key)
                if tok is not None:
                    self._sem(tok[1][0] if tok[0] == "dma" else tok[0])
        with nc.Block() as block:
            for e in self.names:
                items = self.stream[e]
                if not items:
                    continue

                def body(eng, items=items):
                    for waits, fn, tok in items:
                        for key, val in waits:
                            eng.wait_ge(self._sem(key), val)
                        if fn is None:
                            continue
                        ins = fn(eng)
                        if tok is not None:
                            if tok[0] == "dma":
                                ins.then_inc(self._sem(tok[1][0]), 16)
                            else:
                                ins.then_inc(self._sem(tok[0]), 1)
                getattr(block, handles[e])(body)


class Alloc:
    def __init__(self, nc):
        self.nc = nc
        self.off = SB_BASE
        self.n = 0

    def tile(self, shape, dt, name="t"):
        esz = 4 if dt in (F32, I32) else 2
        nbytes = int(np.prod(shape[1:])) * esz
        nbytes = (nbytes + 63) // 64 * 64
        assert self.off + nbytes <= SB_LIMIT, "SBUF overflow at %s: %d + %d" % (name, self.off, nbytes)
        self.n += 1
        h = self.nc.alloc_sbuf_tensor_at("%s_%d" % (name, self.n), list(shape), dt, offset=self.off)
        self.off += nbytes
        return h

    def mark(self):
        return self.off

    def reset(self, m):
        self.off = m


class Ctx:
    pass


def setup_consts(c):
    k, nc, A = c.k, c.nc, c.A
    c.ident_bf = A.tile([128, 128], BF16, "identbf")
    c.ident_f = A.tile([128, 128], F32, "identf")
    c.B_const = Buf("const")
    onesf = A.tile([128, 128], F32, "onesf")
    c.ones_f = onesf
    k.pool(lambda e: e.memset(onesf[:], 1.0), w=[c.B_const])
    k.pool(lambda e: e.affine_select(out=c.ident_f[:], in_=onesf[:], pattern=[[-1, 128]], compare_op=ALU.is_equal,
                                     fill=0.0, base=0, channel_multiplier=1), r=[c.B_const], w=[c.B_const])
    k.pool(lambda e: e.tensor_copy(out=c.ident_bf[:], in_=c.ident_f[:]), r=[c.B_const], w=[c.B_const])
    c.ones_bf = A.tile([128, 128], BF16, "onesbf")
    k.pool(lambda e: e.memset(c.ones_bf[:], 1.0), w=[c.B_const])
    c.eps_col = A.tile([128, 1], F32, "epscol")
    k.pool(lambda e: e.memset(c.eps_col[:], EPS), w=[c.B_const])


def masked(c, out_ap, in_ap, pattern, cmp, base, cm, fill=0.0, r=(), w=()):
    c.k.pool(lambda e: e.affine_select(out=out_ap, in_=in_ap, pattern=pattern, compare_op=cmp, fill=fill,
                                       base=base, channel_multiplier=cm), r=list(r), w=list(w))


def dram_cols_FM(ap2d):
    return ap2d.rearrange("(kt p) e -> p kt e", p=128)


def alloc_mod(c):
    c.modFM1 = c.A.tile([128, 4, 8], F32, "modFM")
    c.gbc1 = [c.A.tile([128, 1024], F32, "gbc") for _ in range(2)]
    c.modFM = [c.modFM1, c.modFM1]
    c.gbc = [c.gbc1, c.gbc1]
    c.B_mod = Buf("mod")


def phase_adaln(c, l, c_in, ada_w, ada_b):
    k, nc, A, ps = c.k, c.nc, c.A, c.ps
    m = A.mark()
    ccol = A.tile([128, 8], F32, "ccol")
    cact = A.tile([128, 8], F32, "cact")
    crep = A.tile([128, 8, 128], F32, "crep")
    wch = [A.tile([128, 8, 1024], F32, "wch") for _ in range(2)]
    bFM = A.tile([128, 8], F32, "bFM")
    brow = A.tile([128, 1024], F32, "brow")
    Bc, Bw, Bb, Bp = Buf(), [Buf(), Buf()], Buf(), [Buf(), Buf()]
    k.dma("sp", ccol[:], c_in.rearrange("(kt p) -> p kt", p=128), w=[Bc], allow_slow_non_contiguous=True)
    k.act(lambda e: e.activation(out=cact[:], in_=ccol[:], func=AF.Silu), r=[Bc], w=[Bc])
    k.dve(lambda e: e.tensor_copy(out=crep[:], in_=cact[:].unsqueeze(2).broadcast_to([128, 8, 128])), r=[Bc], w=[Bc])
    it = 0
    if True:
        for q in range(6):
            wb = it % 2
            it += 1
            cols = slice(q * 1024, (q + 1) * 1024)
            k.dma("sp", wch[wb][:], dram_cols_FM(ada_w[l][:, cols]), w=[Bw[wb]])
            if q in (2, 5):
                gi = 0 if q == 2 else 1
                k.dma("sp", brow[:], ada_b[l][cols].partition_broadcast(128), w=[Bb])
                for half in range(2):
                    pt = ps[half]
                    cs = slice(half * 512, (half + 1) * 512)
                    for kt in range(8):
                        k.pe(lambda e, pt=pt, kt=kt, wb=wb, cs=cs: e.matmul(pt[:, :], lhsT=crep[:, kt, :], rhs=wch[wb][:, kt, cs],
                                                                           start=(kt == 0), stop=(kt == 7)),
                             r=[Bc, Bw[wb]], w=[Bp[half]], inc=(kt == 7))
                    k.dve(lambda e, pt=pt, cs=cs, l=l, gi=gi: e.tensor_tensor(out=c.gbc[l][gi][:, cs], in0=pt[:, :], in1=brow[:, cs], op=ALU.add),
                          r=[Bp[half], Bb], w=[c.B_mod])
            else:
                mi = {0: 0, 1: 1, 3: 2, 4: 3}[q]
                k.dma("sp", bFM[:], ada_b[l][cols].rearrange("(j p) -> p j", p=128), w=[Bb], allow_slow_non_contiguous=True)
                pt = ps[2 + (it % 2)]
                Bq = Bp[0] if it % 2 == 0 else Bp[1]
                Bq = Buf()
                for j in range(8):
                    for kt in range(8):
                        last = (j == 7 and kt == 7)
                        k.pe(lambda e, pt=pt, kt=kt, wb=wb, j=j: e.matmul(pt[:, j:j + 1], lhsT=wch[wb][:, kt, j * 128:(j + 1) * 128],
                                                                          rhs=cact[:, kt:kt + 1], start=(kt == 0), stop=(kt == 7)),
                             r=[Bc, Bw[wb]], w=[c.B_ps[2 + (it % 2)]], inc=last)
                k.dve(lambda e, pt=pt, l=l, mi=mi: e.tensor_tensor(out=c.modFM[l][:, mi, :], in0=pt[:, 0:8], in1=bFM[:], op=ALU.add),
                      r=[c.B_ps[2 + (it % 2)], Bb], w=[c.B_mod])
                if mi in (1, 3):
                    k.dve(lambda e, l=l, mi=mi: e.tensor_scalar(out=c.modFM[l][:, mi, :], in0=c.modFM[l][:, mi, :], scalar1=1.0,
                                                               scalar2=None, op0=ALU.add), r=[c.B_mod], w=[c.B_mod])
    k.barrier()
    A.reset(m)


def norm_transpose(c, xt, Bx, hT, BhT, tcols, l, which, scr, pbank, Bpbank, part=0):
    k = c.k
    mi_sh, mi_sc = (0, 1) if which == 1 else (2, 3)
    if part in (0, 1):
      k.act(lambda e: e.activation(out=scr["hn"][:], in_=xt, func=AF.Square, accum_out=scr["ss"][:]), r=[Bx], w=[scr["Bhn"], scr["Bss"]])
      k.act(lambda e: e.activation(out=scr["ss"][:], in_=scr["ss"][:], func=AF.Sqrt, bias=c.eps_col[:], scale=1.0 / D),
            r=[scr["Bss"], c.B_const], w=[scr["Bss"]])
      k.dve(lambda e: e.reciprocal(out=scr["rstd"][:], in_=scr["ss"][:]), r=[scr["Bss"]], w=[scr["Brstd"]])
      k.dve(lambda e: e.tensor_scalar(out=scr["hn"][:], in0=xt, scalar1=scr["rstd"][:, 0:1], scalar2=None, op0=ALU.mult),
            r=[Bx, scr["Brstd"]], w=[scr["Bhn"]])
    if part == 1:
        return
    pv = pbank[:].bitcast(BF16)
    for j in range(8):
        k.pe(lambda e, j=j: e.transpose(pv[:, j * 128:(j + 1) * 128], scr["hn"][:, j * 128:(j + 1) * 128], c.ident_bf[:]),
             r=[scr["Bhn"], c.B_const], w=[Bpbank], inc=(j == 7))
    for j in range(8):
        eng = k.act if j % 2 == 0 else k.dve
        if j % 2 == 0:
            k.act(lambda e, j=j: e.activation(out=hT[:, j, tcols], in_=pv[:, j * 128:(j + 1) * 128], func=AF.Identity,
                                              bias=c.modFM[l][:, mi_sh, j:j + 1], scale=c.modFM[l][:, mi_sc, j:j + 1]),
                  r=[Bpbank, c.B_mod], w=[BhT])
        else:
            k.dve(lambda e, j=j: e.tensor_scalar(out=hT[:, j, tcols], in0=pv[:, j * 128:(j + 1) * 128],
                                                 scalar1=c.modFM[l][:, mi_sc, j:j + 1], scalar2=c.modFM[l][:, mi_sh, j:j + 1],
                                                 op0=ALU.mult, op1=ALU.add),
                  r=[Bpbank, c.B_mod], w=[BhT])


def norm_scratch(c, n=2):
    A = c.A
    out = []
    for i in range(n):
        out.append(dict(ss=A.tile([128, 1], F32, "ss"), rstd=A.tile([128, 1], F32, "rstd"),
                        hn=A.tile([128, 1024], BF16, "hn"), Bjunk=Buf(), Bss=Buf(), Brstd=Buf(), Bhn=Buf()))
    return out


def phase_outproj(c, l, MixT, Xin, Xout, w_out):
    k, nc, A, ps, S = c.k, c.nc, c.A, c.ps, c.S
    m = A.mark()
    Wo = A.tile([128, 8, 1024], BF16, "Wo")
    BWo = Buf()
    k.dma("pool", Wo[:], dram_cols_FM(w_out), w=[BWo])
    NB = 2
    mix = [A.tile([128, 8, 512], BF16, "mix") for _ in range(NB)]
    Bmix = [Buf() for _ in range(NB)]
    NX = 4
    xs = [A.tile([128, 1024], F32, "xo") for _ in range(NX)]
    Bxs = [Buf() for _ in range(NX)]
    tmp = [A.tile([128, 512], F32, "tmpo") for _ in range(2)]
    Btmp = [Buf(), Buf()]
    MixT_v = MixT.rearrange("(et p) t -> p et t", p=128)
    ti = 0
    pi = 0
    for sb in range(S // 512):
        b = sb % NB
        k.dma("sp", mix[b][:], MixT_v[:, :, sb * 512:(sb + 1) * 512], w=[Bmix[b]])
        for tt in range(4):
            t0 = sb * 512 + tt * 128
            xb = ti % NX
            ti += 1
            k.dma("sp", xs[xb][:], Xin[t0:t0 + 128, :], w=[Bxs[xb]])
            for half in range(2):
                pb = pi % 4
                pi += 1
                cs = slice(half * 512, (half + 1) * 512)
                for et in range(8):
                    k.pe(lambda e, pb=pb, b=b, et=et, tt=tt, cs=cs: e.matmul(ps[pb][:, :], lhsT=mix[b][:, et, tt * 128:(tt + 1) * 128],
                                                                             rhs=Wo[:, et, cs], start=(et == 0), stop=(et == 7)),
                         r=[Bmix[b], BWo], w=[c.B_ps[pb]], inc=(et == 7))
                th = pi % 2
                k.dve(lambda e, pb=pb, cs=cs, th=th: e.tensor_tensor(out=tmp[th][:], in0=ps[pb][:, :], in1=c.gbc[l][0][:, cs], op=ALU.mult),
                      r=[c.B_ps[pb], c.B_mod], w=[Btmp[th]])
                getattr(k, c.add_eng)(lambda e, xb=xb, cs=cs, th=th: e.tensor_tensor(out=xs[xb][:, cs], in0=xs[xb][:, cs], in1=tmp[th][:], op=ALU.add),
                       r=[Btmp[th], Bxs[xb]], w=[Bxs[xb]])
            k.dma(c.stq, Xout[t0:t0 + 128, :], xs[xb][:], r=[Bxs[xb]], w=[c.B_X[l]])
    k.barrier()
    A.reset(m)


def phase_mlp(c, l, Xin, Xout, w1, w2):
    k, nc, A, ps, S = c.k, c.nc, c.A, c.ps, c.S
    m = A.mark()
    W1 = A.tile([128, 8, DFF], BF16, "W1")
    W2 = A.tile([128, 32, D], BF16, "W2")
    BW1 = [Buf() for _ in range(4)]
    BW2 = [Buf() for _ in range(4)]
    w1v = dram_cols_FM(w1)
    w2v = dram_cols_FM(w2)
    for q in range(4):
        k.dma("pool", W1[:, :, q * 1024:(q + 1) * 1024], w1v[:, :, q * 1024:(q + 1) * 1024], w=[BW1[q]])
    for q in range(4):
        k.dma("pool", W2[:, q * 8:(q + 1) * 8, :], w2v[:, q * 8:(q + 1) * 8, :], w=[BW2[q]])
    NX = 3
    xs = [A.tile([128, 1024], F32, "xm") for _ in range(NX)]
    Bxs = [Buf() for _ in range(NX)]
    scr = norm_scratch(c, 2)
    hTs = [A.tile([128, 8, 512], BF16, "h2T") for _ in range(2)]
    BhTs = [[Buf() for _ in range(4)] for _ in range(2)]
    hid = A.tile([128, 32, 512], BF16, "hid")
    Bhid = [Buf() for _ in range(32)]
    tmp = [A.tile([128, 512], F32, "tmpm") for _ in range(2)]
    Btmp = [Buf(), Buf()]
    ti = 0
    hp = 0
    yp = 0
    tic = [0]

    def do_norm(sb_):
        for tt in range(4):
            t0 = sb_ * 512 + tt * 128
            xb = tic[0] % NX
            tic[0] += 1
            k.dma("sp", xs[xb][:], Xin[t0:t0 + 128, :], r=[c.B_X[l]], w=[Bxs[xb]])
            norm_transpose(c, xs[xb][:], Bxs[xb], hTs[sb_ % 2], BhTs[sb_ % 2][tt], slice(tt * 128, (tt + 1) * 128), l, 2, scr[tt % 2], ps[0], c.B_ps[0])

    do_norm(0)
    for sb in range(S // 512):
        hT = hTs[sb % 2]
        BhT = BhTs[sb % 2]
        for ft in range(32):
            if ft == 16 and sb + 1 < S // 512:
                do_norm(sb + 1)
            pb = 1 + hp % 3
            hp += 1
            for kt in range(8):
                k.pe(lambda e, hT=hT, pb=pb, kt=kt, ft=ft: e.matmul(ps[pb][:, :], lhsT=W1[:, kt, ft * 128:(ft + 1) * 128], rhs=hT[:, kt, :],
                                                             start=(kt == 0), stop=(kt == 7)),
                     r=BhT + [BW1[ft // 8]], w=[c.B_ps[pb]], inc=(kt == 7))
            if c.relu2_dve:
                k.dve(lambda e, pb=pb, ft=ft: e.scalar_tensor_tensor(out=hid[:, ft, :], in0=ps[pb][:, :], scalar=0.0, in1=ps[pb][:, :],
                                                                     op0=ALU.max, op1=ALU.mult), r=[c.B_ps[pb]], w=[Bhid[ft]])
            else:
                k.act(lambda e, pb=pb, ft=ft: e.activation(out=hid[:, ft, :], in_=ps[pb][:, :], func=AF.Relu), r=[c.B_ps[pb]], w=[Bhid[ft]])
                k.dve(lambda e, ft=ft: e.tensor_tensor(out=hid[:, ft, :], in0=hid[:, ft, :], in1=hid[:, ft, :], op=ALU.mult),
                      r=[Bhid[ft]], w=[Bhid[ft]])
        for tt in range(4):
            t0 = sb * 512 + tt * 128
            xb = tic[0] % NX
            tic[0] += 1
            k.dma("sp", xs[xb][:], Xin[t0:t0 + 128, :], r=[c.B_X[l]], w=[Bxs[xb]])
            for half in range(2):
                pb = 4 + yp % 4
                yp += 1
                cs = slice(half * 512, (half + 1) * 512)
                for ft in range(32):
                    k.pe(lambda e, pb=pb, ft=ft, tt=tt, cs=cs: e.matmul(ps[pb][:, :], lhsT=hid[:, ft, tt * 128:(tt + 1) * 128], rhs=W2[:, ft, cs],
                                                                        start=(ft == 0), stop=(ft == 31)),
                         r=[Bhid[ft], BW2[ft // 8]], w=[c.B_ps[pb]], inc=(ft == 31))
                th = yp % 2
                k.dve(lambda e, pb=pb, cs=cs, th=th: e.tensor_tensor(out=tmp[th][:], in0=ps[pb][:, :], in1=c.gbc[l][1][:, cs], op=ALU.mult),
                      r=[c.B_ps[pb], c.B_mod], w=[Btmp[th]])
                getattr(k, c.add_eng)(lambda e, xb=xb, cs=cs, th=th: e.tensor_tensor(out=xs[xb][:, cs], in0=xs[xb][:, cs], in1=tmp[th][:], op=ALU.add),
                       r=[Btmp[th], Bxs[xb]], w=[Bxs[xb]])
            k.dma(c.stq, Xout[t0:t0 + 128, :], xs[xb][:], r=[Bxs[xb]], w=[c.B_Xout[l]])
    k.barrier()
    A.reset(m)


def make_ctx(nc, st, S):
    c = Ctx()
    c.nc, c.S = nc, S
    c.k = KB(nc, st)
    c.A = Alloc(nc)
    c.ps = [nc.alloc_psum_tensor("psb%d" % i, [128, 512], F32) for i in range(8)]
    c.B_ps = [Buf("ps%d" % i) for i in range(8)]
    c.B_X = [Buf("X1_0"), Buf("X1_1")]
    c.B_Xout = [Buf("X2_0"), Buf("X2_1")]
    c.relu2_dve = False
    c.stq = "pool"
    c.add_eng = "dve"
    setup_consts(c)
    alloc_mod(c)
    return c


def finish(c):
    k = c.k
    k.barrier(engs=["sp"])
    k.emit()


def phase_n1_even(c, l, Xin, w_in, w_lr, b_lr, b_f, q_gain, k_gain, T):
    k, nc, A, ps, S = c.k, c.nc, c.A, c.ps, c.S
    m = A.mark()
    LSn = A.tile([8, S], F32, "LSn")
    BLS = Buf()
    m_post = A.mark()
    W = A.tile([128, 8, EVEN_W], BF16, "Win")
    BW = [Buf() for _ in range(4)]
    wv = dram_cols_FM(w_in)
    bounds = [0, 1024, 2048, 2576, EVEN_W]
    for q in range(4):
        k.dma("pool", W[:, :, bounds[q]:bounds[q + 1]], wv[:, :, bounds[q]:bounds[q + 1]], w=[BW[q]])

    def BWof(c0):
        return [BW[q] for q in range(4) if bounds[q] <= c0 < bounds[q + 1]][0]
    wlr = A.tile([17, 512], BF16, "wlr")
    Bsm = Buf("small")
    k.dma("pool", wlr[0:16, :], w_lr, w=[Bsm])
    k.dma("pool", wlr[16:17, :], b_lr.rearrange("(o e) -> o e", o=1), w=[Bsm])
    nbf = A.tile([8, 1], F32, "nbf")
    k.dma("sp", nbf[:], b_f.rearrange("(h o) -> h o", o=1), w=[Bsm], allow_slow_non_contiguous=True)
    k.dve(lambda e: e.tensor_scalar(out=nbf[:], in0=nbf[:], scalar1=-1.0, scalar2=None, op0=ALU.mult), r=[Bsm], w=[Bsm])
    grow = A.tile([128, 512], F32, "grow")
    g2 = A.tile([128, 512], F32, "grow2")
    k.dma("sp", grow[:], q_gain.rearrange("h d -> (h d)").partition_broadcast(128), w=[Bsm])
    k.dma("sp", g2[:], k_gain.rearrange("h d -> (h d)").partition_broadcast(128), w=[Bsm])
    k.dve(lambda e: e.scalar_tensor_tensor(out=grow[:], in0=grow[:], scalar=0.125, in1=g2[:], op0=ALU.mult, op1=ALU.mult), r=[Bsm], w=[Bsm])
    glrT = A.tile([17, 512], BF16, "glrT")
    Bglr = Buf()
    k.pool(lambda e: e.memset(glrT[:], 1.0), w=[Bglr])
    NX = 4
    xs = [A.tile([128, 1024], F32, "xn") for _ in range(NX)]
    Bxs = [Buf() for _ in range(NX)]
    scr = norm_scratch(c, 4)
    hTs = [A.tile([128, 8, 512], BF16, "h1T") for _ in range(2)]
    BhTs = [[Buf() for _ in range(4)] for _ in range(2)]
    NST = 6
    fst = [A.tile([128, 512], BF16, "fst") for _ in range(NST)]
    Bfst = [Buf() for _ in range(NST)]
    vst = [A.tile([128, 8, 65], BF16, "vst") for _ in range(2)]
    Bvst = [Buf(), Buf()]
    for i in range(2):
        k.pool(lambda e, i=i: e.memset(vst[i][:], 1.0), w=[Bvst[i]])
    sq = [A.tile([128, 512], F32, "sq") for _ in range(2)]
    Bsq = [Buf(), Buf()]
    ssh = [A.tile([128, 8], F32, "ssh") for _ in range(2)]
    Bssh = [Buf(), Buf()]
    qn = [A.tile([128, 512], BF16, "qn") for _ in range(2)]
    Bqn = [Buf(), Buf()]
    qTs = [A.tile([128, 4, 512], BF16, "qTs") for _ in range(2)]
    BqTs = [Buf(), Buf()]
    last = [A.tile([128, 512], F32, "last") for _ in range(2)]
    Blast = [Buf(), Buf()]
    e8 = A.tile([8, 512], F32, "e8")
    Be8 = Buf()
    cnt = dict(x=0, st=0, p=0, v=0, s=0, l=0)

    def pbank():
        b = (1, 2, 3, 6, 7)[cnt["p"] % 5]
        cnt["p"] += 1
        return b

    def stage():
        i = cnt["st"] % NST
        cnt["st"] += 1
        return i

    QgT, KgT, GgT, Kg, Vg, La, QfT, KfT, Vf = (T[n] for n in ("QgT", "KgT", "GgT", "Kg", "Vg", "La", "QfT", "KfT", "Vf"))
    xbuf = {}

    def do_norm1(sb_):
        for tt in range(4):
            t0 = sb_ * 512 + tt * 128
            xb = cnt["x"] % NX
            cnt["x"] += 1
            xbuf[(sb_, tt)] = xb
            k.dma("sp", xs[xb][:], Xin[t0:t0 + 128, :], w=[Bxs[xb]])
            norm_transpose(c, xs[xb][:], Bxs[xb], hTs[sb_ % 2], BhTs[sb_ % 2][tt], slice(tt * 128, (tt + 1) * 128), l, 1, scr[tt], ps[0], c.B_ps[0], part=1)

    def do_norm2(sb_):
        for tt in range(4):
            xb = xbuf[(sb_, tt)]
            norm_transpose(c, xs[xb][:], Bxs[xb], hTs[sb_ % 2], BhTs[sb_ % 2][tt], slice(tt * 128, (tt + 1) * 128), l, 1, scr[tt], ps[0], c.B_ps[0], part=2)

    deferred = []

    def flush():
        while deferred:
            deferred.pop(0)()

    do_norm1(0)
    do_norm2(0)
    for sb in range(S // 512):
        tc = slice(sb * 512, (sb + 1) * 512)
        hT = hTs[sb % 2]
        BhT = BhTs[sb % 2]
        if sb + 1 < S // 512:
            do_norm1(sb + 1)
        for mt in range(12):
            c0 = [0, 128, 256, 384, 512, 640, 768, 896, 1536, 1664, 1792, 1920][mt]
            pb = pbank()
            for kt in range(8):
                k.pe(lambda e, hT=hT, pb=pb, kt=kt, c0=c0: e.matmul(ps[pb][:, :], lhsT=W[:, kt, c0:c0 + 128], rhs=hT[:, kt, :], start=(kt == 0), stop=(kt == 7)),
                     r=BhT + [BWof(c0)], w=[c.B_ps[pb]], inc=(kt == 7))
            si = stage()
            if mt < 4:
                k.act(lambda e, pb=pb, si=si: e.mul(out=fst[si][:], in_=ps[pb][:, :], mul=0.125), r=[c.B_ps[pb]], w=[Bfst[si]])
                dst = QgT[mt * 128:(mt + 1) * 128, tc]
            elif mt < 8:
                k.dve(lambda e, pb=pb, si=si: e.tensor_copy(out=fst[si][:], in_=ps[pb][:, :]), r=[c.B_ps[pb]], w=[Bfst[si]])
                dst = KgT[(mt - 4) * 128:(mt - 3) * 128, tc]
            else:
                k.act(lambda e, pb=pb, si=si: e.activation(out=fst[si][:], in_=ps[pb][:, :], func=AF.Silu), r=[c.B_ps[pb]], w=[Bfst[si]])
                dst = GgT[(mt - 8) * 128:(mt - 7) * 128, tc]
            k.dma(c.stq, dst, fst[si][:], r=[Bfst[si]])
        pb = pbank()
        for kt in range(8):
            k.pe(lambda e, hT=hT, pb=pb, kt=kt: e.matmul(ps[pb][0:16, :], lhsT=W[:, kt, 2048:2064], rhs=hT[:, kt, :], start=(kt == 0), stop=(kt == 7)),
                 r=BhT + [BWof(2048)], w=[c.B_ps[pb]], inc=(kt == 7))
        k.dve(lambda e, pb=pb: e.tensor_copy(out=glrT[0:16, :], in_=ps[pb][0:16, :]), r=[c.B_ps[pb]], w=[Bglr])
        pb = pbank()
        for kt in range(8):
            k.pe(lambda e, hT=hT, pb=pb, kt=kt: e.matmul(ps[pb][0:8, :], lhsT=W[:, kt, 3600:3608], rhs=hT[:, kt, :], start=(kt == 0), stop=(kt == 7)),
                 r=BhT + [BWof(3600)], w=[c.B_ps[pb]], inc=(kt == 7))
        k.act(lambda e, pb=pb: e.activation(out=e8[:], in_=ps[pb][0:8, :], func=AF.Exp, bias=nbf[:, 0:1], scale=-1.0), r=[c.B_ps[pb], Bsm], w=[Be8])
        k.act(lambda e, tc=tc: e.activation(out=LSn[:, tc], in_=e8[:], func=AF.Ln, bias=1.0, scale=1.0), r=[Be8], w=[BLS])
        if sb + 1 < S // 512:
            do_norm2(sb + 1)
        for tt in range(4):
            t0 = sb * 512 + tt * 128
            tcs = slice(tt * 128, (tt + 1) * 128)
            pb = pbank()
            k.pe(lambda e, pb=pb, tcs=tcs: e.matmul(ps[pb][:, :], lhsT=glrT[0:17, tcs], rhs=wlr[0:17, :], start=True, stop=True),
                 r=[Bglr, Bsm], w=[c.B_ps[pb]])
            li = cnt["l"] % 2
            cnt["l"] += 1
            k.act(lambda e, pb=pb, li=li: e.activation(out=last[li][:], in_=ps[pb][:, :], func=AF.Exp, scale=-1.0), r=[c.B_ps[pb]], w=[Blast[li]])
            k.act(lambda e, li=li: e.activation(out=last[li][:], in_=last[li][:], func=AF.Ln, bias=1.0, scale=1.0), r=[Blast[li]], w=[Blast[li]])
            k.dve(lambda e, li=li: e.tensor_scalar(out=last[li][:], in0=last[li][:], scalar1=-1.0 / 16.0, scalar2=None, op0=ALU.mult), r=[Blast[li]], w=[Blast[li]])
            k.dma(c.stq, La[t0:t0 + 128, :], last[li][:], r=[Blast[li]])
            for gi, c0 in enumerate([512, 1024, 2064, 2576, 3088]):
                pb = pbank()
                for kt in range(8):
                    k.pe(lambda e, hT=hT, pb=pb, kt=kt, c0=c0, tcs=tcs: e.matmul(ps[pb][:, :], lhsT=hT[:, kt, tcs], rhs=W[:, kt, c0:c0 + 512], start=(kt == 0), stop=(kt == 7)),
                         r=[BhT[tt], BWof(c0), BWof(c0 + 511)], w=[c.B_ps[pb]], inc=(kt == 7))
                if gi < 2:
                    si = stage()
                    if gi == 0:
                        k.act(lambda e, pb=pb, si=si: e.copy(out=fst[si][:], in_=ps[pb][:, :]), r=[c.B_ps[pb]], w=[Bfst[si]])
                    else:
                        k.dve(lambda e, pb=pb, si=si: e.tensor_copy(out=fst[si][:], in_=ps[pb][:, :]), r=[c.B_ps[pb]], w=[Bfst[si]])
                    k.dma(c.stq, (Kg if gi == 0 else Vg)[t0:t0 + 128, :], fst[si][:], r=[Bfst[si]])
                elif gi == 4:
                    vi = cnt["v"] % 2
                    cnt["v"] += 1
                    k.act(lambda e, pb=pb, vi=vi: e.copy(out=vst[vi][:, :, 0:64], in_=ps[pb][:, :].rearrange("p (h d) -> p h d", d=64)),
                          r=[c.B_ps[pb]], w=[Bvst[vi]])
                    k.dma(c.stq, Vf[t0:t0 + 128, :, :], vst[vi][:], r=[Bvst[vi]])
                else:
                    qi = gi - 2
                    s_ = cnt["s"] % 2
                    cnt["s"] += 1
                    k.act(lambda e, pb=pb, s_=s_: e.activation(out=sq[s_][:], in_=ps[pb][:, :], func=AF.Square), r=[c.B_ps[pb]], w=[Bsq[s_]])
                    k.dve(lambda e, s_=s_: e.tensor_reduce(out=ssh[s_][:], in_=sq[s_][:].rearrange("p (h d) -> p h d", d=64), axis=AX.X, op=ALU.add),
                          r=[Bsq[s_]], w=[Bssh[s_]])
                    k.act(lambda e, s_=s_: e.activation(out=ssh[s_][:], in_=ssh[s_][:], func=AF.Sqrt, bias=c.eps_col[:], scale=1.0 / 64.0),
                          r=[Bssh[s_], c.B_const], w=[Bssh[s_]])
                    k.dve(lambda e, s_=s_: e.reciprocal(out=ssh[s_][:], in_=ssh[s_][:]), r=[Bssh[s_]], w=[Bssh[s_]])
                    if qi == 0:
                        k.dve(lambda e, pb=pb, s_=s_: e.tensor_tensor(out=sq[s_][:].rearrange("p (h d) -> p h d", d=64),
                                                                      in0=ps[pb][:, :].rearrange("p (h d) -> p h d", d=64),
                                                                      in1=ssh[s_][:].unsqueeze(2).broadcast_to([128, 8, 64]), op=ALU.mult),
                              r=[c.B_ps[pb], Bssh[s_]], w=[Bsq[s_]])
                        k.dve(lambda e, s_=s_: e.tensor_tensor(out=qn[s_][:], in0=sq[s_][:], in1=grow[:], op=ALU.mult), r=[Bsq[s_], Bsm], w=[Bqn[s_]])
                    else:
                        k.dve(lambda e, pb=pb, s_=s_: e.tensor_tensor(out=qn[s_][:].rearrange("p (h d) -> p h d", d=64),
                                                                      in0=ps[pb][:, :].rearrange("p (h d) -> p h d", d=64),
                                                                      in1=ssh[s_][:].unsqueeze(2).broadcast_to([128, 8, 64]), op=ALU.mult),
                              r=[c.B_ps[pb], Bssh[s_]], w=[Bqn[s_]])
                    def tr(qi=qi, s_=s_, tcs=tcs):
                        pv = ps[4 + qi][:].bitcast(BF16)
                        for pr in range(4):
                            k.pe(lambda e, pr=pr: e.transpose(pv[:, pr * 128:(pr + 1) * 128], qn[s_][:, pr * 128:(pr + 1) * 128], c.ident_bf[:]),
                                 r=[Bqn[s_], c.B_const], w=[c.B_ps[4 + qi]], inc=(pr == 3))
                        if qi == 0:
                            k.act(lambda e: e.copy(out=qTs[0][:, :, tcs], in_=pv[:, 0:512].rearrange("p (a t) -> p a t", t=128)),
                                  r=[c.B_ps[4]], w=[BqTs[0]])
                        else:
                            k.dve(lambda e: e.tensor_copy(out=qTs[1][:, :, tcs], in_=pv[:, 0:512].rearrange("p (a t) -> p a t", t=128)),
                                  r=[c.B_ps[5]], w=[BqTs[1]])
                    deferred.append(tr)
                if gi == 4 or gi == 1:
                    flush()
            flush()
        for qi, dst in enumerate([QfT, KfT]):
            for pr in range(4):
                for hh in range(2):
                    k.dma(c.stq, dst[2 * pr + hh, 0:64, tc], qTs[qi][64 * hh:64 * hh + 64, pr, :], r=[BqTs[qi]])
    k.barrier()
    A.reset(m_post)
    cumN = A.tile([8, S], F32, "cumN")
    Bcum = Buf()
    k.dve(lambda e: e.tensor_tensor_scan(out=cumN[:], data0=c.ones_f[0:8, 0:1].broadcast_to([8, S]), data1=LSn[:], initial=0.0,
                                         op0=ALU.mult, op1=ALU.add), r=[BLS, c.B_const], w=[Bcum])
    CH = min(2048, S)
    augq = A.tile([8, 6, CH], BF16, "augq")
    augk = A.tile([8, 6, CH], BF16, "augk")
    r1 = A.tile([8, CH], F32, "r1")
    hf = A.tile([8, CH], F32, "hf")
    Baq, Bak, Br1, Bhf = Buf(), Buf(), Buf(), Buf()
    for ch in range(S // CH):
        cs = slice(ch * CH, (ch + 1) * CH)
        k.pool(lambda e: e.memset(augq[:, 3:6, :], 1.0), w=[Baq])
        k.pool(lambda e: e.memset(augk[:, 0:3, :], 1.0), w=[Bak])
        src = cumN[:, cs]
        for j in range(3):
            k.dve(lambda e, j=j, src=src: e.tensor_copy(out=augk[:, 3 + j, :], in_=(src if j == 0 else r1[:])), r=[Bcum, Br1], w=[Bak])
            k.dve(lambda e, j=j: e.tensor_scalar(out=augq[:, j, :], in0=augk[:, 3 + j, :], scalar1=-1.0, scalar2=None, op0=ALU.mult), r=[Bak], w=[Baq])
            if j < 2:
                k.dve(lambda e, j=j: e.tensor_copy(out=hf[:], in_=augk[:, 3 + j, :]), r=[Bak], w=[Bhf])
                k.dve(lambda e, j=j, src=src: e.tensor_tensor(out=r1[:], in0=(src if j == 0 else r1[:]), in1=hf[:], op=ALU.subtract),
                      r=[Bcum, Bhf, Br1], w=[Br1])
        k.dma(c.stq, QfT[:, 64:70, cs], augq[:], r=[Baq])
        k.dma(c.stq, KfT[:, 64:70, cs], augk[:], r=[Bak])
    k.barrier()
    A.reset(m)


def phase_gla(c, T, gla_gain, MixT):
    k, nc, A, ps, S = c.k, c.nc, c.A, c.ps, c.S
    m = A.mark()
    Bc = Buf("glaconst")
    tri = A.tile([128, 128], F32, "tri")
    triBD = A.tile([128, 128], F32, "triBD")
    upBD = A.tile([128, 128], F32, "upBD")
    cind = A.tile([128, 2], F32, "cind")
    m64 = A.tile([128, 64], F32, "m64")
    bones = A.tile([128, 128], BF16, "bones")
    k.pool(lambda e: e.affine_select(out=tri[:], in_=c.ones_f[:], pattern=[[1, 128]], compare_op=ALU.is_ge, fill=0.0, base=0, channel_multiplier=-1),
           r=[c.B_const], w=[Bc])
    k.pool(lambda e: e.tensor_copy(out=triBD[:], in_=tri[:]), r=[Bc], w=[Bc])
    k.pool(lambda e: e.memset(triBD[0:64, 64:128], 0.0), w=[Bc])
    k.pool(lambda e: e.affine_select(out=upBD[:], in_=c.ones_f[:], pattern=[[-1, 128]], compare_op=ALU.is_gt, fill=0.0, base=0, channel_multiplier=1),
           r=[c.B_const], w=[Bc])
    k.pool(lambda e: e.memset(upBD[64:128, 0:64], 0.0), w=[Bc])
    k.pool(lambda e: e.memset(cind[:], 0.0), w=[Bc])
    k.pool(lambda e: e.memset(cind[0:64, 0:1], 1.0), w=[Bc])
    k.pool(lambda e: e.memset(cind[64:128, 1:2], 1.0), w=[Bc])
    k.pool(lambda e: e.tensor_copy(out=m64[0:64, :], in_=tri[0:64, 0:64]), r=[Bc], w=[Bc])
    k.pool(lambda e: e.tensor_copy(out=m64[64:128, :], in_=tri[64:128, 64:128]), r=[Bc], w=[Bc])
    k.pool(lambda e: e.memset(bones[:], 0.0), w=[Bc])
    k.pool(lambda e: e.memset(bones[0:64, 0:64], 1.0), w=[Bc])
    k.pool(lambda e: e.memset(bones[64:128, 64:128], 1.0), w=[Bc])
    gcol = A.tile([128, 4], F32, "gcol")
    for pr in range(4):
        k.dma("sp", gcol[:, pr:pr + 1], gla_gain[2 * pr:2 * pr + 2, :].rearrange("h (v o) -> (h v) o", o=1), w=[Bc], allow_slow_non_contiguous=True)
    state = A.tile([128, 4, 64], F32, "state")
    state_bf = A.tile([128, 4, 64], BF16, "statebf")
    Bst, Bstb = Buf(), Buf()
    k.dve(lambda e: e.memset(state[:], 0.0), w=[Bst])
    k.dve(lambda e: e.memset(state_bf[:], 0.0), w=[Bstb])
    NB = 2
    qT = [A.tile([128, 4, 512], BF16, "gqT") for _ in range(NB)]
    kT = [A.tile([128, 4, 512], BF16, "gkT") for _ in range(NB)]
    gT = [A.tile([128, 4, 512], BF16, "ggT") for _ in range(NB)]
    kg = [A.tile([128, 4, 512], BF16, "gkg") for _ in range(NB)]
    vg = [A.tile([128, 4, 512], BF16, "gvg") for _ in range(NB)]
    la = [A.tile([128, 4, 512], F32, "gla") for _ in range(NB)]
    Bin = [[Buf() for _ in range(6)] for _ in range(NB)]
    mixo = [A.tile([128, 4, 512], BF16, "gmix") for _ in range(NB)]
    Bmixo = [Buf() for _ in range(NB)]
    P2 = range(2)
    e1 = [A.tile([128, 512], F32, "ge1") for _ in P2]; Be1 = [Buf() for _ in P2]
    e2 = [A.tile([128, 512], F32, "ge2") for _ in P2]; Be2 = [Buf() for _ in P2]
    ed = [A.tile([128, 512], F32, "ged") for _ in P2]; Bed = [Buf() for _ in P2]
    ebs = [A.tile([128, 8], F32, "gebs") for _ in P2]; Bebs = [Buf() for _ in P2]
    qd = [A.tile([128, 4, 2, 128], BF16, "gqd") for _ in P2]; Bqd = [Buf() for _ in P2]
    e1z = [A.tile([128, 4, 2, 128], F32, "ge1z") for _ in P2]; Be1z = [Buf() for _ in P2]
    e2z = [A.tile([128, 2, 512], F32, "ge2z") for _ in P2]; Be2z = [Buf() for _ in P2]
    kd = [A.tile([128, 4, 128], BF16, "gkd") for _ in P2]; Bkd = [Buf() for _ in P2]
    kdec = [A.tile([128, 2, 512], BF16, "gkdec") for _ in P2]; Bkdec = [Buf() for _ in P2]
    scT = [A.tile([128, 8, 2, 64], BF16, "gscT") for _ in P2]; BscT = [Buf() for _ in P2]
    m64z = A.tile([128, 2, 64], F32, "m64z")
    k.pool(lambda e: e.memset(m64z[:], 0.0), w=[Bc])
    k.pool(lambda e: e.tensor_copy(out=m64z[0:64, 0, :], in_=tri[0:64, 0:64]), r=[Bc], w=[Bc])
    k.pool(lambda e: e.tensor_copy(out=m64z[64:128, 1, :], in_=tri[64:128, 64:128]), r=[Bc], w=[Bc])
    osb = A.tile([128, 512], F32, "gosb"); Bosb = Buf()
    sqb = A.tile([128, 512], BF16, "gsqb"); Bsqb = Buf()
    rs = A.tile([128, 512], F32, "grs"); Brs = Buf()
    BP = c.B_ps
    v3 = lambda ap: ap.rearrange("p (a t) -> p a t", a=4)
    NT = S // 128

    def load_sb(sb):
        b = sb % NB
        tc = slice(sb * 512, (sb + 1) * 512)
        k.dma("sp", qT[b][:], T["QgT"].rearrange("(a p) t -> p a t", p=128)[:, :, tc], w=[Bin[b][0]])
        k.dma("sp", kT[b][:], T["KgT"].rearrange("(a p) t -> p a t", p=128)[:, :, tc], w=[Bin[b][1]])
        k.dma("sp", gT[b][:], T["GgT"].rearrange("(a p) t -> p a t", p=128)[:, :, tc], w=[Bin[b][2]])
        k.dma("sp", kg[b][:], T["Kg"][tc, :].rearrange("(a p) e -> p a e", p=128), w=[Bin[b][3]])
        k.dma("sp", vg[b][:], T["Vg"][tc, :].rearrange("(a p) e -> p a e", p=128), w=[Bin[b][4]])
        k.dma("sp", la[b][:], T["La"][tc, :].rearrange("(a p) e -> p a e", p=128), w=[Bin[b][5]])
        k.dve(lambda e: e.tensor_tensor(out=gT[b][:], in0=gT[b][:], in1=gcol[:].unsqueeze(2).broadcast_to([128, 4, 512]), op=ALU.mult),
              r=[Bin[b][2], Bc], w=[Bin[b][2]])

    def stageA(i):
        sb, tt = divmod(i, 4)
        if tt == 0:
            load_sb(sb)
        b, q = sb % NB, i % 2
        tcs = slice(tt * 128, (tt + 1) * 128)
        for pr in range(4):
            k.pe(lambda e, pr=pr: e.matmul(ps[1][:, pr * 128:(pr + 1) * 128], lhsT=la[b][:, tt, pr * 128:(pr + 1) * 128], rhs=triBD[:],
                                           start=True, stop=True), r=[Bin[b][5], Bc], w=[BP[1]], inc=(pr == 3))
        for pr in range(4):
            k.pe(lambda e, pr=pr: e.matmul(ps[2][:, pr * 2:pr * 2 + 2], lhsT=la[b][:, tt, pr * 128:(pr + 1) * 128], rhs=cind[:],
                                           start=True, stop=True), r=[Bin[b][5], Bc], w=[BP[2]], inc=(pr == 3))
        k.pe(lambda e: e.matmul(ps[3][:, :], lhsT=upBD[:], rhs=la[b][:, tt, :], start=True, stop=True), r=[Bin[b][5], Bc], w=[BP[3]])
        k.act(lambda e: e.activation(out=e1[q][:], in_=ps[1][:, :], func=AF.Exp), r=[BP[1]], w=[Be1[q]])
        k.act(lambda e: e.activation(out=e2[q][:], in_=ps[1][:, :], func=AF.Exp, scale=-1.0), r=[BP[1]], w=[Be2[q]])
        k.act(lambda e: e.activation(out=ed[q][:], in_=ps[3][:, :], func=AF.Exp), r=[BP[3]], w=[Bed[q]])
        k.act(lambda e: e.activation(out=ebs[q][:], in_=ps[2][:, 0:8], func=AF.Exp), r=[BP[2]], w=[Bebs[q]])
        k.dve(lambda e: e.tensor_tensor(out=e1z[q][:], in0=v3(e1[q][:]).unsqueeze(2).broadcast_to([128, 4, 2, 128]),
                                        in1=cind[:].unsqueeze(1).unsqueeze(3).broadcast_to([128, 4, 2, 128]), op=ALU.mult), r=[Be1[q], Bc], w=[Be1z[q]])
        k.dve(lambda e: e.tensor_tensor(out=kd[q][:], in0=kT[b][:, :, tcs], in1=v3(e2[q][:]), op=ALU.mult), r=[Bin[b][1], Be2[q]], w=[Bkd[q]])
        k.dve(lambda e: e.tensor_tensor(out=e2z[q][:], in0=ed[q][:].unsqueeze(1).broadcast_to([128, 2, 512]),
                                        in1=cind[:].unsqueeze(2).broadcast_to([128, 2, 512]), op=ALU.mult), r=[Bed[q], Bc], w=[Be2z[q]])
        k.dve(lambda e: e.tensor_tensor(out=qd[q][:], in0=qT[b][:, :, tcs].unsqueeze(2).broadcast_to([128, 4, 2, 128]), in1=e1z[q][:], op=ALU.mult),
              r=[Bin[b][0], Be1z[q]], w=[Bqd[q]])
        k.dve(lambda e: e.tensor_tensor(out=kdec[q][:], in0=kg[b][:, tt, :].unsqueeze(1).broadcast_to([128, 2, 512]), in1=e2z[q][:], op=ALU.mult),
              r=[Bin[b][3], Be2z[q]], w=[Bkdec[q]])

    def stageB(i):
        q = i % 2
        n = 0
        for pr in range(4):
            for hh in range(2):
                for cc in range(2):
                    n += 1
                    k.pe(lambda e, pr=pr, hh=hh, cc=cc: e.matmul(ps[4][64 * cc:64 * cc + 64, (2 * pr + hh) * 64:(2 * pr + hh) * 64 + 64],
                                                                 lhsT=kd[q][:, pr, 64 * cc:64 * cc + 64],
                                                                 rhs=qd[q][:, pr, hh, 64 * cc:64 * cc + 64], start=True, stop=True),
                         r=[Bkd[q], Bqd[q]], w=[BP[4]], inc=(n == 16))
        k.dve(lambda e: e.tensor_tensor(out=scT[q][:], in0=ps[4][:, :].rearrange("p (a t) -> p a t", t=64).unsqueeze(2).broadcast_to([128, 8, 2, 64]),
                                        in1=m64z[:].unsqueeze(1).broadcast_to([128, 8, 2, 64]), op=ALU.mult), r=[BP[4], Bc], w=[BscT[q]])

    def stageC(i, cc):
        sb, tt = divmod(i, 4)
        b, q = sb % NB, i % 2
        n = 0
        for pr in range(4):
            for hh in range(2):
                h = 2 * pr + hh
                oc = slice(pr * 128 + cc * 64, pr * 128 + cc * 64 + 64)
                k.pe(lambda e, pr=pr, hh=hh, oc=oc: e.matmul(ps[5][64 * hh:64 * hh + 64, oc], lhsT=state_bf[:, pr, :],
                                                             rhs=qd[q][:, pr, hh, 64 * cc:64 * cc + 64], start=True, stop=False),
                     r=[Bstb, Bqd[q]], w=[BP[5]], inc=False)
                n += 1
                k.pe(lambda e, h=h, hh=hh, oc=oc: e.matmul(ps[5][64 * hh:64 * hh + 64, oc], lhsT=vg[b][:, tt, h * 64:(h + 1) * 64],
                                                           rhs=scT[q][:, h, cc, :], start=False, stop=True),
                     r=[Bin[b][4], BscT[q]], w=[BP[5]], inc=(n == 8))
        n = 0
        for pr in range(4):
            for hh in range(2):
                h = 2 * pr + hh
                n += 1
                k.pe(lambda e, h=h, hh=hh, pr=pr: e.matmul(ps[6][64 * hh:64 * hh + 64, (cc * 4 + pr) * 64:(cc * 4 + pr) * 64 + 64],
                                                           lhsT=kdec[q][:, cc, h * 64:(h + 1) * 64],
                                                           rhs=vg[b][:, tt, h * 64:(h + 1) * 64], start=True, stop=True),
                     r=[Bkdec[q], Bin[b][4]], w=[BP[6]], inc=(n == 8))
        for pr in range(4):
            k.dve(lambda e, pr=pr: e.scalar_tensor_tensor(out=state[:, pr, :], in0=state[:, pr, :], scalar=ebs[q][:, pr * 2 + cc:pr * 2 + cc + 1],
                                                          in1=ps[6][:, (cc * 4 + pr) * 64:(cc * 4 + pr) * 64 + 64], op0=ALU.mult, op1=ALU.add),
                  r=[Bst, Bebs[q], BP[6]], w=[Bst])
        k.act(lambda e: e.copy(out=state_bf[:], in_=state[:]), r=[Bst], w=[Bstb])

    def stageD(i):
        sb, tt = divmod(i, 4)
        b = sb % NB
        tcs = slice(tt * 128, (tt + 1) * 128)
        k.act(lambda e: e.copy(out=osb[:], in_=ps[5][:, :]), r=[BP[5]], w=[Bosb])
        k.act(lambda e: e.activation(out=sqb[:], in_=ps[5][:, :], func=AF.Square), r=[BP[5]], w=[Bsqb])
        k.pe(lambda e: e.matmul(ps[7][:, :], lhsT=bones[:], rhs=sqb[:], start=True, stop=True), r=[Bsqb, Bc], w=[BP[7]])
        k.act(lambda e: e.activation(out=rs[:], in_=ps[7][:, :], func=AF.Sqrt, bias=c.eps_col[:], scale=1.0 / 64.0), r=[BP[7], c.B_const], w=[Brs])
        k.dve(lambda e: e.reciprocal(out=rs[:], in_=rs[:]), r=[Brs], w=[Brs])
        k.dve(lambda e: e.tensor_tensor(out=osb[:], in0=osb[:], in1=rs[:], op=ALU.mult), r=[Bosb, Brs], w=[Bosb])
        k.dve(lambda e: e.tensor_tensor(out=mixo[b][:, :, tcs], in0=v3(osb[:]), in1=gT[b][:, :, tcs], op=ALU.mult),
              r=[Bosb, Bin[b][2]], w=[Bmixo[b]])
        if tt == 3:
            tc = slice(sb * 512, (sb + 1) * 512)
            k.dma(c.stq, MixT[0:512, :].rearrange("(a p) t -> p a t", p=128)[:, :, tc], mixo[b][:], r=[Bmixo[b]])

    stageA(0)
    stageB(0)
    for i in range(NT):
        if i + 1 < NT:
            stageA(i + 1)
        stageC(i, 0)
        if i + 1 < NT:
            stageB(i + 1)
        stageC(i, 1)
        stageD(i)
    k.barrier()
    A.reset(m)


def phase_fox(c, T, MixT):
    k, nc, A, ps, S = c.k, c.nc, c.A, c.ps, c.S
    m = A.mark()
    NBLK = S // 128
    Bc = Buf("foxconst")
    negm = A.tile([128, 128], BF16, "negm")
    zer = A.tile([128, 128], F32, "zer")
    k.pool(lambda e: e.memset(zer[:], 0.0), w=[Bc])
    k.pool(lambda e: e.affine_select(out=zer[:], in_=zer[:], pattern=[[1, 128]], compare_op=ALU.is_ge, fill=-30000.0, base=0, channel_multiplier=-1),
           r=[Bc], w=[Bc])
    k.pool(lambda e: e.tensor_copy(out=negm[:], in_=zer[:]), r=[Bc], w=[Bc])
    m8 = A.tile([128, 1], F32, "m8")
    k.pool(lambda e: e.memset(m8[:], -8.0), w=[Bc])
    V = A.tile([128, NBLK, 8 * 65], BF16, "foxV")
    BV = Buf()
    k.dma("sp", V[:], T["Vf"].rearrange("(a p) h d -> p a (h d)", p=128), w=[BV])
    KT = [A.tile([128, S], BF16, "foxK") for _ in range(2)]
    QT = [A.tile([128, S], BF16, "foxQ") for _ in range(2)]
    BKQ = [Buf(), Buf()]
    for hb_ in range(2):
        k.pool(lambda e, hb_=hb_: e.memset(KT[hb_][:], 0.0), w=[BKQ[hb_]])
        k.pool(lambda e, hb_=hb_: e.memset(QT[hb_][:], 0.0), w=[BKQ[hb_]])
    NPT = 4
    pt = [A.tile([128, 512], BF16, "foxP") for _ in range(NPT)]
    Bpt = [Buf() for _ in range(NPT)]
    osb = [A.tile([128, 512], F32, "foxO") for _ in range(2)]
    Bosb = [Buf(), Buf()]
    fout = [A.tile([64, 512], BF16, "foxF") for _ in range(2)]
    Bfout = [Buf(), Buf()]
    BP = c.B_ps
    work = []
    for h in range(8):
        for qb in range(S // 512):
            nj = 4 * qb + 4
            for j in range(nj):
                work.append((h, qb, j, nj))
    LA = 2
    state = dict(si=0)

    def issue_qk(i):
        h, qb, j, nj = work[i]
        hb = h % 2
        if qb == 0 and j == 0:
            k.dma("sp", KT[hb][0:70, :], T["KfT"][h], w=[BKQ[hb]])
            k.dma("sp", QT[hb][0:70, :], T["QfT"][h], w=[BKQ[hb]])
        sbank = i % 4
        q0 = qb * 512
        jj = j - 4 * qb
        lk = KT[hb][:, j * 128:(j + 1) * 128]
        if jj < 0:
            k.pe(lambda e: e.matmul(ps[sbank][:, :], lhsT=lk, rhs=QT[hb][:, q0:q0 + 512], start=True, stop=True), r=[BKQ[hb]], w=[BP[sbank]])
            c0 = 0
        else:
            c0 = 128 * jj
            k.pe(lambda e: e.matmul(ps[sbank][:, c0:c0 + 128], lhsT=lk, rhs=QT[hb][:, q0 + c0:q0 + c0 + 128], start=True, stop=False),
                 r=[BKQ[hb]], w=[BP[sbank]], inc=False)
            k.pe(lambda e: e.matmul(ps[sbank][:, c0:c0 + 128], lhsT=c.ident_bf[:], rhs=negm[:], start=False, stop=True),
                 r=[c.B_const, Bc], w=[BP[sbank]], inc=(c0 + 128 >= 512))
            if c0 + 128 < 512:
                k.pe(lambda e: e.matmul(ps[sbank][:, c0 + 128:512], lhsT=lk, rhs=QT[hb][:, q0 + c0 + 128:q0 + 512], start=True, stop=True),
                     r=[BKQ[hb]], w=[BP[sbank]])
        pi = i % NPT
        k.act(lambda e: e.activation(out=pt[pi][:, c0:512], in_=ps[sbank][:, c0:512], func=AF.Exp, bias=m8[:, 0:1], scale=1.0),
              r=[BP[sbank], Bc], w=[Bpt[pi]])

    def issue_pv(i):
        h, qb, j, nj = work[i]
        jj = j - 4 * qb
        c0 = 0 if jj < 0 else 128 * jj
        pi = i % NPT
        ob = 4 + (h * (S // 512) + qb) % 2
        k.pe(lambda e: e.matmul(ps[ob][0:65, c0:512], lhsT=V[:, j, h * 65:(h + 1) * 65], rhs=pt[pi][:, c0:512], start=(j == 0), stop=(j == nj - 1)),
             r=[BV, Bpt[pi]], w=[BP[ob]], inc=True)
        if j == nj - 1:
            oi = state["si"] % 2
            state["si"] += 1
            k.act(lambda e: e.copy(out=osb[oi][0:65, :], in_=ps[ob][0:65, :]), r=[BP[ob]], w=[Bosb[oi]])
            k.dve(lambda e: e.reciprocal(out=osb[oi][64:65, :], in_=osb[oi][64:65, :]), r=[Bosb[oi]], w=[Bosb[oi]])
            k.pe(lambda e: e.matmul(ps[6][0:64, :], lhsT=c.ones_f[64:65, 0:64], rhs=osb[oi][64:65, :], start=True, stop=True),
                 r=[Bosb[oi], c.B_const], w=[BP[6]])
            k.dve(lambda e: e.tensor_tensor(out=fout[oi][:], in0=osb[oi][0:64, :], in1=ps[6][0:64, :], op=ALU.mult), r=[Bosb[oi], BP[6]], w=[Bfout[oi]])
            k.dma(c.stq, MixT[512 + 64 * h:512 + 64 * h + 64, qb * 512:(qb + 1) * 512], fout[oi][:], r=[Bfout[oi]])

    n = len(work)
    for i in range(n + LA):
        if i < n:
            issue_qk(i)
        if i >= LA:
            issue_pv(i - LA)
    k.barrier()
    A.reset(m)


TWO_PI = 6.283185307179586
PI = 3.141592653589793


def s5_range_reduce(c, arg, shape, scr_i, scr_f, Bs):
    k = c.k
    k.dve(lambda e: e.tensor_scalar(out=scr_i, in0=arg, scalar1=1.0 / TWO_PI, scalar2=None, op0=ALU.mult), r=[Bs], w=[Bs])
    k.dve(lambda e: e.tensor_copy(out=scr_f, in_=scr_i), r=[Bs], w=[Bs])
    k.dve(lambda e: e.scalar_tensor_tensor(out=arg, in0=scr_f, scalar=-TWO_PI, in1=arg, op0=ALU.mult, op1=ALU.add), r=[Bs], w=[Bs])
    k.dve(lambda e: e.tensor_scalar(out=scr_f, in0=arg, scalar1=PI, scalar2=TWO_PI, op0=ALU.is_gt, op1=ALU.mult), r=[Bs], w=[Bs])
    k.dve(lambda e: e.tensor_tensor(out=arg, in0=arg, in1=scr_f, op=ALU.subtract), r=[Bs], w=[Bs])
    k.dve(lambda e: e.tensor_scalar(out=scr_f, in0=arg, scalar1=-PI, scalar2=TWO_PI, op0=ALU.is_lt, op1=ALU.mult), r=[Bs], w=[Bs])
    k.dve(lambda e: e.tensor_tensor(out=arg, in0=arg, in1=scr_f, op=ALU.add), r=[Bs], w=[Bs])
    k.dve(lambda e: e.tensor_scalar(out=arg, in0=arg, scalar1=PI, scalar2=-PI, op0=ALU.min, op1=ALU.max), r=[Bs], w=[Bs])


def s5_disc(c, Lre, Lim, ldt, F, Bs):
    k, A = c.k, c.A
    t = {n: A.tile([128, F], F32, "s5_" + n) for n in ("dt", "lr", "th", "r", "sn", "cs", "ar", "ai", "den", "t1", "t2", "cre", "cim", "sf")}
    ti = A.tile([128, F], I32, "s5_i")
    k.act(lambda e: e.activation(out=t["dt"][:], in_=ldt, func=AF.Exp), r=[Bs], w=[Bs])
    k.dve(lambda e: e.tensor_tensor(out=t["lr"][:], in0=Lre, in1=t["dt"][:], op=ALU.mult), r=[Bs], w=[Bs])
    k.dve(lambda e: e.tensor_tensor(out=t["th"][:], in0=Lim, in1=t["dt"][:], op=ALU.mult), r=[Bs], w=[Bs])
    k.act(lambda e: e.activation(out=t["r"][:], in_=t["lr"][:], func=AF.Exp), r=[Bs], w=[Bs])
    k.dve(lambda e: e.tensor_copy(out=t["sn"][:], in_=t["th"][:]), r=[Bs], w=[Bs])
    s5_range_reduce(c, t["sn"][:], [128, F], ti[:], t["sf"][:], Bs)
    k.act(lambda e: e.activation(out=t["sn"][:], in_=t["sn"][:], func=AF.Sin), r=[Bs], w=[Bs])
    k.dve(lambda e: e.tensor_scalar(out=t["cs"][:], in0=t["th"][:], scalar1=PI / 2, scalar2=None, op0=ALU.add), r=[Bs], w=[Bs])
    s5_range_reduce(c, t["cs"][:], [128, F], ti[:], t["sf"][:], Bs)
    k.act(lambda e: e.activation(out=t["cs"][:], in_=t["cs"][:], func=AF.Sin), r=[Bs], w=[Bs])
    k.dve(lambda e: e.tensor_tensor(out=t["ar"][:], in0=t["r"][:], in1=t["cs"][:], op=ALU.mult), r=[Bs], w=[Bs])
    k.dve(lambda e: e.tensor_tensor(out=t["ai"][:], in0=t["r"][:], in1=t["sn"][:], op=ALU.mult), r=[Bs], w=[Bs])
    k.dve(lambda e: e.tensor_tensor(out=t["den"][:], in0=Lre, in1=Lre, op=ALU.mult), r=[Bs], w=[Bs])
    k.dve(lambda e: e.tensor_tensor(out=t["t1"][:], in0=Lim, in1=Lim, op=ALU.mult), r=[Bs], w=[Bs])
    k.dve(lambda e: e.tensor_tensor(out=t["den"][:], in0=t["den"][:], in1=t["t1"][:], op=ALU.add), r=[Bs], w=[Bs])
    k.dve(lambda e: e.reciprocal(out=t["den"][:], in_=t["den"][:]), r=[Bs], w=[Bs])
    k.dve(lambda e: e.tensor_scalar(out=t["t1"][:], in0=t["ar"][:], scalar1=-1.0, scalar2=None, op0=ALU.add), r=[Bs], w=[Bs])
    k.dve(lambda e: e.tensor_tensor(out=t["cre"][:], in0=t["t1"][:], in1=Lre, op=ALU.mult), r=[Bs], w=[Bs])
    k.dve(lambda e: e.tensor_tensor(out=t["t2"][:], in0=t["ai"][:], in1=Lim, op=ALU.mult), r=[Bs], w=[Bs])
    k.dve(lambda e: e.tensor_tensor(out=t["cre"][:], in0=t["cre"][:], in1=t["t2"][:], op=ALU.add), r=[Bs], w=[Bs])
    k.dve(lambda e: e.tensor_tensor(out=t["cre"][:], in0=t["cre"][:], in1=t["den"][:], op=ALU.mult), r=[Bs], w=[Bs])
    k.dve(lambda e: e.tensor_tensor(out=t["cim"][:], in0=t["ai"][:], in1=Lre, op=ALU.mult), r=[Bs], w=[Bs])
    k.dve(lambda e: e.tensor_tensor(out=t["t2"][:], in0=t["t1"][:], in1=Lim, op=ALU.mult), r=[Bs], w=[Bs])
    k.dve(lambda e: e.tensor_tensor(out=t["cim"][:], in0=t["cim"][:], in1=t["t2"][:], op=ALU.subtract), r=[Bs], w=[Bs])
    k.dve(lambda e: e.tensor_tensor(out=t["cim"][:], in0=t["cim"][:], in1=t["den"][:], op=ALU.mult), r=[Bs], w=[Bs])
    return t


def phase_odd(c, l, Xin, w_in, P, MixT):
    k, nc, A, ps, S = c.k, c.nc, c.A, c.ps, c.S
    BP = c.B_ps
    m0 = A.mark()
    LB = 512
    W = A.tile([128, 8, 1536], BF16, "Wodd")
    BW = [Buf() for _ in range(3)]
    wv = dram_cols_FM(w_in)
    for q in range(3):
        k.dma("pool", W[:, :, q * 512:(q + 1) * 512], wv[:, :, q * 512:(q + 1) * 512], w=[BW[q]])
    Bs = Buf("s5prep")
    Bloads = []

    def newbuf():
        bb_ = Buf()
        Bloads.append(bb_)
        return bb_

    def join_loads():
        if Bloads:
            k.dve(lambda e: e.memset(jn[:], 0.0), r=list(Bloads), w=[Bs])
            del Bloads[:]

    Ecb = A.tile([128, 16, LB], BF16, "Ecb")
    Esb = A.tile([128, 16, LB], BF16, "Esb")
    EcL = A.tile([128, 16], F32, "EcL")
    EsL = A.tile([128, 16], F32, "EsL")
    rcol = A.tile([128, 16], F32, "rcol")
    Bbd = [A.tile([128, 4, 512], BF16, "Bbd") for _ in range(2)]
    Cbd = [A.tile([128, 16, 128], BF16, "Cbd") for _ in range(2)]
    Dd = A.tile([128, 4, 128], BF16, "Dd")
    Wglu = A.tile([128, 4, 512], BF16, "Wglu")
    bglu = A.tile([128, 4], F32, "bglu")
    WmT = A.tile([128, 8, 128], BF16, "WmT")
    bsel = A.tile([8, 4, 128], BF16, "bsel")
    bs8 = A.tile([8, 128], BF16, "bs8")
    lnrow = [A.tile([128, 512], F32, "lnrow") for _ in range(2)]
    jn = A.tile([128, 1], F32, "jn")
    z0 = A.tile([128, 16, 2], F32, "z0")
    Bz0 = Buf()
    k.dve(lambda e: e.memset(z0[:], 0.0), w=[Bz0])
    mp = A.mark()
    Ec = A.tile([128, 16, LB], F32, "Ec")
    Es = A.tile([128, 16, LB], F32, "Es")
    LA_re = A.tile([128, 16], F32, "LAre"); LA_im = A.tile([128, 16], F32, "LAim"); LA_dt = A.tile([128, 16], F32, "LAdt")
    for gl in range(2):
        k.dma("sp", LA_re[64 * gl:64 * gl + 64, :], P["lam_re"].rearrange("(m gl) p -> gl p m", gl=2)[gl], w=[newbuf()], allow_slow_non_contiguous=True)
        k.dma("sp", LA_im[64 * gl:64 * gl + 64, :], P["lam_im"].rearrange("(m gl) p -> gl p m", gl=2)[gl], w=[newbuf()], allow_slow_non_contiguous=True)
    ldv = P["log_dt"].rearrange("(m gl) -> gl m", gl=2)
    for gl in range(2):
        k.dma("sp", LA_dt[64 * gl:64 * gl + 64, :], ldv[gl].partition_broadcast(64), w=[newbuf()], allow_slow_non_contiguous=True)
    join_loads()
    dA = s5_disc(c, LA_re[:], LA_im[:], LA_dt[:], 16, Bs)
    k.dve(lambda e: e.tensor_copy(out=rcol[:], in_=dA["r"][:]), r=[Bs], w=[Bs])
    sI = A.tile([128, LB], F32, "sI")
    sIi = A.tile([128, LB], I32, "sIi")
    k.pool(lambda e: e.iota(sIi[:], pattern=[[1, LB]], base=1, channel_multiplier=0), w=[Bs])
    k.dve(lambda e: e.tensor_copy(out=sI[:], in_=sIi[:]), r=[Bs], w=[Bs])
    argi = A.tile([128, 4, LB], I32, "argi")
    argf = A.tile([128, 4, LB], F32, "argf")
    for tab, shift in ((Es, 0.0), (Ec, PI / 2)):
        for mq in range(4):
            tv = tab[:, 4 * mq:4 * mq + 4, :]
            k.dve(lambda e, tv=tv, mq=mq: e.tensor_tensor(out=tv, in0=dA["th"][:, 4 * mq:4 * mq + 4].unsqueeze(2).broadcast_to([128, 4, LB]),
                                                          in1=sI[:].unsqueeze(1).broadcast_to([128, 4, LB]), op=ALU.mult), r=[Bs], w=[Bs])
            if shift:
                k.dve(lambda e, tv=tv, shift=shift: e.tensor_scalar(out=tv, in0=tv, scalar1=shift, scalar2=None, op0=ALU.add), r=[Bs], w=[Bs])
            s5_range_reduce(c, tv, None, argi[:], argf[:], Bs)
            k.act(lambda e, tv=tv: e.activation(out=tv, in_=tv, func=AF.Sin), r=[Bs], w=[Bs])
    k.dve(lambda e: e.tensor_copy(out=EcL[:], in_=Ec[:, :, LB - 1]), r=[Bs], w=[Bs])
    k.dve(lambda e: e.tensor_copy(out=EsL[:], in_=Es[:, :, LB - 1]), r=[Bs], w=[Bs])
    k.act(lambda e: e.copy(out=Ecb[:], in_=Ec[:]), r=[Bs], w=[Bs])
    k.act(lambda e: e.copy(out=Esb[:], in_=Es[:]), r=[Bs], w=[Bs])
    k.barrier()
    A.reset(mp)
    CT = [A.tile([128, 16, 16], F32, "CT") for _ in range(2)]
    for gl in range(2):
        for m in range(16):
            k.dma("sp", CT[0][64 * gl:64 * gl + 64, m, :], P["c_re"][2 * m + gl].rearrange("i p -> p i"), w=[newbuf()], allow_slow_non_contiguous=True)
            k.dma("sp", CT[1][64 * gl:64 * gl + 64, m, :], P["c_im"][2 * m + gl].rearrange("i p -> p i"), w=[newbuf()], allow_slow_non_contiguous=True)
    join_loads()
    k.dve(lambda e: e.tensor_scalar(out=CT[1][:], in0=CT[1][:], scalar1=-1.0, scalar2=None, op0=ALU.mult), r=[Bs], w=[Bs])
    for ri in range(2):
        k.dve(lambda e, ri=ri: e.memset(Cbd[ri][:], 0.0), w=[Bs])
        for m in range(16):
            for gl in range(2):
                c0 = (2 * (m % 4) + gl) * 16
                k.dve(lambda e, ri=ri, m=m, gl=gl, c0=c0: e.tensor_copy(out=Cbd[ri][64 * gl:64 * gl + 64, m, c0:c0 + 16], in_=CT[ri][64 * gl:64 * gl + 64, m, :]),
                      r=[Bs], w=[Bs])
    LB_re = A.tile([128, 4, 64], F32, "LBre"); LB_im = A.tile([128, 4, 64], F32, "LBim"); LB_d4 = A.tile([128, 4], F32, "LBd4")
    LB_dt = A.tile([128, 4, 64], F32, "LBdt")
    for gl in range(8):
        k.dma("sp", LB_re[16 * gl:16 * gl + 16, :, :], P["lam_re"].rearrange("(ut gl) p -> gl ut p", gl=8)[gl].partition_broadcast(16), w=[newbuf()])
        k.dma("sp", LB_im[16 * gl:16 * gl + 16, :, :], P["lam_im"].rearrange("(ut gl) p -> gl ut p", gl=8)[gl].partition_broadcast(16), w=[newbuf()])
        k.dma("sp", LB_d4[16 * gl:16 * gl + 16, :], P["log_dt"].rearrange("(ut gl) -> gl ut", gl=8)[gl].partition_broadcast(16), w=[newbuf()],
              allow_slow_non_contiguous=True)
    join_loads()
    k.dve(lambda e: e.tensor_copy(out=LB_dt[:], in_=LB_d4[:].unsqueeze(2).broadcast_to([128, 4, 64])), r=[Bs], w=[Bs])
    fl = lambda t_: t_[:].rearrange("p a b -> p (a b)")
    dB = s5_disc(c, fl(LB_re), fl(LB_im), fl(LB_dt), 256, Bs)
    Bt = [A.tile([128, 4, 64], F32, "Bt") for _ in range(2)]
    for gl in range(8):
        for ut in range(4):
            k.dma("sp", Bt[0][16 * gl:16 * gl + 16, ut, :], P["b_re"][8 * ut + gl].rearrange("p j -> j p"), w=[newbuf()], allow_slow_non_contiguous=True)
            k.dma("sp", Bt[1][16 * gl:16 * gl + 16, ut, :], P["b_im"][8 * ut + gl].rearrange("p j -> j p"), w=[newbuf()], allow_slow_non_contiguous=True)
    join_loads()
    bb = [A.tile([128, 256], F32, "bbar") for _ in range(2)]
    tb = A.tile([128, 256], F32, "tb")
    k.dve(lambda e: e.tensor_tensor(out=bb[0][:], in0=dB["cre"][:], in1=fl(Bt[0]), op=ALU.mult), r=[Bs], w=[Bs])
    k.dve(lambda e: e.tensor_tensor(out=tb[:], in0=dB["cim"][:], in1=fl(Bt[1]), op=ALU.mult), r=[Bs], w=[Bs])
    k.dve(lambda e: e.tensor_tensor(out=bb[0][:], in0=bb[0][:], in1=tb[:], op=ALU.subtract), r=[Bs], w=[Bs])
    k.dve(lambda e: e.tensor_tensor(out=bb[1][:], in0=dB["cre"][:], in1=fl(Bt[1]), op=ALU.mult), r=[Bs], w=[Bs])
    k.dve(lambda e: e.tensor_tensor(out=tb[:], in0=dB["cim"][:], in1=fl(Bt[0]), op=ALU.mult), r=[Bs], w=[Bs])
    k.dve(lambda e: e.tensor_tensor(out=bb[1][:], in0=bb[1][:], in1=tb[:], op=ALU.add), r=[Bs], w=[Bs])
    mk8 = A.tile([128, 8], F32, "mk8")
    k.pool(lambda e: e.memset(mk8[:], 1.0), w=[Bs])
    k.pool(lambda e: e.affine_select(out=mk8[:], in_=mk8[:], pattern=[[-16, 8]], compare_op=ALU.is_ge, fill=0.0, base=0, channel_multiplier=1), r=[Bs], w=[Bs])
    k.pool(lambda e: e.affine_select(out=mk8[:], in_=mk8[:], pattern=[[16, 8]], compare_op=ALU.is_gt, fill=0.0, base=16, channel_multiplier=-1), r=[Bs], w=[Bs])
    for ri in range(2):
        k.dve(lambda e, ri=ri: e.tensor_tensor(out=Bbd[ri][:].rearrange("p u (g q) -> p u g q", q=64),
                                               in0=bb[ri][:].rearrange("p (u q) -> p u q", q=64).unsqueeze(2).broadcast_to([128, 4, 8, 64]),
                                               in1=mk8[:].unsqueeze(1).unsqueeze(3).broadcast_to([128, 4, 8, 64]), op=ALU.mult), r=[Bs], w=[Bs])
    dcol = A.tile([128, 4], F32, "dcol")
    for gl in range(8):
        k.dma("sp", dcol[16 * gl:16 * gl + 16, :], P["d"].rearrange("(o gl) j -> gl j o", gl=8)[gl], w=[newbuf()], allow_slow_non_contiguous=True)
    join_loads()
    for o in range(4):
        k.dve(lambda e, o=o: e.tensor_scalar(out=Dd[:, o, :], in0=c.ident_f[:], scalar1=dcol[:, o:o + 1], scalar2=None, op0=ALU.mult), r=[Bs, c.B_const], w=[Bs])
    k.dma("pool", Wglu[:], P["w_glu"].rearrange("(o p) f -> p o f", p=128), w=[newbuf()])
    k.dma("sp", bglu[:], P["b_glu"].rearrange("(f p) -> p f", p=128), w=[newbuf()], allow_slow_non_contiguous=True)
    wtmp = A.tile([128, 8, 128], F32, "wtmp")
    k.dma("sp", wtmp[:], P["w_s"].rearrange("g t s -> t g s"), w=[newbuf()])
    join_loads()
    for g in range(8):
        k.pe(lambda e, g=g: e.transpose(ps[7][:, g * 128:(g + 1) * 128] if g < 4 else ps[6][:, (g - 4) * 128:(g - 3) * 128], wtmp[:, g, :], c.ident_f[:]),
             r=[Bs, c.B_const], w=[BP[7] if g < 4 else BP[6]])
    for hb, bank in ((0, 7), (1, 6)):
        k.dve(lambda e, hb=hb, bank=bank: e.tensor_copy(out=wtmp[:, 4 * hb:4 * hb + 4, :], in_=ps[bank][:, :].rearrange("p (g t) -> p g t", t=128)), r=[BP[bank]], w=[Bs])
    k.pool(lambda e: e.affine_select(out=wtmp[:], in_=wtmp[:], pattern=[[0, 8], [1, 128]], compare_op=ALU.is_ge, fill=0.0, base=0, channel_multiplier=-1),
           r=[Bs], w=[Bs])
    k.dve(lambda e: e.tensor_copy(out=WmT[:], in_=wtmp[:]), r=[Bs], w=[Bs])
    k.dma("pool", bs8[:], P["b_s"], w=[newbuf()])
    join_loads()
    bself = A.tile([8, 4, 128], F32, "bself")
    k.pool(lambda e: e.memset(bself[:], 1.0), w=[Bs])
    k.pool(lambda e: e.affine_select(out=bself[:], in_=bself[:], pattern=[[128, 4], [1, 128]], compare_op=ALU.is_ge, fill=0.0, base=0, channel_multiplier=-64),
           r=[Bs], w=[Bs])
    k.pool(lambda e: e.affine_select(out=bself[:], in_=bself[:], pattern=[[-128, 4], [-1, 128]], compare_op=ALU.is_gt, fill=0.0, base=64, channel_multiplier=64),
           r=[Bs], w=[Bs])
    k.dve(lambda e: e.tensor_copy(out=bsel[:], in_=bself[:]), r=[Bs], w=[Bs])
    k.dma("sp", lnrow[0][:], P["ln_gain"].partition_broadcast(128), w=[newbuf()])
    k.dma("sp", lnrow[1][:], P["ln_bias"].partition_broadcast(128), w=[newbuf()])
    join_loads()
    k.barrier()
    A.reset(mp)
    NX = 2
    xs = [A.tile([128, 1024], F32, "xo") for _ in range(NX)]
    Bxs = [Buf() for _ in range(NX)]
    scr = norm_scratch(c, 2)
    hTs = [A.tile([128, 8, 512], BF16, "h1T") for _ in range(2)]
    BhTs = [[Buf() for _ in range(4)] for _ in range(2)]
    uT = A.tile([128, 4, 512], BF16, "uT"); BuT = [Buf() for _ in range(4)]
    suT = A.tile([128, 4, 512], BF16, "suT"); BsuT = [Buf() for _ in range(4)]
    vg = [A.tile([128, 512], F32, "vgel") for _ in range(2)]; Bvg = [Buf(), Buf()]
    vsq = A.tile([128, 512], BF16, "vsq"); Bvsq = Buf()
    st4 = [A.tile([128, 4], F32, "st4") for _ in range(2)]; Bst4 = [Buf(), Buf()]
    vn = A.tile([128, 4, 512], BF16, "vn"); Bvn = [Buf() for _ in range(4)]
    vtmp = [A.tile([128, 512], F32, "vtmp") for _ in range(2)]; Bvtmp = [Buf(), Buf()]
    mixo = [A.tile([128, 8, 512], BF16, "omix")] * 2; Bmixo = [Buf()] * 2
    t2 = A.tile([128, 512], F32, "r2")
    Bt_ = {"t2": Buf()}
    PB = []
    for _q in range(2):
        d_ = dict(t1=A.tile([128, 512], BF16, "r1"), t2b=A.tile([128, 512], BF16, "r2b"), t3=A.tile([128, 512], BF16, "r3"), t4=A.tile([128, 512], BF16, "r4"),
                  u1=A.tile([128, 512], BF16, "u1"), u2=A.tile([128, 512], BF16, "u2"), u3=A.tile([128, 512], BF16, "u3"), u4=A.tile([128, 512], BF16, "u4"),
                  bre=A.tile([128, 512], BF16, "bre"), bim=A.tile([128, 512], BF16, "bim"), wreb=A.tile([128, 512], BF16, "wreb"),
                  wimb=A.tile([128, 512], BF16, "wimb"), zt=A.tile([128, 4], F32, "zt"), cre=A.tile([128, 512], BF16, "cre"),
                  cim=A.tile([128, 512], BF16, "cim"), wre=A.tile([128, 512], F32, "wre"), wim=A.tile([128, 512], F32, "wim"))
        for n_ in list(d_):
            d_["B" + n_] = Buf()
        PB.append(d_)
    zre = [A.tile([128, 512], BF16, "zre") for _ in range(4)]; zim = [A.tile([128, 512], BF16, "zim") for _ in range(4)]
    Bz = [Buf() for _ in range(4)]
    yT = A.tile([128, 4, 512], BF16, "yT"); ByT = [Buf() for _ in range(4)]
    sig = [A.tile([128, 512], BF16, "sig") for _ in range(2)]; Bsig = [Buf(), Buf()]
    cnt = dict(x=0, p=0, v=0, s=0)

    def pbank():
        b = 1 + cnt["p"] % 2
        cnt["p"] += 1
        return b

    import os
    OCUT = os.environ.get("ODD_CUT", "Z")
    def do_norm(sb_):
        for tt in range(4):
            t0 = sb_ * 512 + tt * 128
            xb = cnt["x"] % NX
            cnt["x"] += 1
            k.dma("sp", xs[xb][:], Xin[t0:t0 + 128, :], w=[Bxs[xb]])
            norm_transpose(c, xs[xb][:], Bxs[xb], hTs[sb_ % 2], BhTs[sb_ % 2][tt], slice(tt * 128, (tt + 1) * 128), l, 1, scr[tt % 2], ps[0], BP[0])

    do_norm(0)
    for sb in range(S // 512 if OCUT != "P" else 0):
        tc = slice(sb * 512, (sb + 1) * 512)
        ob = sb % 2
        hT = hTs[sb % 2]
        BhT = BhTs[sb % 2]
        for mt in range(8):
            pb = pbank()
            for kt in range(8):
                k.pe(lambda e, hT=hT, pb=pb, kt=kt, mt=mt: e.matmul(ps[pb][:, :], lhsT=W[:, kt, mt * 128:(mt + 1) * 128], rhs=hT[:, kt, :], start=(kt == 0), stop=(kt == 7)),
                     r=BhT + [BW[mt // 4]], w=[BP[pb]], inc=(kt == 7))
            if mt < 4:
                k.act(lambda e, pb=pb, mt=mt: e.copy(out=uT[:, mt, :], in_=ps[pb][:, :]), r=[BP[pb]], w=[BuT[mt]])
            else:
                k.act(lambda e, pb=pb, mt=mt: e.activation(out=suT[:, mt - 4, :], in_=ps[pb][:, :], func=AF.Gelu), r=[BP[pb]], w=[BsuT[mt - 4]])
        if OCUT == "1":
            continue
        for tt in range(4):
            pb = pbank()
            tcs = slice(tt * 128, (tt + 1) * 128)
            for kt in range(8):
                k.pe(lambda e, hT=hT, pb=pb, kt=kt, tcs=tcs: e.matmul(ps[pb][:, :], lhsT=hT[:, kt, tcs], rhs=W[:, kt, 1024:1536], start=(kt == 0), stop=(kt == 7)),
                     r=[BhT[tt], BW[2]], w=[BP[pb]], inc=(kt == 7))
            vi = cnt["v"] % 2
            cnt["v"] += 1
            k.act(lambda e, pb=pb, vi=vi: e.activation(out=vg[vi][:], in_=ps[pb][:, :], func=AF.Gelu), r=[BP[pb]], w=[Bvg[vi]])
            k.dve(lambda e, vi=vi: e.tensor_reduce(out=st4[vi][:, 0:1], in_=vg[vi][:], axis=AX.X, op=ALU.add), r=[Bvg[vi]], w=[Bst4[vi]])
            k.act(lambda e, vi=vi: e.activation(out=vsq[:], in_=vg[vi][:], func=AF.Square, accum_out=st4[vi][:, 1:2]), r=[Bvg[vi]], w=[Bvsq, Bst4[vi]])
            k.dve(lambda e, vi=vi: e.tensor_scalar(out=st4[vi][:, 0:2], in0=st4[vi][:, 0:2], scalar1=1.0 / 512.0, scalar2=None, op0=ALU.mult), r=[Bst4[vi]], w=[Bst4[vi]])
            k.dve(lambda e, vi=vi: e.tensor_tensor(out=st4[vi][:, 2:3], in0=st4[vi][:, 0:1], in1=st4[vi][:, 0:1], op=ALU.mult), r=[Bst4[vi]], w=[Bst4[vi]])
            k.dve(lambda e, vi=vi: e.tensor_tensor(out=st4[vi][:, 2:3], in0=st4[vi][:, 1:2], in1=st4[vi][:, 2:3], op=ALU.subtract), r=[Bst4[vi]], w=[Bst4[vi]])
            k.act(lambda e, vi=vi: e.activation(out=st4[vi][:, 3:4], in_=st4[vi][:, 2:3], func=AF.Sqrt, bias=c.eps_col[:], scale=1.0), r=[Bst4[vi], c.B_const], w=[Bst4[vi]])
            k.dve(lambda e, vi=vi: e.reciprocal(out=st4[vi][:, 3:4], in_=st4[vi][:, 3:4]), r=[Bst4[vi]], w=[Bst4[vi]])
            k.dve(lambda e, vi=vi: e.tensor_scalar(out=vtmp[vi][:], in0=vg[vi][:], scalar1=st4[vi][:, 0:1], scalar2=st4[vi][:, 3:4], op0=ALU.subtract, op1=ALU.mult),
                  r=[Bvg[vi], Bst4[vi]], w=[Bvtmp[vi]])
            k.dve(lambda e, vi=vi: e.tensor_tensor(out=vtmp[vi][:], in0=vtmp[vi][:], in1=lnrow[0][:], op=ALU.mult), r=[Bvtmp[vi], Bs], w=[Bvtmp[vi]])
            k.dve(lambda e, vi=vi, tt=tt: e.tensor_tensor(out=vn[:, tt, :], in0=vtmp[vi][:], in1=lnrow[1][:], op=ALU.add), r=[Bvtmp[vi], Bs], w=[Bvn[tt]])
        if OCUT == "2":
            continue
        for gp in range(4):
            for tt in range(4):
                tcs = slice(tt * 128, (tt + 1) * 128)
                for g2 in range(2):
                    g = 2 * gp + g2
                    k.pe(lambda e, g=g, g2=g2, tt=tt, tcs=tcs: e.matmul(ps[6][64 * g2:64 * g2 + 64, tcs], lhsT=vn[:, tt, g * 64:(g + 1) * 64], rhs=WmT[:, g, :],
                                                                        start=True, stop=False), r=[Bvn[tt], Bs], w=[BP[6]], inc=False)
                k.pe(lambda e, gp=gp, tcs=tcs: e.matmul(ps[6][:, tcs], lhsT=bsel[0:8, gp, :], rhs=bs8[0:8, :], start=False, stop=True), r=[Bs], w=[BP[6]],
                     inc=(tt == 3))
            k.dve(lambda e, gp=gp, ob=ob: e.tensor_tensor(out=mixo[ob][:, 4 + gp, :], in0=ps[6][:, :], in1=suT[:, gp, :], op=ALU.mult),
                  r=[BP[6], BsuT[gp]], w=[Bmixo[ob]])
        if OCUT == "3":
            continue
        if sb + 1 < S // 512:
            do_norm(sb + 1)
        def stageR(m):
            ut, mm = m // 4, m % 4
            d = PB[m % 2]
            ec, es = Ecb[:, m, :], Esb[:, m, :]
            pa, pb2 = (3, 4)
            rb = rcol[:, m:m + 1].broadcast_to([128, 512])

            def g0():
                k.pe(lambda e: e.matmul(ps[pa][:, :], lhsT=Bbd[0][:, ut, mm * 128:(mm + 1) * 128], rhs=uT[:, ut, :], start=True, stop=True),
                     r=[Bs, BuT[ut]], w=[BP[pa]])
                k.pe(lambda e: e.matmul(ps[pb2][:, :], lhsT=Bbd[1][:, ut, mm * 128:(mm + 1) * 128], rhs=uT[:, ut, :], start=True, stop=True),
                     r=[Bs, BuT[ut]], w=[BP[pb2]])
                k.act(lambda e: e.copy(out=d["bre"][:], in_=ps[pa][:, :]), r=[BP[pa]], w=[d["Bbre"]])
                k.act(lambda e: e.copy(out=d["bim"][:], in_=ps[pb2][:, :]), r=[BP[pb2]], w=[d["Bbim"]])

            def g1():
                k.dve(lambda e: e.tensor_tensor(out=d["t1"][:], in0=d["bre"][:], in1=ec, op=ALU.mult), r=[d["Bbre"], Bs], w=[d["Bt1"]])
                k.dve(lambda e: e.tensor_tensor(out=d["t2b"][:], in0=d["bim"][:], in1=es, op=ALU.mult), r=[d["Bbim"], Bs], w=[d["Bt2b"]])
                k.dve(lambda e: e.tensor_tensor(out=d["t3"][:], in0=d["bim"][:], in1=ec, op=ALU.mult), r=[d["Bbim"], Bs], w=[d["Bt3"]])
                k.dve(lambda e: e.tensor_tensor(out=d["t4"][:], in0=d["bre"][:], in1=es, op=ALU.mult), r=[d["Bbre"], Bs], w=[d["Bt4"]])

            def g2():
                k.dve(lambda e: e.tensor_tensor(out=d["cre"][:], in0=d["t1"][:], in1=d["t2b"][:], op=ALU.add), r=[d["Bt1"], d["Bt2b"]], w=[d["Bcre"]])
                k.dve(lambda e: e.tensor_tensor(out=d["cim"][:], in0=d["t3"][:], in1=d["t4"][:], op=ALU.subtract), r=[d["Bt3"], d["Bt4"]], w=[d["Bcim"]])

            def g3():
                k.dve(lambda e: e.tensor_tensor_scan(out=d["wre"][:], data0=rb, data1=d["cre"][:], initial=z0[:, m, 0:1], op0=ALU.mult, op1=ALU.add),
                      r=[d["Bcre"], Bz0, Bs], w=[d["Bwre"]])
                k.dve(lambda e: e.tensor_tensor_scan(out=d["wim"][:], data0=rb, data1=d["cim"][:], initial=z0[:, m, 1:2], op0=ALU.mult, op1=ALU.add),
                      r=[d["Bcim"], Bz0, Bs], w=[d["Bwim"]])
                k.act(lambda e: e.copy(out=d["wreb"][:], in_=d["wre"][:]), r=[d["Bwre"]], w=[d["Bwreb"]])
                k.act(lambda e: e.copy(out=d["wimb"][:], in_=d["wim"][:]), r=[d["Bwim"]], w=[d["Bwimb"]])
            return [g0, g1, g2, g3]

        def stageU(m):
            ut, mm = m // 4, m % 4
            d = PB[m % 2]
            ec, es = Ecb[:, m, :], Esb[:, m, :]
            zt = d["zt"]
            zi = m % 4

            def h1():
                k.dve(lambda e: e.tensor_tensor(out=zt[:, 0:1], in0=d["wre"][:, 511:512], in1=EcL[:, m:m + 1], op=ALU.mult), r=[d["Bwre"], Bs], w=[d["Bzt"]])
                k.dve(lambda e: e.tensor_tensor(out=zt[:, 1:2], in0=d["wim"][:, 511:512], in1=EsL[:, m:m + 1], op=ALU.mult), r=[d["Bwim"], Bs], w=[d["Bzt"]])
                k.dve(lambda e: e.tensor_tensor(out=zt[:, 2:3], in0=d["wre"][:, 511:512], in1=EsL[:, m:m + 1], op=ALU.mult), r=[d["Bwre"], Bs], w=[d["Bzt"]])
                k.dve(lambda e: e.tensor_tensor(out=zt[:, 3:4], in0=d["wim"][:, 511:512], in1=EcL[:, m:m + 1], op=ALU.mult), r=[d["Bwim"], Bs], w=[d["Bzt"]])
                k.dve(lambda e: e.tensor_tensor(out=d["u1"][:], in0=d["wreb"][:], in1=ec, op=ALU.mult), r=[d["Bwreb"], Bs], w=[d["Bu1"]])
                k.dve(lambda e: e.tensor_tensor(out=d["u2"][:], in0=d["wimb"][:], in1=es, op=ALU.mult), r=[d["Bwimb"], Bs], w=[d["Bu2"]])
                k.dve(lambda e: e.tensor_tensor(out=d["u3"][:], in0=d["wreb"][:], in1=es, op=ALU.mult), r=[d["Bwreb"], Bs], w=[d["Bu3"]])
                k.dve(lambda e: e.tensor_tensor(out=d["u4"][:], in0=d["wimb"][:], in1=ec, op=ALU.mult), r=[d["Bwimb"], Bs], w=[d["Bu4"]])

            def h2():
                k.dve(lambda e: e.tensor_tensor(out=z0[:, m, 0:1], in0=zt[:, 0:1], in1=zt[:, 1:2], op=ALU.subtract), r=[d["Bzt"]], w=[Bz0])
                k.dve(lambda e: e.tensor_tensor(out=z0[:, m, 1:2], in0=zt[:, 2:3], in1=zt[:, 3:4], op=ALU.add), r=[d["Bzt"]], w=[Bz0])
                k.dve(lambda e: e.tensor_tensor(out=zre[zi][:], in0=d["u1"][:], in1=d["u2"][:], op=ALU.subtract), r=[d["Bu1"], d["Bu2"]], w=[Bz[zi]])
                k.dve(lambda e: e.tensor_tensor(out=zim[zi][:], in0=d["u3"][:], in1=d["u4"][:], op=ALU.add), r=[d["Bu3"], d["Bu4"]], w=[Bz[zi]])
                if mm == 3:
                    o = ut
                    for m2 in range(4):
                        mg = 4 * o + m2
                        k.pe(lambda e, mg=mg, m2=m2: e.matmul(ps[5][:, :], lhsT=Cbd[0][:, mg, :], rhs=zre[m2][:], start=(m2 == 0), stop=False), r=[Bs, Bz[m2]], w=[BP[5]], inc=False)
                        k.pe(lambda e, mg=mg, m2=m2: e.matmul(ps[5][:, :], lhsT=Cbd[1][:, mg, :], rhs=zim[m2][:], start=False, stop=False), r=[Bs, Bz[m2]], w=[BP[5]], inc=False)
                    k.pe(lambda e: e.matmul(ps[5][:, :], lhsT=Dd[:, o, :], rhs=uT[:, o, :], start=False, stop=True), r=[Bs, BuT[o]], w=[BP[5]])
                    k.act(lambda e: e.activation(out=yT[:, o, :], in_=ps[5][:, :], func=AF.Gelu), r=[BP[5]], w=[ByT[o]])
            return [h1, h2]

        Rs = [stageR(m) for m in range(16)]
        Rs[0][0](); Rs[1][0]()
        Rs[0][1](); Rs[0][2](); Rs[0][3]()
        for m in range(16):
            U = stageU(m)
            if m + 2 < 16:
                Rs[m + 2][0]()
            if m + 1 < 16:
                R = Rs[m + 1]
                R[1](); U[0](); R[2](); U[1](); R[3]()
            else:
                U[0](); U[1]()
        for f in range(4 if OCUT != "4" else 0):
            for o in range(4):
                k.pe(lambda e, f=f, o=o: e.matmul(ps[7][:, :], lhsT=Wglu[:, o, f * 128:(f + 1) * 128], rhs=yT[:, o, :], start=(o == 0), stop=(o == 3)),
                     r=[Bs, ByT[o]], w=[BP[7]], inc=(o == 3))
            si = f % 2
            k.dve(lambda e, f=f: e.tensor_scalar(out=t2[:], in0=ps[7][:, :], scalar1=bglu[:, f:f + 1], scalar2=None, op0=ALU.add), r=[BP[7], Bs], w=[Bt_["t2"]])
            k.act(lambda e, f=f, si=si: e.activation(out=sig[si][:], in_=t2[:], func=AF.Sigmoid), r=[Bt_["t2"]], w=[Bsig[si]])
            k.dve(lambda e, f=f, si=si, ob=ob: e.tensor_tensor(out=mixo[ob][:, f, :], in0=yT[:, f, :], in1=sig[si][:], op=ALU.mult), r=[ByT[f], Bsig[si]], w=[Bmixo[ob]])
        k.dma(c.stq, MixT.rearrange("(a p) t -> p a t", p=128)[:, :, tc], mixo[ob][:], r=[Bmixo[ob]])
    k.barrier()
    A.reset(m0)


PARAM_SHAPES = dict(
    ada_w=[2, 1024, 6144], ada_b=[2, 6144], even_w_in=[1, 1024, 3608], even_w_out=[1, 1024, 1024], gla_w_lr=[1, 16, 512], gla_b_lr=[1, 512],
    gla_gain=[1, 8, 64], fox_b_f=[1, 8], fox_q_gain=[1, 8, 64], fox_k_gain=[1, 8, 64], odd_w_in=[1, 1024, 1536], odd_w_out=[1, 1024, 1024],
    s5_lam_re=[1, 32, 64], s5_lam_im=[1, 32, 64], s5_log_dt=[1, 32], s5_b_re=[1, 32, 64, 16], s5_b_im=[1, 32, 64, 16], s5_c_re=[1, 32, 16, 64],
    s5_c_im=[1, 32, 16, 64], s5_d=[1, 32, 16], s5_w_glu=[1, 512, 512], s5_b_glu=[1, 512], sgu_ln_gain=[1, 512], sgu_ln_bias=[1, 512],
    sgu_w_s=[1, 8, 128, 128], sgu_b_s=[1, 8, 128], mlp_w1=[2, 1024, 4096], mlp_w2=[2, 4096, 1024])


def build_program(S):
    nc = bass.Bass("TRN2", target_bir_lowering=False)
    x = nc.dram_tensor("x", [S, 1024], F32, kind="ExternalInput").ap()
    cin = nc.dram_tensor("c", [1024], F32, kind="ExternalInput").ap()
    p = {n: nc.dram_tensor(n, sh, F32, kind="ExternalInput").ap() for n, sh in PARAM_SHAPES.items()}
    out = nc.dram_tensor("out", [S, 1024], F32, kind="ExternalOutput").ap()
    sc = lambda n, sh, dt: nc.dram_tensor(n, sh, dt).ap()
    T = dict(QgT=sc("QgT", [512, S], BF16), KgT=sc("KgT", [512, S], BF16), GgT=sc("GgT", [512, S], BF16), Kg=sc("Kg", [S, 512], BF16),
             Vg=sc("Vg", [S, 512], BF16), La=sc("La", [S, 512], F32), QfT=sc("QfT", [8, 70, S], BF16), KfT=sc("KfT", [8, 70, S], BF16),
             Vf=sc("Vf", [S, 8, 65], BF16))
    MixT = sc("MixT", [1024, S], BF16)
    X1 = sc("X1", [S, 1024], F32)
    X2 = sc("X2", [S, 1024], F32)
    with ExitStack() as st:
        c = make_ctx(nc, st, S)
        phase_adaln(c, 0, cin, p["ada_w"], p["ada_b"])
        phase_n1_even(c, 0, x, p["even_w_in"][0], p["gla_w_lr"][0], p["gla_b_lr"][0], p["fox_b_f"][0], p["fox_q_gain"][0], p["fox_k_gain"][0], T)
        phase_gla(c, T, p["gla_gain"][0], MixT)
        phase_fox(c, T, MixT)
        phase_outproj(c, 0, MixT, x, X1, p["even_w_out"][0])
        phase_mlp(c, 0, X1, X2, p["mlp_w1"][0], p["mlp_w2"][0])
        phase_adaln(c, 1, cin, p["ada_w"], p["ada_b"])
        P = dict(lam_re=p["s5_lam_re"][0], lam_im=p["s5_lam_im"][0], log_dt=p["s5_log_dt"][0], b_re=p["s5_b_re"][0], b_im=p["s5_b_im"][0],
                 c_re=p["s5_c_re"][0], c_im=p["s5_c_im"][0], d=p["s5_d"][0], w_glu=p["s5_w_glu"][0], b_glu=p["s5_b_glu"][0],
                 ln_gain=p["sgu_ln_gain"][0], ln_bias=p["sgu_ln_bias"][0], w_s=p["sgu_w_s"][0], b_s=p["sgu_b_s"][0])
        phase_odd(c, 1, X2, p["odd_w_in"][0], P, MixT)
        phase_outproj(c, 1, MixT, X2, X1, p["odd_w_out"][0])
        phase_mlp(c, 1, X1, out, p["mlp_w1"][1], p["mlp_w2"][1])
        finish(c)
    return nc


_NC_CACHE = {}


def kernel(**inputs):
    x = np.asarray(inputs["x"], dtype=np.float32)
    cc = np.asarray(inputs["c"], dtype=np.float32)
    B, S, _ = x.shape
    if S not in _NC_CACHE:
        _NC_CACHE[S] = build_program(S)
    nc = _NC_CACHE[S]
    params = {n: np.ascontiguousarray(np.asarray(inputs[n], dtype=np.float32)) for n in PARAM_SHAPES}
    in_maps = []
    for b in range(B):
        m = dict(params)
        m["x"] = np.ascontiguousarray(x[b])
        m["c"] = np.ascontiguousarray(cc[b])
        in_maps.append(m)
    res = run_bass_kernel_spmd(nc, in_maps, core_ids=list(range(B)))
    return np.stack([np.asarray(r["out"], dtype=np.float32) for r in res.results], axis=0)
```
